# Optimizing a Trainium2 kernel written in Bass

```python
import jax, jax.numpy as jnp
from jax import lax
import numpy as np

D_MODEL = 1024
BATCH = 32
SEQ = 256
DEPTH = 4
DEC_BATCH = 4
DEC_SEQ = 4096
PAST_LEN = 512

GRID_W = 64
BLOCK = 128
WINDOW = 128
HEAD_DIM = 64
A_HEADS = 8
A_KV = 2
B_HEADS = 8
B_Q_LORA = 384
B_KV_LORA = 256
B_NOPE = 64
B_ROPE = 32
B_V = 64
C_HEADS = 8
C_KV = 2
D_FF = 4 * D_MODEL
N_BRANCH = 3
N_MOD = 6
ROPE_THETA = 10000.0
EPS = 1e-6
NEG = -1e30

IN_SIZES = (A_HEADS * HEAD_DIM, A_KV * HEAD_DIM, A_KV * HEAD_DIM,
            B_Q_LORA, B_KV_LORA, B_ROPE,
            C_HEADS * HEAD_DIM, C_KV * HEAD_DIM, C_KV * HEAD_DIM,
            N_BRANCH * D_MODEL)
IN_SPLITS = tuple(int(s) for s in np.cumsum(IN_SIZES)[:-1])
IN_COLS = int(sum(IN_SIZES))

kernel_name = 'hybrid_diffusion_step'


def _rmsnorm(x, g):
    xf = x.astype(jnp.float32)
    y = xf * lax.rsqrt(jnp.mean(xf * xf, axis=-1, keepdims=True) + EPS)
    return (y * g.astype(jnp.float32)).astype(x.dtype)


def _rope_1d(x, pos):
    half = x.shape[-1] // 2
    inv = ROPE_THETA ** (-jnp.arange(half, dtype=jnp.float32) / half)
    ang = pos.astype(jnp.float32)[:, None] * inv[None, :]
    cos = jnp.cos(ang)[:, None, :].astype(x.dtype)
    sin = jnp.sin(ang)[:, None, :].astype(x.dtype)
    x1, x2 = x[..., :half], x[..., half:]
    return jnp.concatenate([x1 * cos - x2 * sin, x2 * cos + x1 * sin], axis=-1)


def _axial_rope(x):
    n = x.shape[1]
    rows = n // GRID_W
    row = jnp.repeat(jnp.arange(rows, dtype=jnp.int32), GRID_W)
    col = jnp.tile(jnp.arange(GRID_W, dtype=jnp.int32), rows)
    half = x.shape[-1] // 2
    return jnp.concatenate([_rope_1d(x[..., :half], row), _rope_1d(x[..., half:], col)], axis=-1)


def _attend_block(qb, ks, vs, masks, sink, scale):
    bsz, nq, nh, dk = qb.shape
    ng = ks[0].shape[2]
    nr = nh // ng
    qg = qb.reshape(bsz, nq, ng, nr, dk)
    logits = []
    for k, m in zip(ks, masks):
        s = jnp.einsum('bqgrd,bkgd->bgrqk', qg, k).astype(jnp.float32) * scale
        if m is not None:
            s = jnp.where(m, s, NEG)
        logits.append(s)
    if sink is not None:
        logits.append(jnp.broadcast_to(sink.astype(jnp.float32).reshape(1, ng, nr, 1, 1), (bsz, ng, nr, nq, 1)))
    probs = jax.nn.softmax(jnp.concatenate(logits, axis=-1), axis=-1)
    out = None
    off = 0
    for v in vs:
        n = v.shape[1]
        term = jnp.einsum('bgrqk,bkgd->bqgrd', probs[..., off:off + n].astype(v.dtype), v)
        out = term if out is None else out + term
        off += n
    return out.reshape(bsz, nq, nh, vs[0].shape[-1])


def _sweep(q, k_ctx, v_ctx, sink, k_lat=None, v_lat=None, window=None):
    bsz, n, nh, dk = q.shape
    nb = n // BLOCK
    scale = dk ** -0.5
    qb = q.reshape(bsz, nb, BLOCK, nh, dk).swapaxes(0, 1)
    if k_lat is None:
        def body(args):
            _, qi = args
            return _attend_block(qi, [k_ctx], [v_ctx], [None], sink, scale)
    elif window is None:
        def body(args):
            _, qi = args
            return _attend_block(qi, [k_lat, k_ctx], [v_lat, v_ctx], [None, None], sink, scale)
    else:
        pad = [(0, 0), (BLOCK, BLOCK), (0, 0), (0, 0)]
        kp = jnp.pad(k_lat, pad)
        vp = jnp.pad(v_lat, pad)

        def body(args):
            b, qi = args
            start = b * BLOCK
            kw = lax.dynamic_slice_in_dim(kp, start, 3 * BLOCK, axis=1)
            vw = lax.dynamic_slice_in_dim(vp, start, 3 * BLOCK, axis=1)
            qpos = start + jnp.arange(BLOCK)
            kpos = start - BLOCK + jnp.arange(3 * BLOCK)
            mask = ((jnp.abs(qpos[:, None] - kpos[None, :]) <= window)
                    & (kpos >= 0)[None, :] & (kpos < n)[None, :])
            return _attend_block(qi, [kw, k_ctx], [vw, v_ctx], [mask, None], sink, scale)
    out = lax.map(body, (jnp.arange(nb, dtype=jnp.int32), qb))
    return out.swapaxes(0, 1).reshape(bsz, n, nh, out.shape[-1])


def _project(h, p):
    bsz, n, _ = h.shape
    z = h @ p['w_in']
    qa, ka, va, qbd, kvbd, kbr, qc, kc, vc, gates = jnp.split(z, IN_SPLITS, axis=-1)
    qa = qa.reshape(bsz, n, A_HEADS, HEAD_DIM)
    ka = ka.reshape(bsz, n, A_KV, HEAD_DIM)
    va = va.reshape(bsz, n, A_KV, HEAD_DIM)
    qb = (_rmsnorm(qbd, p['g_qa']) @ p['w_qup']).reshape(bsz, n, B_HEADS, B_NOPE + B_ROPE)
    ckv = _rmsnorm(kvbd, p['g_kva'])
    qc = _rmsnorm(qc.reshape(bsz, n, C_HEADS, HEAD_DIM), p['g_qc'])
    kc = _rmsnorm(kc.reshape(bsz, n, C_KV, HEAD_DIM), p['g_kc'])
    vc = vc.reshape(bsz, n, C_KV, HEAD_DIM)
    return qa, ka, va, qb, ckv, kbr, qc, kc, vc, gates


def _mla_expand(ckv, kbr, w_kvup):
    bsz, n, _ = ckv.shape
    kv = (ckv @ w_kvup).reshape(bsz, n, B_HEADS, B_NOPE + B_V)
    k_nope, v = kv[..., :B_NOPE], kv[..., B_NOPE:]
    k_rope = jnp.broadcast_to(kbr[:, :, None, :], (bsz, n, B_HEADS, B_ROPE))
    return jnp.concatenate([k_nope, k_rope], axis=-1), v


def _merge(gates, oa, ob, oc, p):
    bsz, n = oa.shape[:2]
    ga, gb, gc = jnp.split(jax.nn.sigmoid(gates), N_BRANCH, axis=-1)
    m = (ga * (oa.reshape(bsz, n, -1) @ p['w_oa'])
         + gb * (ob.reshape(bsz, n, -1) @ p['w_ob'])
         + gc * (oc.reshape(bsz, n, -1) @ p['w_oc']))
    return m @ p['w_out']


def _mix_context(h, p):
    qa, ka, va, qb, ckv, kbr, qc, kc, vc, gates = _project(h, p)
    kb, vb = _mla_expand(ckv, kbr, p['w_kvup'])
    oa = _sweep(qa, ka, va, p['a_sink'])
    ob = _sweep(qb, kb, vb, None)
    oc = _sweep(qc, kc, vc, None)
    return _merge(gates, oa, ob, oc, p), (ka, va, ckv, kbr, kc, vc)


def _mix_latent(h, p, ctx):
    ka_c, va_c, ckv_c, kbr_c, kc_c, vc_c = ctx
    qa, ka, va, qb, ckv, kbr, qc, kc, vc, gates = _project(h, p)
    qa = _axial_rope(qa)
    ka = _axial_rope(ka)
    qb = jnp.concatenate([qb[..., :B_NOPE], _axial_rope(qb[..., B_NOPE:])], axis=-1)
    kbr = _axial_rope(kbr[:, :, None, :])[:, :, 0, :]
    qc = _axial_rope(qc)
    kc = _axial_rope(kc)
    kb, vb = _mla_expand(ckv, kbr, p['w_kvup'])
    kb_c, vb_c = _mla_expand(ckv_c, kbr_c, p['w_kvup'])
    oa = _sweep(qa, ka_c, va_c, p['a_sink'], ka, va, WINDOW)
    ob = _sweep(qb, kb_c, vb_c, None, kb, vb)
    oc = _sweep(qc, kc_c, vc_c, None, kc, vc)
    return _merge(gates, oa, ob, oc, p), None


def _sandwich_layer(x, mod, p, mix_fn):
    sh1, sc1, g1, sh2, sc2, g2 = jnp.split(mod, N_MOD, axis=-1)
    h = _rmsnorm(x, p['g_pre_mix']) * (1 + sc1) + sh1
    mixed, aux = mix_fn(h)
    x = x + g1 * _rmsnorm(mixed, p['g_post_mix'])
    h = _rmsnorm(x, p['g_pre_mlp']) * (1 + sc2) + sh2
    f = jnp.square(jax.nn.relu(h @ p['w_mlp1'])) @ p['w_mlp2']
    x = x + g2 * _rmsnorm(f, p['g_post_mlp'])
    return x, aux


def setup_inputs(seed: int = 0) -> dict:
    key = jax.random.key(seed)
    ks = jax.random.split(key, 32)

    def nrm(k, shape, scale=1.0):
        return jax.random.normal(k, shape, jnp.float32) * scale

    def gain(k, shape):
        return 1.0 + 0.1 * jax.random.normal(k, shape, jnp.float32)

    return {
        'x_prompt': nrm(ks[0], (BATCH, SEQ, D_MODEL)),
        'x_sample': nrm(ks[1], (DEC_BATCH, DEC_SEQ, D_MODEL)),
        'cache_a_k': nrm(ks[2], (DEC_BATCH, DEPTH, PAST_LEN, A_KV, HEAD_DIM)),
        'cache_a_v': nrm(ks[3], (DEC_BATCH, DEPTH, PAST_LEN, A_KV, HEAD_DIM)),
        'cache_b_ckv': nrm(ks[4], (DEC_BATCH, DEPTH, PAST_LEN, B_KV_LORA)),
        'cache_b_krope': nrm(ks[5], (DEC_BATCH, DEPTH, PAST_LEN, B_ROPE)),
        'cache_c_k': nrm(ks[6], (DEC_BATCH, DEPTH, PAST_LEN, C_KV, HEAD_DIM)),
        'cache_c_v': nrm(ks[7], (DEC_BATCH, DEPTH, PAST_LEN, C_KV, HEAD_DIM)),
        'c': nrm(ks[8], (DEC_BATCH, D_MODEL)),
        'c_ctx': nrm(ks[9], (D_MODEL,)),
        'w_mod': nrm(ks[10], (DEPTH, D_MODEL, N_MOD * D_MODEL), 0.5 * D_MODEL ** -0.5),
        'b_mod': nrm(ks[11], (DEPTH, N_MOD * D_MODEL), 0.01),
        'g_pre_mix': gain(ks[12], (DEPTH, D_MODEL)),
        'g_post_mix': gain(ks[13], (DEPTH, D_MODEL)),
        'g_pre_mlp': gain(ks[14], (DEPTH, D_MODEL)),
        'g_post_mlp': gain(ks[15], (DEPTH, D_MODEL)),
        'w_in': nrm(ks[16], (DEPTH, D_MODEL, IN_COLS), D_MODEL ** -0.5),
        'a_sink': nrm(ks[17], (DEPTH, A_HEADS), 0.5),
        'g_qa': gain(ks[18], (DEPTH, B_Q_LORA)),
        'w_qup': nrm(ks[19], (DEPTH, B_Q_LORA, B_HEADS * (B_NOPE + B_ROPE)), B_Q_LORA ** -0.5),
        'g_kva': gain(ks[20], (DEPTH, B_KV_LORA)),
        'w_kvup': nrm(ks[21], (DEPTH, B_KV_LORA, B_HEADS * (B_NOPE + B_V)), B_KV_LORA ** -0.5),
        'g_qc': gain(ks[22], (DEPTH, HEAD_DIM)),
        'g_kc': gain(ks[23], (DEPTH, HEAD_DIM)),
        'w_oa': nrm(ks[24], (DEPTH, A_HEADS * HEAD_DIM, D_MODEL), (A_HEADS * HEAD_DIM) ** -0.5),
        'w_ob': nrm(ks[25], (DEPTH, B_HEADS * B_V, D_MODEL), (B_HEADS * B_V) ** -0.5),
        'w_oc': nrm(ks[26], (DEPTH, C_HEADS * HEAD_DIM, D_MODEL), (C_HEADS * HEAD_DIM) ** -0.5),
        'w_out': nrm(ks[27], (DEPTH, D_MODEL, D_MODEL), D_MODEL ** -0.5),
        'w_mlp1': nrm(ks[28], (DEPTH, D_MODEL, D_FF), D_MODEL ** -0.5),
        'w_mlp2': nrm(ks[29], (DEPTH, D_FF, D_MODEL), D_FF ** -0.5),
    }


def reference(x_prompt, x_sample, cache_a_k, cache_a_v, cache_b_ckv, cache_b_krope, cache_c_k, cache_c_v,
              c, c_ctx, w_mod, b_mod, g_pre_mix, g_post_mix, g_pre_mlp, g_post_mlp, w_in, a_sink,
              g_qa, w_qup, g_kva, w_kvup, g_qc, g_kc, w_oa, w_ob, w_oc, w_out, w_mlp1, w_mlp2):
    def layer_params(l):
        return {'g_pre_mix': g_pre_mix[l], 'g_post_mix': g_post_mix[l],
                'g_pre_mlp': g_pre_mlp[l], 'g_post_mlp': g_post_mlp[l],
                'w_in': w_in[l], 'a_sink': a_sink[l], 'g_qa': g_qa[l], 'w_qup': w_qup[l],
                'g_kva': g_kva[l], 'w_kvup': w_kvup[l], 'g_qc': g_qc[l], 'g_kc': g_kc[l],
                'w_oa': w_oa[l], 'w_ob': w_ob[l], 'w_oc': w_oc[l], 'w_out': w_out[l],
                'w_mlp1': w_mlp1[l], 'w_mlp2': w_mlp2[l]}

    y_prompt = x_prompt
    ctx_list = []
    for l in range(DEPTH):
        p = layer_params(l)
        mod = jax.nn.silu(c_ctx) @ w_mod[l] + b_mod[l]
        y_prompt, ctx = _sandwich_layer(y_prompt, mod, p, lambda h: _mix_context(h, p))
        ctx_list.append(ctx)
    new_a_k = jnp.stack([t[0] for t in ctx_list], axis=1)
    new_a_v = jnp.stack([t[1] for t in ctx_list], axis=1)
    new_b_ckv = jnp.stack([t[2] for t in ctx_list], axis=1)
    new_b_krope = jnp.stack([t[3] for t in ctx_list], axis=1)
    new_c_k = jnp.stack([t[4] for t in ctx_list], axis=1)
    new_c_v = jnp.stack([t[5] for t in ctx_list], axis=1)

    y_sample = x_sample
    for l in range(DEPTH):
        p = layer_params(l)
        mod = (jax.nn.silu(c) @ w_mod[l] + b_mod[l])[:, None, :]
        ctx = (cache_a_k[:, l], cache_a_v[:, l], cache_b_ckv[:, l], cache_b_krope[:, l],
               cache_c_k[:, l], cache_c_v[:, l])
        y_sample, _ = _sandwich_layer(y_sample, mod, p, lambda h: _mix_latent(h, p, ctx))

    return (y_prompt, y_sample, new_a_k, new_a_v, new_b_ckv, new_b_krope, new_c_k, new_c_v)
```

```python
import os
import numpy as np
from contextlib import ExitStack
import concourse.bass as bass
import concourse.mybir as mybir
from concourse.bass_utils import run_bass_kernel_spmd

F32 = mybir.dt.float32
BF16 = mybir.dt.bfloat16
AF = mybir.ActivationFunctionType
ALU = mybir.AluOpType

L = 4
D = 1024
NCOL = 5280
EPS = 1e-6
CH = 512
SC_A = 64 ** -0.5
SC_B = 96 ** -0.5
O_QA, O_KA, O_VA, O_QBD, O_KVBD, O_KBR, O_QC, O_KC, O_VC, O_G = 0, 512, 640, 768, 1152, 1408, 1440, 1952, 2080, 2208


class _Stop(Exception):
    pass


def ckpt(name):
    if os.environ.get('KSTOP') == name:
        raise _Stop()


class T:
    def __init__(self, name):
        self.name = name
        self.w = {}
        self.r = {}


def _merge(d, s):
    for k, v in s.items():
        if d.get(k, 0) < v:
            d[k] = v


class Prog:
    ENG = ('pe', 'act', 'dve', 'pool', 'sp')

    def __init__(self, nc):
        self.nc = nc
        self.streams = {e: [] for e in self.ENG}
        self.nops = {e: 0 for e in self.ENG}
        self.needed = {e: set() for e in self.ENG}
        self.dmacnt = {}
        self.keys = []

    def _deps(self, r, w, wp):
        deps = {}
        for t in r:
            _merge(deps, t.w)
        for t in w:
            _merge(deps, t.w)
            _merge(deps, t.r)
        for t in wp:
            _merge(deps, t.w)
            _merge(deps, t.r)
        return deps

    def _mark(self, deps):
        for k, v in deps.items():
            if isinstance(k, str):
                self.needed[k].add(v)

    def _update(self, ev, r, w, wp):
        for t in r:
            _merge(t.r, ev)
        for t in w:
            t.w = dict(ev)
            t.r = {}
        for t in wp:
            _merge(t.w, ev)

    def op(self, eng, fn, r=(), w=(), wp=()):
        deps = self._deps(r, w, wp)
        self._mark(deps)
        self.nops[eng] += 1
        idx = self.nops[eng]
        self.streams[eng].append((deps, fn, ('c', eng, idx)))
        self._update({eng: idx}, r, w, wp)

    def dma(self, q, out, in_, r=(), w=(), wp=(), key=None, slow=False):
        assert key is not None
        deps = self._deps(r, w, wp)
        self._mark(deps)
        kid = ('dma', id(key), q)
        if kid not in self.dmacnt:
            self.dmacnt[kid] = 0
            self.keys.append(kid)
        self.dmacnt[kid] += 16
        cnt = self.dmacnt[kid]

        def fn(e, out=out, in_=in_, slow=slow):
            if slow:
                return e.dma_start(out=out, in_=in_, allow_slow_non_contiguous=True)
            return e.dma_start(out=out, in_=in_)
        self.streams[q].append((deps, fn, ('d', kid, cnt)))
        self._update({kid: cnt}, r, w, wp)

    def finalize(self, es, block):
        nc = self.nc
        sems = {}
        for e in self.ENG:
            sems[e] = es.enter_context(nc.semaphore("s_" + e))
        for i, k in enumerate(self.keys):
            sems[k] = es.enter_context(nc.semaphore("d%d" % i))
        cmap = {}
        for e in self.ENG:
            m = {}
            c = 0
            nd = self.needed[e]
            for i in range(1, self.nops[e] + 1):
                if i in nd:
                    c += 1
                m[i] = c
            cmap[e] = m
        final = dict(self.dmacnt)

        def emit(ename, eng):
            waited = {}
            for deps, fn, me in self.streams[ename]:
                for k, v in deps.items():
                    if isinstance(k, str):
                        if k == ename and ename == 'pe':
                            pass
                        cnt = cmap[k][v]
                    else:
                        cnt = v
                    if cnt > 0 and waited.get(k, 0) < cnt:
                        eng.wait_ge(sems[k], cnt)
                        waited[k] = cnt
                inst = fn(eng)
                if me[0] == 'c':
                    if me[2] in self.needed[me[1]]:
                        inst.then_inc(sems[me[1]], 1)
                else:
                    inst.then_inc(sems[me[1]], 16)
            if ename == 'sp':
                for k, v in final.items():
                    eng.wait_ge(sems[k], v)

        block.sync(lambda e: emit('sp', e))
        block.gpsimd(lambda e: emit('pool', e))
        block.vector(lambda e: emit('dve', e))
        block.scalar(lambda e: emit('act', e))
        block.tensor(lambda e: emit('pe', e))


def build_program():
    nc = bass.Bass("TRN2", target_bir_lowering=False)
    P = Prog(nc)

    def din(name, shape, dt=F32):
        return nc.dram_tensor(name, list(shape), dt, kind="ExternalInput").ap()

    def dout(name, shape):
        return nc.dram_tensor(name, list(shape), F32, kind="ExternalOutput").ap()

    def dscr(name, shape, dt=BF16):
        return nc.dram_tensor(name, list(shape), dt)

    xp_in = din("xp", [1024, D])
    xs_in = din("xs", [2048, D])
    ca_k = din("ca_k", [L, 512, 128]); ca_v = din("ca_v", [L, 512, 128])
    cb_ckv = din("cb_ckv", [L, 512, 256]); cb_kr = din("cb_kr", [L, 512, 32])
    cc_k = din("cc_k", [L, 512, 128]); cc_v = din("cc_v", [L, 512, 128])
    cvecT = din("cvecT", [128, 8, 2])
    h_bmodT = din("h_bmodT", [128, L, 48]); h_gvecs = din("h_gvecs", [128, 4 * L * 8])
    h_gqaT = din("h_gqaT", [128, L, 3]); h_gkvaT = din("h_gkvaT", [128, L, 2])
    h_gqc2 = din("h_gqc2", [128, L]); h_gkc2 = din("h_gkc2", [128, L]); h_sinkb = din("h_sinkb", [128, L, 8])
    w_mod = din("w_mod", [L, D, 6 * D])
    w_in = din("w_in", [L, D, NCOL])
    w_qup = din("w_qup", [L, 384, 768])
    w_kvup = din("w_kvup", [L, 256, 1024])
    w_oa = din("w_oa", [L, 512, D]); w_ob = din("w_ob", [L, 512, D]); w_oc = din("w_oc", [L, 512, D])
    w_out = din("w_out", [L, D, D]); w_mlp1 = din("w_mlp1", [L, D, 4 * D]); w_mlp2 = din("w_mlp2", [L, 4 * D, D])
    c_ident = din("c_ident", [128, 128]); c_ones = din("c_ones", [128, 128]); c_bones = din("c_bones", [128, 128])
    c_perm = din("c_perm", [128, 128])
    c_ropeA = din("c_ropeA", [2, 128, 2048]); c_ropeB = din("c_ropeB", [2, 96, 2048])
    c_mask = din("c_mask", [3, 128, 384])

    y_p = dout("y_p", [1024, D]); y_s = dout("y_s", [2048, D])
    o_ak = dout("o_ak", [4, L, 256, 128]); o_av = dout("o_av", [4, L, 256, 128])
    o_ckv = dout("o_ckv", [4, L, 256, 256]); o_kr = dout("o_kr", [4, L, 256, 32])
    o_ck = dout("o_ck", [4, L, 256, 128]); o_cv = dout("o_cv", [4, L, 256, 128])

    win_b = dscr("win_b", [L, D, NCOL])
    wsw_b = dscr("wsw_b", [L, D, 1408])
    wqup_b = dscr("wqup_b", [L, 384, 768]); wqupsw_b = dscr("wqupsw_b", [L, 384, 768])
    wkvup_b = dscr("wkvup_b", [L, 256, 1024])
    wg_b = dscr("wg_b", [L, 8, 128, 8 * 384])
    wo_b = dscr("wo_b", [L, 2, 128, 12 * 512])
    wout_b = dscr("wout_b", [L, 2, 128, 8 * 512])
    w1_b = dscr("w1_b", [L, 8, 128, 8 * 512])
    w2_b = dscr("w2_b", [L, 8, 128, 32 * 128])
    xp_scr = dscr("xp_scr", [2, 128, 8 * CH], F32)
    h_scr = dscr("h_scr", [6, 128, 8 * CH])
    q_p = dscr("q_p", [1792, 1024]); k_p = dscr("k_p", [1024, 1024]); v_p = dscr("v_p", [1024, 768]); o_p = dscr("o_p", [1536, 1024])
    q_s = dscr("q_s", [1792, 2048]); o_s = dscr("o_s", [1536, 2048])
    kgi = [dscr("kg_in%d" % t, [1024, 1024]) for t in range(2)]; kgo = [dscr("kg_out%d" % t, [2048, 1024]) for t in range(2)]
    vgi = [dscr("vg_in%d" % t, [1024, 768]) for t in range(2)]; vgo = [dscr("vg_out%d" % t, [2048, 768]) for t in range(2)]
    kctx = dscr("kctx", [L, 1024, 512]); vctx = dscr("vctx", [L, 512, 768])

    es = ExitStack()
    with es:
        def sb(name, shape, dt=F32):
            return es.enter_context(nc.sbuf_tensor(name, list(shape), dt))

        xTs = sb("xTs", [128, 8, 2048]); t_xTs = [T("xTs%d" % i) for i in range(4)]
        XC = sb("XC", [128, 8, CH]); t_XC = T("XC")
        identF = sb("identF", [128, 128]); onesB = sb("onesB", [128, 128], BF16); bonesB = sb("bonesB", [128, 128], BF16)
        permF = sb("permF", [128, 128])
        t_const = T("const")
        modraw = sb("modraw", [128, L, 48, 2]); t_modraw = T("modraw")
        bmodT = sb("bmodT", [128, L, 48]); gvecs = sb("gvecs", [128, 4, L, 8])
        MV = sb("MV", [128, 6, L, 2, 8]); t_MV = T("MV")
        gqaT = sb("gqaT", [128, L, 3]); gkvaT = sb("gkvaT", [128, L, 2])
        gq4 = sb("gq4", [128, 4, L]); t_gq4 = T("gq4")
        sinkE = sb("sinkE", [128, L, 8])
        maskB = sb("maskB", [128, 3, 384], BF16)
        scT = sb("scT", [128, 8, 2])
        ARENA = sb("ARENA", [128, 24576], BF16)
        BIG = sb("BIG", [128, 16384], BF16)
        TMP = sb("TMP", [128, 6, CH]); t_TMP = [T("TMP%d" % i) for i in range(6)]
        TMPB = sb("TMPB", [128, 4, CH], BF16); t_TMPB = [T("TMPB%d" % i) for i in range(4)]
        ROPE = sb("ROPE", [128, 4, CH]); t_ROPE = T("ROPE")
        psb = [es.enter_context(nc.psum_tensor("ps%d" % i, [128, CH], F32)) for i in range(8)]
        t_ps = [T("ps%d" % i) for i in range(8)]
        bank_ctr = [0]

        def bank():
            i = bank_ctr[0] % 6
            bank_ctr[0] += 1
            return psb[i], t_ps[i]
        obank_ctr = [0]

        def obank():
            i = 6 + obank_ctr[0] % 2
            obank_ctr[0] += 1
            return psb[i], t_ps[i]

        WS = [ARENA[:, i * 4096:(i + 1) * 4096] for i in range(3)]
        t_WS = [T("WS%d" % i) for i in range(3)]
        HTflat = ARENA[:, 12288:16384]
        HT = HTflat.rearrange("p (k c) -> p k c", k=8); t_HT = T("HT")
        RES = ARENA[:, 16384:24576].bitcast(F32).rearrange("p (k c) -> p k c", k=8); t_RES = T("RES")
        ws_ctr = [0]

        def wslot():
            i = ws_ctr[0] % 3
            ws_ctr[0] += 1
            return WS[i], t_WS[i]
        KT = [ARENA[:, i * 4608:(i + 1) * 4608] for i in range(2)]
        VT = [ARENA[:, 9216 + i * 4608: 9216 + (i + 1) * 4608].rearrange("p (t c) -> p t c", c=128) for i in range(2)]
        QT = [ARENA[:, 18432 + i * 2048: 18432 + (i + 1) * 2048] for i in range(2)]
        OH = [ARENA[:, 22528 + i * 1024: 22528 + (i + 1) * 1024] for i in range(2)]
        arena_all = t_WS + [t_HT, t_RES]
        t_BIG = T("BIG")

        dma_in = T("dma_in")

        try:
            kc = T("kconst")
            P.dma('sp', identF[:], c_ident, w=[t_const], key=kc)
            P.dma('sp', permF[:], c_perm, wp=[t_const], key=kc)
            P.dma('pool', onesB[:], c_ones, wp=[t_const], key=kc)
            P.dma('pool', bonesB[:], c_bones, wp=[t_const], key=kc)
            P.dma('pool', maskB[:], c_mask.rearrange("v p c -> p v c"), wp=[t_const], key=kc)
            t_wcs = [T("wcast%d" % i) for i in range(L)]
            kw = T("kwcast")

            cast_l = [0]

            def cast(dst, src):
                P.dma('pool', dst, src, wp=[t_wcs[cast_l[0]]], key=kw)

            def cast_layer(l):
                cast_l[0] = l
                for rb in range(8):
                    rs = slice(rb * 128, (rb + 1) * 128)
                    for (o, ) in ((O_QA,), (O_QC,)):
                        for a in range(2):
                            cast(win_b.ap()[l, rs, o:o + 512].rearrange("r (j a d) -> r j a d", j=4, a=2)[:, :, a, :],
                                 w_in[l, rs, o:o + 512].rearrange("r (a j d) -> r a j d", a=2, j=4)[:, a])
                    cast(win_b.ap()[l, rs, O_KA:O_QC], w_in[l, rs, O_KA:O_QC])
                    cast(win_b.ap()[l, rs, O_KC:O_G], w_in[l, rs, O_KC:O_G])
                    for b in range(3):
                        cast(wg_b.ap()[l, :, :, rb * 384 + b * 128:rb * 384 + (b + 1) * 128].rearrange("n p c -> p n c"),
                             w_in[l, rs, O_G + b * 1024:O_G + (b + 1) * 1024].rearrange("r (n c) -> r n c", n=8))
                    cast(wout_b.ap()[l, :, :, rb * 512:(rb + 1) * 512].rearrange("g p c -> p g c"), w_out[l, rs, :].rearrange("p (g c) -> p g c", g=2))
                    cast(w1_b.ap()[l, :, :, rb * 512:(rb + 1) * 512].rearrange("g p c -> p g c"), w_mlp1[l, rs, :].rearrange("p (g c) -> p g c", g=8))
                for rb in range(3):
                    rs = slice(rb * 128, (rb + 1) * 128)
                    cast(wqup_b.ap()[l, rs, :], w_qup[l, rs, :])
                for rb in range(2):
                    rs = slice(rb * 128, (rb + 1) * 128)
                    for t in range(2):
                        cast(wkvup_b.ap()[l, rs, t * 512:(t + 1) * 512].rearrange("r (h d) -> r h d", h=8),
                             w_kvup[l, rs, :].rearrange("r (h t d) -> r h t d", h=8, t=2)[:, :, t, :])
                for bi, wsrc in enumerate((w_oa, w_ob, w_oc)):
                    for rb in range(4):
                        k12 = bi * 4 + rb
                        cast(wo_b.ap()[l, :, :, k12 * 512:(k12 + 1) * 512].rearrange("g p c -> p g c"), wsrc[l, rb * 128:(rb + 1) * 128, :].rearrange("p (g c) -> p g c", g=2))
                for k in range(32):
                    cast(w2_b.ap()[l, :, :, k * 128:(k + 1) * 128].rearrange("n p c -> p n c"),
                         w_mlp2[l, k * 128:(k + 1) * 128, :].rearrange("p (n c) -> p n c", n=8))
                cast(vctx.ap()[l, :, 0:128], ca_v[l])
                cast(vctx.ap()[l, :, 128:256], cc_v[l])

            cast_layer(0)
            ckpt('cast')
            t_wsws = [T("wsw%d" % i) for i in range(L)]
            ksw = T("ksw")
            def swap_layer(l):
                ws, tw = wslot()
                ws2, tw2 = wslot()
                for (so, wd, dst) in ((O_QA, 512, 0), (O_KA, 128, 512), (O_QC, 512, 640), (O_KC, 128, 1152), (O_KBR - 64, 96, 1280)):
                    src_v = ws[:, 0:8 * wd].rearrange("p (k c) -> p k c", k=8)
                    dst_v = ws2[:, 0:8 * wd].rearrange("p (k c) -> p k c", k=8)
                    P.dma('sp', src_v, win_b.ap()[l, :, so:so + wd].rearrange("(k p) c -> p k c", p=128), r=[t_wcs[l]], w=[tw], key=tw)
                    if wd == 96:
                        P.op('dve', lambda e, d=dst_v, s=src_v: e.tensor_copy(out=d[:, :, 0:64], in_=s[:, :, 0:64]), r=[tw], w=[tw2])
                        for b in range(2):
                            sv = src_v[:, :, 64:96].rearrange("p k (a b i) -> p k a b i", a=2, b=2)[:, :, :, 1 - b, :]
                            dv = dst_v[:, :, 64:96].rearrange("p k (a b i) -> p k a b i", a=2, b=2)[:, :, :, b, :]
                            P.op('dve', lambda e, d=dv, s=sv: e.tensor_copy(out=d, in_=s), r=[tw], wp=[tw2])
                    else:
                        for b in range(2):
                            sv = src_v.rearrange("p k (h a b i) -> p k h a b i", a=2, b=2, i=16)[:, :, :, :, 1 - b, :]
                            dv = dst_v.rearrange("p k (h a b i) -> p k h a b i", a=2, b=2, i=16)[:, :, :, :, b, :]
                            for k in range(8):
                                P.op('dve', lambda e, d=dv[:, k], s=sv[:, k]: e.tensor_copy(out=d, in_=s), r=[tw], wp=[tw2] if (b or k) else (), w=() if (b or k) else [tw2])
                    P.dma('pool', wsw_b.ap()[l, :, dst:dst + wd].rearrange("(k p) c -> p k c", p=128), dst_v, r=[tw2], wp=[t_wsws[l]], key=ksw)
                src_v = ws[:, 0:3 * 768].rearrange("p (k c) -> p k c", k=3)
                dst_v = ws2[:, 0:3 * 768].rearrange("p (k c) -> p k c", k=3)
                P.dma('sp', src_v, wqup_b.ap()[l].rearrange("(k p) c -> p k c", p=128), r=[t_wcs[l]], w=[tw], key=tw)
                P.op('dve', lambda e, d=dst_v, s=src_v: e.tensor_copy(out=d, in_=s), r=[tw], w=[tw2])
                for b in range(2):
                    for k in range(3):
                        sv = src_v[:, k].rearrange("p (h c) -> p h c", h=8)[:, :, 64:96].rearrange("p h (a b i) -> p h a b i", a=2, b=2)[:, :, :, 1 - b, :]
                        dv = dst_v[:, k].rearrange("p (h c) -> p h c", h=8)[:, :, 64:96].rearrange("p h (a b i) -> p h a b i", a=2, b=2)[:, :, :, b, :]
                        P.op('dve', lambda e, d=dv, s=sv: e.tensor_copy(out=d, in_=s), r=[tw], wp=[tw2])
                P.dma('pool', wqupsw_b.ap()[l].rearrange("(k p) c -> p k c", p=128), dst_v, r=[tw2], wp=[t_wsws[l]], key=ksw)

            swap_layer(0)
            ckpt('swap')
            kv = T("kvec")
            t_vec = T("vec")
            P.dma('sp', scT[:], cvecT, w=[t_vec], key=kv)
            P.dma('sp', bmodT[:], h_bmodT, wp=[t_vec], key=kv)
            P.dma('sp', gvecs[:].rearrange("p a l k -> p (a l k)"), h_gvecs, wp=[t_vec], key=kv)
            P.dma('sp', gqaT[:], h_gqaT, wp=[t_vec], key=kv)
            P.dma('sp', gkvaT[:], h_gkvaT, wp=[t_vec], key=kv)
            P.dma('sp', gq4[:, 0, :], h_gqc2, wp=[t_vec], key=kv)
            P.dma('sp', gq4[:, 2, :], h_gkc2, wp=[t_vec], key=kv)
            P.dma('sp', sinkE[:], h_sinkb, wp=[t_vec], key=kv)
            P.op('act', lambda e: e.activation(out=scT[:], in_=scT[:], func=AF.Silu), r=[t_vec], wp=[t_vec])
            P.op('act', lambda e: e.activation(out=sinkE[:], in_=sinkE[:], func=AF.Exp), r=[t_vec], wp=[t_vec])
            pb, tb = bank()
            P.op('pe', lambda e, pb=pb: e.matmul(pb[:, 0:L], lhsT=permF[:], rhs=gq4[:, 0, :], start=True, stop=True), r=[t_vec, t_const], w=[tb])
            P.op('dve', lambda e, pb=pb: e.tensor_copy(out=gq4[:, 1, :], in_=pb[:, 0:L]), r=[tb], wp=[t_gq4])
            pb, tb = bank()
            P.op('pe', lambda e, pb=pb: e.matmul(pb[:, 0:L], lhsT=permF[:], rhs=gq4[:, 2, :], start=True, stop=True), r=[t_vec, t_const], w=[tb])
            P.op('dve', lambda e, pb=pb: e.tensor_copy(out=gq4[:, 3, :], in_=pb[:, 0:L]), r=[tb], wp=[t_gq4])

            ckpt('vec')
            def mod_layer(l):
                pb, tb = bank()
                for g in range(12):
                    ws, tw = wslot()
                    ws2, tw2 = wslot()
                    wv = [ws.bitcast(F32).rearrange("p (k c) -> p k c", k=4), ws2.bitcast(F32).rearrange("p (k c) -> p k c", k=4)]
                    P.dma('sp', wv[0], w_mod[l, 0:512, g * 512:(g + 1) * 512].rearrange("(k p) c -> p k c", p=128), w=[tw], key=tw)
                    P.dma('sp', wv[1], w_mod[l, 512:1024, g * 512:(g + 1) * 512].rearrange("(k p) c -> p k c", p=128), w=[tw2], key=tw2)

                    def mm(e, pb=pb, wv=wv, g=g):
                        inst = None
                        for nb in range(4):
                            n = g * 4 + nb
                            for k in range(8):
                                inst = e.matmul(pb[:, n * 2:n * 2 + 2], lhsT=wv[k // 4][:, k % 4, nb * 128:(nb + 1) * 128], rhs=scT[:, k, :],
                                                start=(k == 0), stop=(k == 7))
                        return inst
                    P.op('pe', mm, r=[tw, tw2, t_vec], wp=[tb] if g else (), w=() if g else [tb])
                for j in range(2):
                    P.op('dve', lambda e, pb=pb, l=l, j=j: e.tensor_tensor(out=modraw[:, l, :, j], in0=pb[:, 0:96].rearrange("p (n j) -> p n j", j=2)[:, :, j],
                                                                          in1=bmodT[:, l, :], op=ALU.add), r=[tb, t_vec], wp=[t_modraw])
            mod_layer(0)
            def mv_layer(l):
                for j in range(2):
                    mr = lambda i, l=l, j=j: modraw[:, l, i * 8:(i + 1) * 8, j]
                    P.op('dve', lambda e, l=l, j=j, mr=mr: e.scalar_tensor_tensor(out=MV[:, 0, l, j, :], in0=mr(1), scalar=1.0, in1=gvecs[:, 0, l, :], op0=ALU.add, op1=ALU.mult), r=[t_modraw, t_vec], wp=[t_MV])
                    P.op('dve', lambda e, l=l, j=j, mr=mr: e.tensor_copy(out=MV[:, 1, l, j, :], in_=mr(0)), r=[t_modraw], wp=[t_MV])
                    P.op('dve', lambda e, l=l, j=j, mr=mr: e.tensor_tensor(out=MV[:, 2, l, j, :], in0=mr(2), in1=gvecs[:, 1, l, :], op=ALU.mult), r=[t_modraw, t_vec], wp=[t_MV])
                    P.op('dve', lambda e, l=l, j=j, mr=mr: e.scalar_tensor_tensor(out=MV[:, 3, l, j, :], in0=mr(4), scalar=1.0, in1=gvecs[:, 2, l, :], op0=ALU.add, op1=ALU.mult), r=[t_modraw, t_vec], wp=[t_MV])
                    P.op('dve', lambda e, l=l, j=j, mr=mr: e.tensor_copy(out=MV[:, 4, l, j, :], in_=mr(3)), r=[t_modraw], wp=[t_MV])
                    P.op('dve', lambda e, l=l, j=j, mr=mr: e.tensor_tensor(out=MV[:, 5, l, j, :], in0=mr(5), in1=gvecs[:, 3, l, :], op=ALU.mult), r=[t_modraw, t_vec], wp=[t_MV])
            mv_layer(0)
            t_par = [t_MV, t_vec, t_gq4, t_const]

            ckpt('mod')
            t_xps = [T("xps%d" % i) for i in range(2)]
            kxs = T("kxs")
            for grp, xin, ntile in ((1, xs_in, 16), (0, xp_in, 8)):
                for tt in range(ntile):
                    c = tt // 4
                    xt = TMP[:, 0:4].rearrange("p a c -> p (a c)") if tt % 2 == 0 else TMP[:, 4:6].rearrange("p a c -> p (a c)")
                    xt = TMP[:, (tt % 2) * 2:(tt % 2) * 2 + 2].rearrange("p a c -> p (a c)")
                    tx = t_TMP[(tt % 2) * 2]
                    tx2 = t_TMP[(tt % 2) * 2 + 1]
                    P.dma('sp', xt, xin[tt * 128:(tt + 1) * 128, :], w=[tx, tx2], key=tx)
                    for half in range(2):
                        pb, tb = bank()

                        def tr(e, pb=pb, xt=xt, half=half):
                            inst = None
                            for q in range(4):
                                k = half * 4 + q
                                inst = e.transpose(pb[:, q * 128:(q + 1) * 128], xt[:, k * 128:(k + 1) * 128], identF[:])
                            return inst
                        P.op('pe', tr, r=[tx, tx2, t_const], w=[tb])
                        if grp == 1:
                            dstv = xTs[:, half * 4:(half + 1) * 4, tt * 128:(tt + 1) * 128]
                            P.op('act' if half else 'dve',
                                 (lambda e, d=dstv, pb=pb: e.activation(out=d, in_=pb[:].rearrange("p (q c) -> p q c", q=4), func=AF.Identity)) if half else
                                 (lambda e, d=dstv, pb=pb: e.tensor_copy(out=d, in_=pb[:].rearrange("p (q c) -> p q c", q=4))),
                                 r=[tb], wp=[t_xTs[c]])
                        else:
                            dstv = XC[:, half * 4:(half + 1) * 4, (tt % 4) * 128:(tt % 4 + 1) * 128]
                            P.op('act' if half else 'dve',
                                 (lambda e, d=dstv, pb=pb: e.activation(out=d, in_=pb[:].rearrange("p (q c) -> p q c", q=4), func=AF.Identity)) if half else
                                 (lambda e, d=dstv, pb=pb: e.tensor_copy(out=d, in_=pb[:].rearrange("p (q c) -> p q c", q=4))),
                                 r=[tb], wp=[t_XC])
                    if grp == 0 and tt % 4 == 3:
                        P.dma('pool', xp_scr.ap()[c], XC[:].rearrange("p k c -> p (k c)"), r=[t_XC], w=[t_xps[c]], key=t_XC)

            ckpt('xT')
            t_kctxs = [T("kctx%d" % i) for i in range(L)]
            kck = T("kck")
            def ctx_layer(l):
                ws, tw = wslot()
                wkv = ws[:, 0:2048].rearrange("p (k c) -> p k c", k=2)
                P.dma('sp', wkv, wkvup_b.ap()[l].rearrange("(k p) c -> p k c", p=128), r=[t_wcs[l]], w=[tw], key=tw)
                ckf = TMP[:, 0:2].rearrange("p a c -> p (a c)")
                P.dma('sp', TMP[:, 0].rearrange("p (t c) -> p t c", t=4), ca_k[l].rearrange("(t p) c -> p t c", p=128), w=[t_TMP[0]], key=t_TMP[0])
                P.dma('sp', TMP[:, 1].rearrange("p (t c) -> p t c", t=4), cc_k[l].rearrange("(t p) c -> p t c", p=128), w=[t_TMP[1]], key=t_TMP[1])
                P.dma('sp', TMP[:, 2:4].rearrange("p a c -> p (a c)").rearrange("p (t c) -> p t c", t=4), cb_ckv[l].rearrange("(t p) c -> p t c", p=128), w=[t_TMP[2]], key=t_TMP[2])
                krp = TMP[:, 4, 0:384].rearrange("p (t c) -> p t c", t=4)
                P.op('pool', lambda e, krp=krp: e.memset(krp, 0.0), w=[t_TMP[4]])
                P.dma('sp', krp[:, :, 64:96], cb_kr[l].rearrange("(t p) c -> p t c", p=128), r=[], wp=[t_TMP[4]], key=t_TMP[4], slow=True)
                for si, (srcv, ts_, rows) in enumerate(((TMP[:, 0], t_TMP[0], 0), (TMP[:, 1], t_TMP[1], 128))):
                    pb, tb = bank()

                    def tr(e, pb=pb, srcv=srcv):
                        inst = None
                        for tt in range(4):
                            inst = e.transpose(pb[:, tt * 128:(tt + 1) * 128], srcv[:, tt * 128:(tt + 1) * 128], identF[:])
                        return inst
                    P.op('pe', tr, r=[ts_, t_const], w=[tb])
                    P.op('act', lambda e, pb=pb, si=si: e.activation(out=TMPB[:, si, :], in_=pb[:], func=AF.Identity), r=[tb], w=[t_TMPB[si]])
                    P.dma('pool', kctx.ap()[l, rows:rows + 128, :], TMPB[:, si, :], r=[t_TMPB[si]], wp=[t_kctxs[l]], key=t_TMPB[si])
                for j in range(2):
                    pb, tb = bank()

                    def tr(e, pb=pb, j=j):
                        inst = None
                        for tt in range(4):
                            inst = e.transpose(pb[:, tt * 128:(tt + 1) * 128], TMP[:, 2 + tt // 2, (tt % 2) * 256 + j * 128:(tt % 2) * 256 + (j + 1) * 128], identF[:])
                        return inst
                    P.op('pe', tr, r=[t_TMP[2], t_const], w=[tb])
                    P.op('act', lambda e, pb=pb, j=j: e.activation(out=TMPB[:, 2 + j, :], in_=pb[:], func=AF.Identity), r=[tb], w=[t_TMPB[2 + j]])
                pb, tb = bank()

                def tr(e, pb=pb, krp=krp):
                    inst = None
                    for tt in range(4):
                        inst = e.transpose(pb[0:96, tt * 128:(tt + 1) * 128], krp[:, tt, :], identF[:])
                    return inst
                P.op('pe', tr, r=[t_TMP[4], t_const], w=[tb])
                krb = BIG[0:96, 0:512]
                P.op('act', lambda e, pb=pb, krb=krb: e.activation(out=krb[64:96, :], in_=pb[64:96, :], func=AF.Identity), r=[tb], w=[t_BIG])
                for h in range(8):
                    pb, tb = bank()

                    def mm(e, pb=pb, h=h, wkv=wkv):
                        inst = None
                        for j in range(2):
                            inst = e.matmul(pb[0:64, :], lhsT=wkv[:, j, h * 64:(h + 1) * 64], rhs=TMPB[:, 2 + j, :], start=(j == 0), stop=(j == 1))
                        return inst
                    P.op('pe', mm, r=[tw, t_TMPB[2], t_TMPB[3]], w=[tb])
                    kh = BIG[0:96, 512 * (1 + h % 2):512 * (2 + h % 2)]
                    tkh = t_TMP[h % 2]
                    P.op('act', lambda e, pb=pb, kh=kh: e.activation(out=kh[0:64, :], in_=pb[0:64, :], func=AF.Identity), r=[tb], w=[tkh])
                    P.op('pool', lambda e, kh=kh, krb=krb: e.tensor_copy(out=kh[64:96, :], in_=krb[64:96, :]), r=[t_BIG], wp=[tkh])
                    P.dma('pool', kctx.ap()[l, 256 + h * 96:256 + (h + 1) * 96, :], kh, r=[tkh, t_BIG], wp=[t_kctxs[l]], key=tkh)
                for tt in range(4):
                    pb, tb = bank()

                    def mm(e, pb=pb, tt=tt, wkv=wkv):
                        inst = None
                        for j in range(2):
                            inst = e.matmul(pb[:, :], lhsT=TMPB[:, 2 + j, tt * 128:(tt + 1) * 128], rhs=wkv[:, j, 512:1024], start=(j == 0), stop=(j == 1))
                        return inst
                    P.op('pe', mm, r=[tw, t_TMPB[2], t_TMPB[3]], w=[tb])
                    vst = BIG[:, 2048 + (tt % 2) * 512: 2048 + (tt % 2 + 1) * 512]
                    tvs = t_TMP[4 + tt % 2]
                    P.op('dve', lambda e, pb=pb, vst=vst: e.tensor_copy(out=vst, in_=pb[:]), r=[tb], w=[tvs])
                    P.dma('pool', vctx.ap()[l, tt * 128:(tt + 1) * 128, 256:768], vst, r=[tvs, t_BIG], wp=[t_kctxs[l]], key=tvs)
            ctx_layer(0)

            ckpt('ctx')
            t_hscr = [T("hscr%d" % i) for i in range(6)]
            t_q = {0: T("q_p"), 1: T("q_s")}
            t_k = {0: T("k_p"), 1: T("kg_in")}
            t_v = {0: T("v_p"), 1: T("vg_in")}
            t_o = {0: T("o_p"), 1: T("o_s")}
            t_kgo = T("kg_out"); t_vgo = T("vg_out")
            kst = T("kstore")
            t_outs = T("outs")

            def rms_stat(src_t, nk, sq_views, scale, out_rstd, t_out, blockdiag=False):
                pb, tb = bank()

                def mm(e, pb=pb):
                    inst = None
                    for k in range(nk):
                        inst = e.matmul(pb[:], lhsT=(bonesB if blockdiag else onesB)[:], rhs=sq_views[k], start=(k == 0), stop=(k == nk - 1))
                    return inst
                P.op('pe', mm, r=list(src_t) + [t_const], w=[tb])
                P.op('act', lambda e, pb=pb: e.activation(out=out_rstd, in_=pb[:], func=AF.Ln, scale=scale, bias=EPS), r=[tb], w=[t_out])
                P.op('act', lambda e: e.activation(out=out_rstd, in_=out_rstd, func=AF.Exp, scale=-0.5), r=[t_out], w=[t_out])

            def prenorm(xv, t_x, l, j, ia, ish):
                for k in range(8):
                    P.op('pool', lambda e, k=k: e.tensor_tensor(out=HT[:, k, :], in0=xv[:, k, :], in1=xv[:, k, :], op=ALU.mult), r=[t_x], w=[t_HT] if k == 0 else (), wp=() if k == 0 else [t_HT])
                rstd = TMP[:, 5, :]
                rms_stat([t_HT], 8, [HT[:, k, :] for k in range(8)], 1.0 / D, rstd, t_TMP[5])
                for k in range(8):
                    P.op('dve', lambda e, k=k: e.scalar_tensor_tensor(out=RES[:, k, :], in0=xv[:, k, :], scalar=MV[:, ia, l, j, k:k + 1], in1=rstd, op0=ALU.mult, op1=ALU.mult),
                         r=[t_x, t_TMP[5]] + t_par, w=[t_RES] if k == 0 else (), wp=() if k == 0 else [t_RES])
                for k in range(8):
                    P.op('act', lambda e, k=k: e.activation(out=HT[:, k, :], in_=RES[:, k, :], func=AF.Identity, bias=MV[:, ish, l, j, k:k + 1], scale=1.0),
                         r=[t_RES] + t_par, w=[t_HT] if k == 0 else (), wp=() if k == 0 else [t_HT])

            def postnorm_resid(xv, t_x, l, j, ig):
                for k in range(8):
                    P.op('act', lambda e, k=k: e.activation(out=HT[:, k, :], in_=RES[:, k, :], func=AF.Square), r=[t_RES], w=[t_HT] if k == 0 else (), wp=() if k == 0 else [t_HT])
                rstd = TMP[:, 5, :]
                rms_stat([t_HT], 8, [HT[:, k, :] for k in range(8)], 1.0 / D, rstd, t_TMP[5])
                for k in range(8):
                    P.op('dve', lambda e, k=k: e.scalar_tensor_tensor(out=RES[:, k, :], in0=RES[:, k, :], scalar=MV[:, ig, l, j, k:k + 1], in1=rstd, op0=ALU.mult, op1=ALU.mult),
                         r=[t_TMP[5]] + t_par, wp=[t_RES])
                for k in range(8):
                    P.op('pool', lambda e, k=k: e.tensor_tensor(out=xv[:, k, :], in0=xv[:, k, :], in1=RES[:, k, :], op=ALU.add), r=[t_RES], wp=[t_x])

            def load_w(dram_view, shape_k, ncols, deps):
                ws, tw = wslot()
                v = ws[:, 0:shape_k * ncols].rearrange("p (k c) -> p k c", k=shape_k)
                P.dma('sp', v, dram_view, r=deps, w=[tw], key=tw)
                return v, tw

            def proj_fm(wv, tw, c0, m, nk, rhs_of_k, rhs_t, prow=None):
                pb, tb = bank()

                def mm(e, pb=pb):
                    inst = None
                    for k in range(nk):
                        inst = e.matmul(pb[0:m, :], lhsT=wv[:, k, c0:c0 + m], rhs=rhs_of_k(k), start=(k == 0), stop=(k == nk - 1))
                    return inst
                P.op('pe', mm, r=[tw] + list(rhs_t), w=[tb])
                return pb, tb

            def D1(l, grp, ci, xv, t_x, col0):
                j = grp
                hidx = ci if grp == 0 else 2 + ci
                q_d = (q_p if grp == 0 else q_s).ap()
                cs = slice(col0, col0 + CH)
                if grp == 0:
                    k_d = k_p.ap()
                    v_dst = v_p.ap()[col0:col0 + CH, :]
                    kcs = cs
                else:
                    k_d = kgi[ci // 2].ap()
                    v_dst = vgi[ci // 2].ap()[(ci % 2) * CH:(ci % 2 + 1) * CH, :]
                    kcs = slice((ci % 2) * CH, (ci % 2 + 1) * CH)
                prenorm(xv, t_x, l, j, 0, 1)
                P.dma('pool', h_scr.ap()[hidx], HTflat, r=[t_HT], w=[t_hscr[hidx]], key=t_HT)
                if grp == 1:
                    P.dma('sp', ROPE[:, 0:2, :], c_ropeA[:, :, cs].rearrange("t p c -> p t c"), w=[t_ROPE], key=t_ROPE)
                    P.dma('sp', ROPE[0:96, 2:4, :], c_ropeB[:, :, cs].rearrange("t p c -> p t c"), wp=[t_ROPE], key=t_ROPE)
                ckpt('d1a')
                hk = lambda k: HT[:, k, :]
                stage_ctr = [0]

                def stage_bf():
                    i = stage_ctr[0] % 2
                    stage_ctr[0] += 1
                    return TMPB[:, i, :], t_TMPB[i]

                def rope_combine(pb1, tb1, pb2, tb2, rows, ci_c, ci_s, outv, t_out, gcol=None, rstd=None, t_rstd=None):
                    r0, r1 = rows
                    if gcol is None:
                        P.op('dve', lambda e: e.tensor_tensor(out=TMP[r0:r1, 0, :], in0=pb1[r0:r1, :], in1=ROPE[r0:r1, ci_c, :], op=ALU.mult), r=[tb1, t_ROPE], w=[t_TMP[0]])
                        P.op('dve', lambda e: e.tensor_tensor(out=TMP[r0:r1, 1, :], in0=pb2[r0:r1, :], in1=ROPE[r0:r1, ci_s, :], op=ALU.mult), r=[tb2, t_ROPE], w=[t_TMP[1]])
                    else:
                        P.op('act', lambda e: e.activation(out=TMP[r0:r1, 0, :], in_=pb1[r0:r1, :], func=AF.Identity, scale=gq4[r0:r1, gcol, l:l + 1]), r=[tb1] + t_par, w=[t_TMP[0]])
                        P.op('act', lambda e: e.activation(out=TMP[r0:r1, 1, :], in_=pb2[r0:r1, :], func=AF.Identity, scale=gq4[r0:r1, gcol + 1, l:l + 1]), r=[tb2] + t_par, w=[t_TMP[1]])
                        P.op('dve', lambda e: e.tensor_tensor(out=TMP[r0:r1, 0, :], in0=TMP[r0:r1, 0, :], in1=ROPE[r0:r1, ci_c, :], op=ALU.mult), r=[t_ROPE], w=[t_TMP[0]])
                        P.op('dve', lambda e: e.tensor_tensor(out=TMP[r0:r1, 1, :], in0=TMP[r0:r1, 1, :], in1=ROPE[r0:r1, ci_s, :], op=ALU.mult), r=[t_ROPE], w=[t_TMP[1]])
                    if rstd is None:
                        P.op('pool', lambda e: e.tensor_tensor(out=outv[r0:r1, :], in0=TMP[r0:r1, 0, :], in1=TMP[r0:r1, 1, :], op=ALU.add), r=[t_TMP[0], t_TMP[1]], w=[t_out])
                    else:
                        P.op('pool', lambda e: e.tensor_tensor(out=TMP[r0:r1, 0, :], in0=TMP[r0:r1, 0, :], in1=TMP[r0:r1, 1, :], op=ALU.add), r=[t_TMP[1]], w=[t_TMP[0]])
                        P.op('dve', lambda e: e.tensor_tensor(out=outv[r0:r1, :], in0=TMP[r0:r1, 0, :], in1=rstd, op=ALU.mult), r=[t_TMP[0], t_rstd], w=[t_out])

                def out_tok(srcF, t_src, rows, ncols_per, dst_ap, colsel=None):
                    pb, tb = bank()

                    def tr(e, pb=pb):
                        inst = None
                        for tt in range(4):
                            inst = e.transpose(pb[:, tt * 128:tt * 128 + rows], srcF[0:rows, tt * 128:(tt + 1) * 128], identF[0:rows, 0:rows])
                        return inst
                    P.op('pe', tr, r=[t_src, t_const], w=[tb])
                    stg = TMP[:, 4, :]
                    P.op('dve', lambda e, pb=pb: e.tensor_copy(out=stg, in_=pb[:]), r=[tb], w=[t_TMP[4]])
                    sv = stg.rearrange("p (t c) -> p t c", t=4)
                    if colsel is not None:
                        sv = sv[:, :, colsel[0]:colsel[1]]
                    else:
                        sv = sv[:, :, 0:ncols_per]
                    for s_ in range(2):
                        P.dma('pool', dst_ap[s_], sv[:, 2 * s_:2 * s_ + 2, :], r=[t_TMP[4]], wp=[t_outs], key=t_TMP[4], slow=True)

                def tok_dst(o_ap, c0=None, c1=None):
                    vs_ = []
                    for s_ in range(2):
                        v = o_ap[2 * ci + s_, l].rearrange("(u p) f -> p u f", p=128)
                        if c0 is not None:
                            v = v[:, :, c0:c1]
                        vs_.append(v)
                    return vs_

                for (off, swoff, nblk, qrow0, is_c) in ((O_QA, 0, 4, 0, False), (O_KA, 512, 1, None, False), (O_QC, 640, 4, 1280, True), (O_KC, 1152, 1, None, True)):
                    wd = nblk * 128
                    is_k = nblk == 1
                    wv, tw = load_w(win_b.ap()[l, :, off:off + wd].rearrange("(k p) c -> p k c", p=128), 8, wd, [t_wcs[l]])
                    if grp == 1:
                        wv2, tw2 = load_w(wsw_b.ap()[l, :, swoff:swoff + wd].rearrange("(k p) c -> p k c", p=128), 8, wd, [t_wsws[l]])
                    for b in range(nblk):
                        pb1, tb1 = proj_fm(wv, tw, b * 128, 128, 8, hk, [t_HT])
                        if grp == 1:
                            pb2, tb2 = proj_fm(wv2, tw2, b * 128, 128, 8, hk, [t_HT])
                        outv, t_out = stage_bf()
                        rstd = None
                        if is_c:
                            P.op('act', lambda e, pb1=pb1: e.activation(out=TMPB[:, 2, :], in_=pb1[:], func=AF.Square), r=[tb1], w=[t_TMPB[2]])
                            rstd = TMP[:, 2, :]
                            rms_stat([t_TMPB[2]], 1, [TMPB[:, 2, :]], 1.0 / 64, rstd, t_TMP[2], blockdiag=True)
                        gcol = (2 if is_k else 0) if is_c else None
                        if grp == 1:
                            rope_combine(pb1, tb1, pb2, tb2, (0, 128), 0, 1, outv, t_out, gcol=gcol, rstd=rstd, t_rstd=t_TMP[2])
                        else:
                            if is_c:
                                P.op('act', lambda e, pb1=pb1, gcol=gcol: e.activation(out=TMP[:, 3, :], in_=pb1[:], func=AF.Identity, scale=gq4[:, gcol, l:l + 1]), r=[tb1] + t_par, w=[t_TMP[3]])
                                P.op('dve', lambda e, rstd=rstd: e.tensor_tensor(out=TMP[:, 3, :], in0=TMP[:, 3, :], in1=rstd, op=ALU.mult), r=[t_TMP[2]], w=[t_TMP[3]])
                            else:
                                P.op('dve', lambda e, pb1=pb1: e.tensor_copy(out=TMP[:, 3, :], in_=pb1[:]), r=[tb1], w=[t_TMP[3]])
                            P.op('act', lambda e, outv=outv: e.activation(out=outv, in_=TMP[:, 3, :], func=AF.Identity), r=[t_TMP[3]], w=[t_out])
                            if is_k:
                                out_tok(TMP[:, 3, :], t_TMP[3], 128, 128, tok_dst(o_ck if is_c else o_ak))
                        if l == 0 and grp == 1 and ci == 0:
                            ckpt('blk_%d_%d' % (off, b))
                        if is_k:
                            krow = 128 if is_c else 0
                            P.dma('pool', k_d[krow:krow + 128, kcs], outv, r=[t_out], wp=[t_k[grp]], key=t_out)
                        else:
                            P.dma('pool', q_d[qrow0 + b * 128:qrow0 + (b + 1) * 128, cs], outv, r=[t_out], wp=[t_q[grp]], key=t_out)

                ckpt('d1b')
                wv, tw = load_w(win_b.ap()[l, :, O_VA:O_VA + 128].rearrange("(k p) c -> p k c", p=128), 8, 128, [t_wcs[l]])
                wvc, twc = load_w(win_b.ap()[l, :, O_VC:O_VC + 128].rearrange("(k p) c -> p k c", p=128), 8, 128, [t_wcs[l]])
                vstage = BIG[:, 0:4 * 768].rearrange("p (t c) -> p t c", t=4)
                for vi, (wvx, twx, o_dst) in enumerate(((wv, tw, o_av), (wvc, twc, o_cv))):
                    pb, tb = bank()

                    def mm(e, pb=pb, wvx=wvx):
                        inst = None
                        for tt in range(4):
                            for k in range(8):
                                inst = e.matmul(pb[:, tt * 128:(tt + 1) * 128], lhsT=HT[:, k, tt * 128:(tt + 1) * 128], rhs=wvx[:, k, :], start=(k == 0), stop=(k == 7))
                        return inst
                    P.op('pe', mm, r=[twx, t_HT], w=[tb])
                    P.op('act', lambda e, pb=pb, vi=vi: e.activation(out=vstage[:, :, vi * 128:(vi + 1) * 128], in_=pb[:].rearrange("p (t c) -> p t c", t=4), func=AF.Identity),
                         r=[tb], w=[t_BIG] if vi == 0 else (), wp=() if vi == 0 else [t_BIG])
                    if grp == 0:
                        P.op('dve', lambda e, pb=pb: e.tensor_copy(out=TMP[:, 4, :], in_=pb[:]), r=[tb], w=[t_TMP[4]])
                        for s_ in range(2):
                            P.dma('pool', tok_dst(o_dst)[s_], TMP[:, 4, :].rearrange("p (t c) -> p t c", t=4)[:, 2 * s_:2 * s_ + 2, :], r=[t_TMP[4]], wp=[t_outs], key=t_TMP[4])

                ckpt('d1c')
                wv, tw = load_w(win_b.ap()[l, :, O_QBD:O_QBD + 384].rearrange("(k p) c -> p k c", p=128), 8, 384, [t_wcs[l]])
                QN = BIG[:, 4096:4096 + 1536].rearrange("p (k c) -> p k c", k=3)
                t_QN = T("QN")
                QF = RES[:, 0:3, :]
                for b in range(3):
                    pb, tb = proj_fm(wv, tw, b * 128, 128, 8, hk, [t_HT])
                    P.op('act', lambda e, pb=pb, b=b: e.activation(out=QF[:, b, :], in_=pb[:], func=AF.Identity), r=[tb], w=[t_RES] if b == 0 else (), wp=() if b == 0 else [t_RES])
                    P.op('pool', lambda e, b=b: e.tensor_tensor(out=QN[:, b, :], in0=QF[:, b, :], in1=QF[:, b, :], op=ALU.mult), r=[t_RES, t_BIG], w=[t_QN] if b == 0 else (), wp=() if b == 0 else [t_QN])
                rms_stat([t_QN], 3, [QN[:, b, :] for b in range(3)], 1.0 / 384, TMP[:, 2, :], t_TMP[2])
                for b in range(3):
                    P.op('dve', lambda e, b=b: e.scalar_tensor_tensor(out=QN[:, b, :], in0=QF[:, b, :], scalar=gqaT[:, l, b:b + 1], in1=TMP[:, 2, :], op0=ALU.mult, op1=ALU.mult),
                         r=[t_RES, t_TMP[2]] + t_par, w=[t_QN] if b == 0 else (), wp=() if b == 0 else [t_QN])
                wq, twq = load_w(wqup_b.ap()[l].rearrange("(k p) c -> p k c", p=128), 3, 768, [t_wcs[l]])
                if grp == 1:
                    wq2, twq2 = load_w(wqupsw_b.ap()[l].rearrange("(k p) c -> p k c", p=128), 3, 768, [t_wsws[l]])
                qn_k = lambda k: QN[:, k, :]
                for h in range(8):
                    pb1, tb1 = proj_fm(wq, twq, h * 96, 96, 3, qn_k, [t_QN])
                    outv, t_out = stage_bf()
                    if grp == 1:
                        pb2, tb2 = proj_fm(wq2, twq2, h * 96, 96, 3, qn_k, [t_QN])
                        rope_combine(pb1, tb1, pb2, tb2, (0, 96), 2, 3, outv, t_out)
                    else:
                        P.op('act', lambda e, pb1=pb1, outv=outv: e.activation(out=outv[0:96, :], in_=pb1[0:96, :], func=AF.Identity), r=[tb1], w=[t_out])
                    P.dma('pool', q_d[512 + h * 96:512 + (h + 1) * 96, cs], outv[0:96, :], r=[t_out], wp=[t_q[grp]], key=t_out)

                ckpt('d1d')
                wv, tw = load_w(win_b.ap()[l, :, O_KVBD:O_KVBD + 288].rearrange("(k p) c -> p k c", p=128), 8, 288, [t_wcs[l]])
                CKN = BIG[:, 4096:4096 + 1024].rearrange("p (k c) -> p k c", k=2)
                KF = RES[:, 0:2, :]
                for b in range(2):
                    pb, tb = proj_fm(wv, tw, b * 128, 128, 8, hk, [t_HT])
                    P.op('act', lambda e, pb=pb, b=b: e.activation(out=KF[:, b, :], in_=pb[:], func=AF.Identity), r=[tb], w=[t_RES] if b == 0 else (), wp=() if b == 0 else [t_RES])
                    P.op('pool', lambda e, b=b: e.tensor_tensor(out=CKN[:, b, :], in0=KF[:, b, :], in1=KF[:, b, :], op=ALU.mult), r=[t_RES, t_BIG], w=[t_QN] if b == 0 else (), wp=() if b == 0 else [t_QN])
                rms_stat([t_QN], 2, [CKN[:, b, :] for b in range(2)], 1.0 / 256, TMP[:, 2, :], t_TMP[2])
                for b in range(2):
                    P.op('dve', lambda e, b=b: e.scalar_tensor_tensor(out=KF[:, b, :], in0=KF[:, b, :], scalar=gkvaT[:, l, b:b + 1], in1=TMP[:, 2, :], op0=ALU.mult, op1=ALU.mult),
                         r=[t_TMP[2]] + t_par, wp=[t_RES])
                    P.op('act', lambda e, b=b: e.activation(out=CKN[:, b, :], in_=KF[:, b, :], func=AF.Identity), r=[t_RES], w=[t_QN] if b == 0 else (), wp=() if b == 0 else [t_QN])
                    if grp == 0:
                        out_tok(KF[:, b, :], t_RES, 128, 128, tok_dst(o_ckv, b * 128, (b + 1) * 128))
                pbk, tbk = proj_fm(wv, tw, 192, 96, 8, hk, [t_HT])
                KR = TMPB[:, 3, :]
                if grp == 1:
                    wv2, tw2 = load_w(wsw_b.ap()[l, :, 1280:1376].rearrange("(k p) c -> p k c", p=128), 8, 96, [t_wsws[l]])
                    pbk2, tbk2 = proj_fm(wv2, tw2, 0, 96, 8, hk, [t_HT])
                    rope_combine(pbk, tbk, pbk2, tbk2, (64, 96), 2, 3, KR, t_TMPB[3])
                else:
                    P.op('dve', lambda e: e.tensor_copy(out=TMP[0:96, 3, :], in_=pbk[0:96, :]), r=[tbk], w=[t_TMP[3]])
                    P.op('act', lambda e: e.activation(out=KR[64:96, :], in_=TMP[64:96, 3, :], func=AF.Identity), r=[t_TMP[3]], w=[t_TMPB[3]])
                    out_tok(TMP[:, 3, :], t_TMP[3], 96, 96, tok_dst(o_kr), colsel=(64, 96))
                wk, twk = load_w(wkvup_b.ap()[l].rearrange("(k p) c -> p k c", p=128), 2, 1024, [t_wcs[l]])
                ck_k = lambda k: CKN[:, k, :]
                for h in range(8):
                    pb, tb = proj_fm(wk, twk, h * 64, 64, 2, ck_k, [t_QN])
                    outv, t_out = stage_bf()
                    P.op('act', lambda e, pb=pb, outv=outv: e.activation(out=outv[0:64, :], in_=pb[0:64, :], func=AF.Identity), r=[tb], w=[t_out])
                    P.op('pool', lambda e, outv=outv: e.tensor_copy(out=outv[64:96, :], in_=KR[64:96, :]), r=[t_TMPB[3]], wp=[t_out])
                    P.dma('pool', k_d[256 + h * 96:256 + (h + 1) * 96, kcs], outv[0:96, :], r=[t_out], wp=[t_k[grp]], key=t_out)
                for tt in range(4):
                    pb, tb = bank()

                    def mm(e, pb=pb, tt=tt):
                        inst = None
                        for k in range(2):
                            inst = e.matmul(pb[:, :], lhsT=CKN[:, k, tt * 128:(tt + 1) * 128], rhs=wk[:, k, 512:1024], start=(k == 0), stop=(k == 1))
                        return inst
                    P.op('pe', mm, r=[twk, t_QN], w=[tb])
                    P.op('act' if tt % 2 else 'dve',
                         (lambda e, pb=pb, tt=tt: e.activation(out=vstage[:, tt, 256:768], in_=pb[:], func=AF.Identity)) if tt % 2 else
                         (lambda e, pb=pb, tt=tt: e.tensor_copy(out=vstage[:, tt, 256:768], in_=pb[:])), r=[tb], wp=[t_BIG])
                P.dma('pool', v_dst.rearrange("(t p) c -> p t c", p=128), vstage, r=[t_BIG], wp=[t_v[grp]], key=t_BIG)

            att_ctr = [0]

            def att_head(Kv, t_K, krows, Vv, t_V, Qv, t_Q, qcols, ktiles, scale, out_rows_ap, t_o_dst, sink_ap=None, odd=False, okey=None, masks=None):
                r0, r1 = krows
                q0, nq = qcols
                po, tpo = obank()
                nkt = len(ktiles)
                LOOK = 3
                pend = {}
                for i in range(nkt + LOOK):
                    if i < nkt:
                        kt = ktiles[i]
                        pbs, tbs = bank()
                        P.op('pe', lambda e, pbs=pbs, kt=kt: e.matmul(pbs[:, 0:nq], lhsT=Kv[r0:r1, kt * 128:(kt + 1) * 128], rhs=Qv[r0:r1, q0:q0 + nq], start=True, stop=True),
                             r=[t_K, t_Q], w=[tbs])
                        pend[i] = (pbs, tbs)
                    jx = i - LOOK
                    if jx >= 0:
                        kt = ktiles[jx]
                        pbs, tbs = pend.pop(jx)
                        pi = att_ctr[0] % 4
                        att_ctr[0] += 1
                        pt = TMPB[:, pi, 0:nq]
                        P.op('act', lambda e, pbs=pbs, pt=pt: e.activation(out=pt, in_=pbs[:, 0:nq], func=AF.Exp, scale=scale), r=[tbs], w=[t_TMPB[pi]])
                        if masks is not None and masks[jx] is not None:
                            P.op('dve', lambda e, pt=pt, m=masks[jx]: e.tensor_tensor(out=pt, in0=pt, in1=m, op=ALU.mult), r=[t_const], w=[t_TMPB[pi]])
                        P.op('pe', lambda e, pt=pt, kt=kt, jx=jx: e.matmul(po[:, 0:nq], lhsT=Vv[:, kt, :], rhs=pt, start=(jx == 0), stop=(jx == nkt - 1)),
                             r=[t_V, t_TMPB[pi]], w=[tpo] if jx == 0 else (), wp=() if jx == 0 else [tpo])
                ti = att_ctr[0] % 2
                dt_ = TMP[0:64, ti, 0:nq]
                tdt = t_TMP[ti]
                if sink_ap is not None:
                    P.op('dve', lambda e: e.tensor_copy(out=dt_, in_=po[64:128, 0:nq]), r=[tpo], w=[tdt])
                    P.op('dve', lambda e: e.tensor_scalar(out=dt_, in0=dt_, scalar1=sink_ap, scalar2=None, op0=ALU.add), r=t_par, w=[tdt])
                    P.op('dve', lambda e: e.reciprocal(out=dt_, in_=dt_), r=[tdt], w=[tdt])
                else:
                    P.op('dve', lambda e: e.reciprocal(out=dt_, in_=po[64:128, 0:nq]), r=[tpo], w=[tdt])
                oi = 2 + att_ctr[0] % 2
                ob = TMP[0:64, oi, :].bitcast(BF16)[:, 0:nq]
                P.op('dve', lambda e: e.tensor_tensor(out=ob, in0=po[0:64, 0:nq], in1=dt_, op=ALU.mult), r=[tpo, tdt], w=[t_TMP[oi]])
                P.dma('pool', out_rows_ap, ob, r=[t_TMP[oi]], wp=[t_o_dst], key=t_TMP[oi])

            def att_multi(Kv, t_K, krows, Vv, t_V, Qv, t_Q, scale, segs, t_o_dst):
                r0, r1 = krows
                flat = []
                for si, sg in enumerate(segs):
                    for i, kt in enumerate(sg['ktiles']):
                        flat.append((si, i, kt))
                LOOK = 3
                pend = {}
                po_of = {}
                n = len(flat)
                for x in range(n + LOOK):
                    if x < n:
                        si, i, kt = flat[x]
                        q0, nq = segs[si]['q']
                        pbs, tbs = bank()
                        P.op('pe', lambda e, pbs=pbs, kt=kt, q0=q0, nq=nq: e.matmul(pbs[:, 0:nq], lhsT=Kv[r0:r1, kt * 128:(kt + 1) * 128], rhs=Qv[r0:r1, q0:q0 + nq], start=True, stop=True),
                             r=[t_K, t_Q], w=[tbs])
                        pend[x] = (pbs, tbs)
                    y = x - LOOK
                    if y < 0:
                        continue
                    si, i, kt = flat[y]
                    sg = segs[si]
                    q0, nq = sg['q']
                    nkt = len(sg['ktiles'])
                    if i == 0:
                        po_of[si] = obank()
                    po, tpo = po_of[si]
                    pbs, tbs = pend.pop(y)
                    pi = att_ctr[0] % 4
                    att_ctr[0] += 1
                    pt = TMPB[:, pi, 0:nq]
                    P.op('act', lambda e, pbs=pbs, pt=pt, nq=nq: e.activation(out=pt, in_=pbs[:, 0:nq], func=AF.Exp, scale=scale), r=[tbs], w=[t_TMPB[pi]])
                    mk = sg.get('masks')
                    if mk is not None and mk[i] is not None:
                        P.op('dve', lambda e, pt=pt, m=mk[i]: e.tensor_tensor(out=pt, in0=pt, in1=m, op=ALU.mult), r=[t_const], w=[t_TMPB[pi]])
                    P.op('pe', lambda e, pt=pt, kt=kt, i=i, po=po, nq=nq, nkt=nkt: e.matmul(po[:, 0:nq], lhsT=Vv[:, kt, :], rhs=pt, start=(i == 0), stop=(i == nkt - 1)),
                         r=[t_V, t_TMPB[pi]], w=[tpo] if i == 0 else (), wp=() if i == 0 else [tpo])
                    if i != nkt - 1:
                        continue
                    ti = att_ctr[0] % 2
                    dt_ = TMP[0:64, ti, 0:nq]
                    tdt = t_TMP[ti]
                    sink_ap = sg.get('sink')
                    if nq <= 256:
                        P.op('dve', lambda e, dt_=dt_, po=po, nq=nq: e.tensor_copy(out=dt_, in_=po[64:128, 0:nq]), r=[tpo], w=[tdt])
                        if sink_ap is not None:
                            P.op('dve', lambda e, dt_=dt_, sink_ap=sink_ap: e.tensor_scalar(out=dt_, in0=dt_, scalar1=sink_ap, scalar2=None, op0=ALU.add), r=t_par, w=[tdt])
                        P.op('act', lambda e, dt_=dt_: e.activation(out=dt_, in_=dt_, func=AF.Ln), r=[tdt], w=[tdt])
                        P.op('act', lambda e, dt_=dt_: e.activation(out=dt_, in_=dt_, func=AF.Exp, scale=-1.0), r=[tdt], w=[tdt])
                    elif sink_ap is not None:
                        P.op('dve', lambda e, dt_=dt_, po=po, nq=nq: e.tensor_copy(out=dt_, in_=po[64:128, 0:nq]), r=[tpo], w=[tdt])
                        P.op('dve', lambda e, dt_=dt_, sink_ap=sink_ap: e.tensor_scalar(out=dt_, in0=dt_, scalar1=sink_ap, scalar2=None, op0=ALU.add), r=t_par, w=[tdt])
                        P.op('dve', lambda e, dt_=dt_: e.reciprocal(out=dt_, in_=dt_), r=[tdt], w=[tdt])
                    else:
                        P.op('dve', lambda e, dt_=dt_, po=po, nq=nq: e.reciprocal(out=dt_, in_=po[64:128, 0:nq]), r=[tpo], w=[tdt])
                    oi = 2 + att_ctr[0] % 2
                    ob = TMP[0:64, oi, :].bitcast(BF16)[:, 0:nq]
                    P.op('dve', lambda e, ob=ob, po=po, dt_=dt_, nq=nq: e.tensor_tensor(out=ob, in0=po[0:64, 0:nq], in1=dt_, op=ALU.mult), r=[tpo, tdt], w=[t_TMP[oi]])
                    P.dma('pool', sg['out'], ob, r=[t_TMP[oi]], wp=[t_o_dst], key=t_TMP[oi])

            t_KT = [T("KT0"), T("KT1")]; t_VT = [T("VT0"), T("VT1")]; t_QT = [T("QT0"), T("QT1")]
            ones_set = [False]

            def arena_to_att():
                for s_ in t_KT + t_VT + t_QT:
                    for t in arena_all:
                        _merge(s_.r, t.r)
                        _merge(s_.r, t.w)
                for i in range(2):
                    P.op('pool', lambda e, i=i: e.memset(VT[i][:, :, 64:128], 1.0), r=[], w=[t_VT[i]])

            def att_to_arena():
                for t in arena_all:
                    _merge(t.w, {})
                for t in arena_all:
                    for s in t_KT + t_VT + t_QT:
                        _merge(t.r, s.r)
                        _merge(t.r, s.w)

            def ATT_prompt(l):
                arena_to_att()
                qd, kd, vd, od = q_p.ap(), k_p.ap(), v_p.ap(), o_p.ap()
                for br, (krow, vcol, qrow, orow, sc) in enumerate(((0, 0, 0, 0, SC_A), (128, 128, 1280, 1024, SC_A))):
                    ks = br % 2
                    Kv = KT[ks][:, 0:1024]
                    P.dma('sp', Kv, kd[krow:krow + 128, :], r=[t_k[0]], w=[t_KT[ks]], key=t_KT[ks])
                    for g in range(2):
                        Vv = VT[g]
                        P.dma('sp', Vv[:, 0:8, 0:64], vd[:, vcol + g * 64:vcol + (g + 1) * 64].rearrange("(t p) c -> p t c", p=128), r=[t_v[0]], wp=[t_VT[g]], key=t_VT[g], slow=True)
                    for jb in range(4):
                        qs_ = jb % 2
                        Qv = QT[qs_][:, 0:1024]
                        P.dma('sp', Qv, qd[qrow + jb * 128:qrow + (jb + 1) * 128, :], r=[t_q[0]], w=[t_QT[qs_]], key=t_QT[qs_])
                        for g in range(2):
                            h = jb + 4 * g
                            att_multi(Kv, t_KT[ks], (g * 64, g * 64 + 64), VT[g], t_VT[g], Qv, t_QT[qs_], sc,
                                      [dict(q=(s_ * 256, 256), ktiles=[2 * s_, 2 * s_ + 1], out=od[orow + h * 64:orow + (h + 1) * 64, s_ * 256:(s_ + 1) * 256],
                                            sink=(sinkE[0:64, l, h:h + 1] if br == 0 else None)) for s_ in range(4)], t_o[0])
                for h in range(8):
                    ks = h % 2
                    Kv = KT[ks][:, 0:1024]
                    P.dma('sp', Kv[0:96, :], kd[256 + h * 96:256 + (h + 1) * 96, :], r=[t_k[0]], w=[t_KT[ks]], key=t_KT[ks])
                    Vv = VT[ks]
                    P.dma('sp', Vv[:, 0:8, 0:64], vd[:, 256 + h * 64:256 + (h + 1) * 64].rearrange("(t p) c -> p t c", p=128), r=[t_v[0]], wp=[t_VT[ks]], key=t_VT[ks], slow=True)
                    Qv = QT[ks][:, 0:1024]
                    P.dma('sp', Qv[0:96, :], qd[512 + h * 96:512 + (h + 1) * 96, :], r=[t_q[0]], w=[t_QT[ks]], key=t_QT[ks])
                    att_multi(Kv, t_KT[ks], (0, 96), Vv, t_VT[ks], Qv, t_QT[ks], SC_B,
                              [dict(q=(s_ * 256, 256), ktiles=[2 * s_, 2 * s_ + 1], out=od[512 + h * 64:512 + (h + 1) * 64, s_ * 256:(s_ + 1) * 256]) for s_ in range(4)], t_o[0])
                att_to_arena()

            def ATT_sample(l):
                arena_to_att()
                qd, od = q_s.ap(), o_s.ap()

                def load_K(ks, row0, nrows):
                    Kv = KT[ks]
                    for r in range(2):
                        for t in range(2):
                            first = (r == 0 and t == 0)
                            P.dma('sp', Kv[0:nrows, (2 * r + t) * 1024:(2 * r + t + 1) * 1024], kgo[t].ap()[r * 1024 + row0:r * 1024 + row0 + nrows, :], r=[t_kgo],
                                  w=[t_KT[ks]] if first else (), wp=() if first else [t_KT[ks]], key=t_KT[ks])
                    P.dma('sp', Kv[0:nrows, 4096:4608], kctx.ap()[l, row0:row0 + nrows, :], r=[t_kctxs[l], t_wcs[l]], wp=[t_KT[ks]], key=t_KT[ks])
                    return Kv

                def load_V(vs, col0):
                    Vv = VT[vs]
                    for r in range(2):
                        for t in range(2):
                            P.dma('sp', Vv[:, (2 * r + t) * 8:(2 * r + t + 1) * 8, 0:64], vgo[t].ap()[r * 1024:(r + 1) * 1024, col0:col0 + 64].rearrange("(t p) c -> p t c", p=128), r=[t_vgo], wp=[t_VT[vs]], key=t_VT[vs], slow=True)
                    P.dma('sp', Vv[:, 32:36, 0:64], vctx.ap()[l, :, col0:col0 + 64].rearrange("(t p) c -> p t c", p=128), r=[t_kctxs[l], t_wcs[l]], wp=[t_VT[vs]], key=t_VT[vs], slow=True)
                    return Vv
                Kv = load_K(0, 128, 128)
                for g in range(2):
                    load_V(g, 128 + g * 64)
                for jb in range(4):
                    qs_ = jb % 2
                    Qv = QT[qs_]
                    P.dma('sp', Qv, qd[1280 + jb * 128:1280 + (jb + 1) * 128, :], r=[t_q[1]], w=[t_QT[qs_]], key=t_QT[qs_])
                    for g in range(2):
                        h = jb + 4 * g
                        att_multi(Kv, t_KT[0], (g * 64, g * 64 + 64), VT[g], t_VT[g], Qv, t_QT[qs_], SC_A,
                                  [dict(q=(qc * 512, 512), ktiles=list(range(36)), out=od[1024 + h * 64:1024 + (h + 1) * 64, qc * 512:(qc + 1) * 512]) for qc in range(4)], t_o[1])
                for h in range(8):
                    ks = (h + 1) % 2
                    Kv = load_K(ks, 256 + h * 96, 96)
                    Vv = load_V(ks, 256 + h * 64)
                    Qv = QT[ks]
                    P.dma('sp', Qv[0:96, :], qd[512 + h * 96:512 + (h + 1) * 96, :], r=[t_q[1]], w=[t_QT[ks]], key=t_QT[ks])
                    att_multi(Kv, t_KT[ks], (0, 96), Vv, t_VT[ks], Qv, t_QT[ks], SC_B,
                              [dict(q=(qc * 512, 512), ktiles=list(range(36)), out=od[512 + h * 64:512 + (h + 1) * 64, qc * 512:(qc + 1) * 512]) for qc in range(4)], t_o[1])
                Kv = KT[0]
                P.dma('sp', Kv[:, 0:128], kgo[1].ap()[0:128, 896:1024], r=[t_kgo], w=[t_KT[0]], key=t_KT[0])
                for t in range(2):
                    P.dma('sp', Kv[:, 128 + t * 1024:128 + (t + 1) * 1024], kgi[t].ap()[0:128, :], r=[t_k[1]], wp=[t_KT[0]], key=t_KT[0])
                P.dma('sp', Kv[:, 2176:2304], kgo[0].ap()[1024:1152, 0:128], r=[t_kgo], wp=[t_KT[0]], key=t_KT[0])
                P.dma('sp', Kv[:, 2304:2816], kctx.ap()[l, 0:128, :], r=[t_kctxs[l], t_wcs[l]], wp=[t_KT[0]], key=t_KT[0])
                for g in range(2):
                    Vv = VT[g]
                    c0 = g * 64
                    P.dma('sp', Vv[:, 0:1, 0:64], vgo[1].ap()[896:1024, c0:c0 + 64].rearrange("(t p) c -> p t c", p=128), r=[t_vgo], wp=[t_VT[g]], key=t_VT[g], slow=True)
                    for r in range(2):
                        P.dma('sp', Vv[:, 1 + r * 8:1 + (r + 1) * 8, 0:64], vgi[r].ap()[:, c0:c0 + 64].rearrange("(t p) c -> p t c", p=128), r=[t_v[1]], wp=[t_VT[g]], key=t_VT[g], slow=True)
                    P.dma('sp', Vv[:, 17:18, 0:64], vgo[0].ap()[1024:1152, c0:c0 + 64].rearrange("(t p) c -> p t c", p=128), r=[t_vgo], wp=[t_VT[g]], key=t_VT[g], slow=True)
                    P.dma('sp', Vv[:, 18:22, 0:64], vctx.ap()[l, :, c0:c0 + 64].rearrange("(t p) c -> p t c", p=128), r=[t_kctxs[l], t_wcs[l]], wp=[t_VT[g]], key=t_VT[g], slow=True)
                for jb in range(4):
                    qs_ = jb % 2
                    Qv = QT[qs_]
                    P.dma('sp', Qv, qd[jb * 128:(jb + 1) * 128, :], r=[t_q[1]], w=[t_QT[qs_]], key=t_QT[qs_])
                    for g in range(2):
                        h = jb + 4 * g
                        segs = []
                        for qb in range(16):
                            var = 0 if qb == 0 else (2 if qb == 15 else 1)
                            masks = [maskB[:, var, 0:128], None if var == 1 else maskB[:, var, 128:256], maskB[:, var, 256:384], None, None, None, None]
                            segs.append(dict(q=(qb * 128, 128), ktiles=[qb, qb + 1, qb + 2, 18, 19, 20, 21], out=od[h * 64:(h + 1) * 64, qb * 128:(qb + 1) * 128],
                                             sink=sinkE[0:64, l, h:h + 1], masks=masks))
                        att_multi(Kv, t_KT[0], (g * 64, g * 64 + 64), VT[g], t_VT[g], Qv, t_QT[qs_], SC_A, segs, t_o[1])
                att_to_arena()

            def D2(l, grp, ci, xv, t_x, col0):
                j = grp
                hidx = ci if grp == 0 else 2 + ci
                o_d = (o_p if grp == 0 else o_s).ap()
                cs = slice(col0, col0 + CH)
                P.dma('sp', HTflat, h_scr.ap()[hidx], r=[t_hscr[hidx]], w=[t_HT], key=t_HT)
                OT = BIG[:, 0:6144].rearrange("p (k c) -> p k c", k=12)
                t_OT = T("OT")
                P.dma('sp', OT, o_d[:, cs].rearrange("(k p) c -> p k c", p=128), r=[t_o[grp]], w=[t_OT, t_BIG], key=t_OT)
                MT = BIG[:, 6144:10240].rearrange("p (k c) -> p k c", k=8)
                t_MT = T("MT")
                WO = BIG[:, 10240:16384].rearrange("p (k c) -> p k c", k=12)
                t_WO = T("WO")
                for n in range(8):
                    if n % 4 == 0:
                        ng = n // 4
                        P.dma('sp', WO, wo_b.ap()[l, ng].rearrange("p (k c) -> p k c", k=12), r=[t_wcs[l]], w=[t_WO], key=t_WO)
                    wg, twg = load_w(wg_b.ap()[l, n].rearrange("p (k c) -> p k c", k=8), 8, 384, [t_wcs[l]])
                    for b in range(3):
                        pg, tg = bank()

                        def mmg(e, pg=pg, b=b, wg=wg):
                            inst = None
                            for k in range(8):
                                inst = e.matmul(pg[:], lhsT=wg[:, k, b * 128:(b + 1) * 128], rhs=HT[:, k, :], start=(k == 0), stop=(k == 7))
                            return inst
                        P.op('pe', mmg, r=[twg, t_HT], w=[tg])
                        sg = TMP[:, b % 2, :]
                        P.op('act', lambda e, pg=pg, sg=sg: e.activation(out=sg, in_=pg[:], func=AF.Sigmoid), r=[tg], w=[t_TMP[b % 2]])
                        pp, tp = bank()

                        def mmp(e, pp=pp, b=b, n=n):
                            inst = None
                            for k in range(4):
                                inst = e.matmul(pp[:], lhsT=WO[:, b * 4 + k, (n % 4) * 128:(n % 4 + 1) * 128], rhs=OT[:, b * 4 + k, :], start=(k == 0), stop=(k == 3))
                            return inst
                        P.op('pe', mmp, r=[t_WO, t_OT], w=[tp])
                        if b == 0:
                            P.op('dve', lambda e, pp=pp, sg=sg: e.tensor_tensor(out=TMP[:, 2, :], in0=pp[:], in1=sg, op=ALU.mult), r=[tp, t_TMP[0]], w=[t_TMP[2]])
                        else:
                            P.op('dve', lambda e, pp=pp, sg=sg: e.tensor_tensor(out=TMP[:, 3, :], in0=pp[:], in1=sg, op=ALU.mult), r=[tp, t_TMP[b % 2]], w=[t_TMP[3]])
                            if b == 1:
                                P.op('pool', lambda e: e.tensor_tensor(out=TMP[:, 2, :], in0=TMP[:, 2, :], in1=TMP[:, 3, :], op=ALU.add), r=[t_TMP[3]], w=[t_TMP[2]])
                            else:
                                P.op('pool', lambda e, n=n: e.tensor_tensor(out=MT[:, n, :], in0=TMP[:, 2, :], in1=TMP[:, 3, :], op=ALU.add), r=[t_TMP[2], t_TMP[3]], wp=[t_MT])
                for ng in range(2):
                    wv, tw = load_w(wout_b.ap()[l, ng].rearrange("p (k c) -> p k c", k=8), 8, 512, [t_wcs[l]])
                    for nb in range(4):
                        n = ng * 4 + nb
                        pb, tb = proj_fm(wv, tw, nb * 128, 128, 8, lambda k: MT[:, k, :], [t_MT])
                        P.op('act' if n % 2 else 'dve',
                             (lambda e, pb=pb, n=n: e.activation(out=RES[:, n, :], in_=pb[:], func=AF.Identity)) if n % 2 else
                             (lambda e, pb=pb, n=n: e.tensor_copy(out=RES[:, n, :], in_=pb[:])), r=[tb], w=[t_RES] if n == 0 else (), wp=() if n == 0 else [t_RES])
                postnorm_resid(xv, t_x, l, j, 2)
                prenorm(xv, t_x, l, j, 3, 4)
                UT = BIG[:].rearrange("p (k c) -> p k c", k=32)
                t_UT = T("UT")
                for g8 in range(8):
                    wv, tw = load_w(w1_b.ap()[l, g8].rearrange("p (k c) -> p k c", k=8), 8, 512, [t_wcs[l]])
                    for nb in range(4):
                        n = g8 * 4 + nb
                        pb, tb = proj_fm(wv, tw, nb * 128, 128, 8, lambda k: HT[:, k, :], [t_HT])
                        ti = n % 2
                        P.op('act', lambda e, pb=pb, ti=ti: e.activation(out=TMP[:, ti, :], in_=pb[:], func=AF.Relu), r=[tb], w=[t_TMP[ti]])
                        first = (n == 0)
                        P.op('pool', lambda e, n=n, ti=ti: e.tensor_tensor(out=UT[:, n, :], in0=TMP[:, ti, :], in1=TMP[:, ti, :], op=ALU.mult), r=[t_TMP[ti]],
                             w=[t_UT, t_BIG, t_OT, t_MT, t_WO] if first else (), wp=() if first else [t_UT])
                for n in range(8):
                    wv, tw = load_w(w2_b.ap()[l, n].rearrange("p (k c) -> p k c", k=32), 32, 128, [t_wcs[l]])
                    pb, tb = proj_fm(wv, tw, 0, 128, 32, lambda k: UT[:, k, :], [t_UT])
                    P.op('act' if n % 2 else 'dve',
                         (lambda e, pb=pb, n=n: e.activation(out=RES[:, n, :], in_=pb[:], func=AF.Identity)) if n % 2 else
                         (lambda e, pb=pb, n=n: e.tensor_copy(out=RES[:, n, :], in_=pb[:])), r=[tb], w=[t_RES] if n == 0 else (), wp=() if n == 0 else [t_RES])
                postnorm_resid(xv, t_x, l, j, 5)
                _merge(t_BIG.r, t_UT.r); _merge(t_BIG.r, t_UT.w)

            ccsems = []
            for l in range(L):
                for ci in range(4):
                    D1(l, 1, ci, xTs[:, :, ci * CH:(ci + 1) * CH], t_xTs[ci], ci * CH)
                ckpt('D1s%d' % l)
                def flat512(d):
                    a = d.ap()
                    if a.shape[1] == 1024:
                        return a.rearrange("r (a c) -> (r a) c", c=512)
                    return a.rearrange("r c -> (r c)").rearrange("(n c) -> n c", c=512)
                for (src, dst, tsrc, tdst) in ((kgi[0], kgo[0], t_k[1], t_kgo), (kgi[1], kgo[1], t_k[1], t_kgo), (vgi[0], vgo[0], t_v[1], t_vgo), (vgi[1], vgo[1], t_v[1], t_vgo)):
                    csem = es.enter_context(nc.semaphore("cc%d" % len(ccsems)))
                    ccsems.append(csem)
                    deps = P._deps([tsrc], [], [tdst])
                    P._mark(deps)
                    kid = ('cc', len(ccsems))
                    P.streams['pool'].append((deps, (lambda e, src=src, dst=dst: e.collective_compute(
                        "AllGather", ALU.bypass, replica_groups=[[0, 1], [2, 3], [4, 5], [6, 7]],
                        ins=[flat512(src)], outs=[flat512(dst)])), ('x', csem)))
                    P._update({kid: 1}, [tsrc], [], [tdst])
                    P.ccmap = getattr(P, 'ccmap', {})
                    P.ccmap[kid] = csem
                ckpt('cc%d' % l)
                if l + 1 < L:
                    mod_layer(l + 1)
                    mv_layer(l + 1)
                    cast_layer(l + 1)
                    swap_layer(l + 1)
                    ctx_layer(l + 1)
                for ci in range(2):
                    P.dma('sp', XC[:].rearrange("p k c -> p (k c)"), xp_scr.ap()[ci], r=[t_xps[ci]], w=[t_XC], key=t_XC)
                    D1(l, 0, ci, XC, t_XC, ci * CH)
                ckpt('D1p%d' % l)
                ATT_prompt(l)
                ckpt('ATTp%d' % l)
                for ci in range(2):
                    P.dma('sp', XC[:].rearrange("p k c -> p (k c)"), xp_scr.ap()[ci], r=[t_xps[ci]], w=[t_XC], key=t_XC)
                    D2(l, 0, ci, XC, t_XC, ci * CH)
                    P.dma('pool', xp_scr.ap()[ci], XC[:].rearrange("p k c -> p (k c)"), r=[t_XC], w=[t_xps[ci]], key=t_XC)
                ckpt('D2p%d' % l)
                ATT_sample(l)
                ckpt('ATTs%d' % l)
                for ci in range(4):
                    D2(l, 1, ci, xTs[:, :, ci * CH:(ci + 1) * CH], t_xTs[ci], ci * CH)

                ckpt('D2s%d' % l)
            for grp, yout, ntile in ((0, y_p, 8), (1, y_s, 16)):
                for tt in range(ntile):
                    c = tt // 4
                    if grp == 0 and tt % 4 == 0:
                        P.dma('sp', XC[:].rearrange("p k c -> p (k c)"), xp_scr.ap()[c], r=[t_xps[c]], w=[t_XC], key=t_XC)
                    stg = TMP[:, (tt % 2) * 2:(tt % 2) * 2 + 2].rearrange("p a c -> p (a c)")
                    tstg = t_TMP[(tt % 2) * 2]
                    tstg2 = t_TMP[(tt % 2) * 2 + 1]
                    for half in range(2):
                        pb, tb = bank()

                        def tr(e, pb=pb, half=half, tt=tt, grp=grp):
                            inst = None
                            for q in range(4):
                                k = half * 4 + q
                                src = XC[:, k, (tt % 4) * 128:(tt % 4 + 1) * 128] if grp == 0 else xTs[:, k, tt * 128:(tt + 1) * 128]
                                inst = e.transpose(pb[:, q * 128:(q + 1) * 128], src, identF[:])
                            return inst
                        P.op('pe', tr, r=[t_XC if grp == 0 else t_xTs[c], t_const], w=[tb])
                        P.op('act' if half else 'dve',
                             (lambda e, pb=pb, stg=stg, half=half: e.activation(out=stg[:, half * 512:(half + 1) * 512], in_=pb[:], func=AF.Identity)) if half else
                             (lambda e, pb=pb, stg=stg, half=half: e.tensor_copy(out=stg[:, half * 512:(half + 1) * 512], in_=pb[:])),
                             r=[tb], w=[tstg, tstg2] if half == 0 else (), wp=() if half == 0 else [tstg])
                    P.dma('pool', yout[tt * 128:(tt + 1) * 128, :], stg, r=[tstg, tstg2], wp=[t_outs], key=tstg)


        except _Stop:
            pass
        block = es.enter_context(nc.Block())
        _finalize_with_cc(P, es, block)
    return nc


def _finalize_with_cc(P, es, block):
    nc = P.nc
    sems = {}
    for e in P.ENG:
        sems[e] = es.enter_context(nc.semaphore("s_" + e))
    for i, k in enumerate(P.keys):
        sems[k] = es.enter_context(nc.semaphore("d%d" % i))
    for kid, s in getattr(P, 'ccmap', {}).items():
        sems[kid] = s
    cmap = {}
    for e in P.ENG:
        m = {}
        c = 0
        nd = P.needed[e]
        for i in range(1, P.nops[e] + 1):
            if i in nd:
                c += 1
            m[i] = c
        cmap[e] = m
    final = dict(P.dmacnt)

    def emit(ename, eng):
        waited = {}
        for deps, fn, me in P.streams[ename]:
            for k, v in deps.items():
                if k == 'pe' and ename == 'pe':
                    continue
                cnt = cmap[k][v] if isinstance(k, str) else v
                if cnt > 0 and waited.get(k, 0) < cnt:
                    eng.wait_ge(sems[k], cnt)
                    waited[k] = cnt
            inst = fn(eng)
            if me[0] == 'c':
                if me[2] in P.needed[me[1]]:
                    inst.then_inc(sems[me[1]], 1)
            elif me[0] == 'd':
                inst.then_inc(sems[me[1]], 16)
            else:
                inst.then_inc(me[1])
        if ename == 'sp':
            for k, v in final.items():
                eng.wait_ge(sems[k], v)

    block.sync(lambda e: emit('sp', e))
    block.gpsimd(lambda e: emit('pool', e))
    block.vector(lambda e: emit('dve', e))
    block.scalar(lambda e: emit('act', e))
    block.tensor(lambda e: emit('pe', e))


def _consts(core):
    half = core % 2
    ident = np.eye(128, dtype=np.float32)
    ones = np.ones((128, 128), np.float32)
    bones = np.zeros((128, 128), np.float32)
    bones[0:64, 0:64] = 1.0
    bones[64:128, 64:128] = 1.0
    perm = np.zeros((128, 128), np.float32)
    for m in range(128):
        hh, d = divmod(m, 64)
        a, r = divmod(d, 32)
        b, i = divmod(r, 16)
        k = hh * 64 + a * 32 + (1 - b) * 16 + i
        perm[k, m] = 1.0
    t = np.arange(2048) + half * 2048
    row = (t // 64).astype(np.float64)
    col = (t % 64).astype(np.float64)

    def tables(hd):
        q = hd // 4
        inv = (10000.0 ** (-np.arange(q, dtype=np.float32) / np.float32(q))).astype(np.float32)
        C = np.zeros((hd, 2048), np.float32)
        S = np.zeros((hd, 2048), np.float32)
        for a, pos in enumerate((row, col)):
            ang = (pos.astype(np.float32)[None, :] * inv[:, None]).astype(np.float32)
            c = np.cos(ang).astype(np.float32)
            s = np.sin(ang).astype(np.float32)
            base = a * 2 * q
            C[base:base + q] = c
            C[base + q:base + 2 * q] = c
            S[base:base + q] = -s
            S[base + q:base + 2 * q] = s
        return C, S
    C64, S64 = tables(64)
    ropeA = np.stack([np.concatenate([C64, C64], 0), np.concatenate([S64, S64], 0)], 0)
    C32, S32 = tables(32)
    CB = np.concatenate([np.ones((64, 2048), np.float32), C32], 0)
    SB = np.concatenate([np.zeros((64, 2048), np.float32), S32], 0)
    ropeB = np.stack([CB, SB], 0)
    kk = np.arange(128)[:, None]
    qq = np.arange(128)[None, :]
    m0 = (kk >= qq).astype(np.float32)
    m1 = np.ones((128, 128), np.float32)
    m2 = (kk <= qq).astype(np.float32)
    mid = np.concatenate([m0, m1, m2], 1)
    first = np.concatenate([m0 * (1.0 if half == 1 else 0.0), m1, m2], 1)
    last = np.concatenate([m0, m1, m2 * (1.0 if half == 0 else 0.0)], 1)
    mask = np.stack([first, mid, last], 0)
    return dict(c_ident=ident, c_ones=ones, c_bones=bones, c_perm=perm, c_ropeA=np.ascontiguousarray(ropeA),
                c_ropeB=np.ascontiguousarray(ropeB), c_mask=np.ascontiguousarray(mask))


_NC_CACHE = {}


def kernel(**inp):
    f = lambda a: np.ascontiguousarray(np.asarray(a, dtype=np.float32))
    if 'nc' not in _NC_CACHE:
        _NC_CACHE['nc'] = build_program()
    nc = _NC_CACHE['nc']
    wnames = ['w_mod', 'w_in', 'w_qup', 'w_kvup', 'w_oa', 'w_ob', 'w_oc', 'w_out', 'w_mlp1', 'w_mlp2']
    shared = {k: f(inp[k]) for k in wnames}
    shared['h_bmodT'] = f(inp['b_mod']).reshape(L, 48, 128).transpose(2, 0, 1)
    shared['h_gvecs'] = np.stack([f(inp[k]).reshape(L, 8, 128).transpose(2, 0, 1) for k in ('g_pre_mix', 'g_post_mix', 'g_pre_mlp', 'g_post_mlp')], 1).reshape(128, 4 * L * 8)
    shared['h_gqaT'] = f(inp['g_qa']).reshape(L, 3, 128).transpose(2, 0, 1)
    shared['h_gkvaT'] = f(inp['g_kva']).reshape(L, 2, 128).transpose(2, 0, 1)
    shared['h_gqc2'] = np.concatenate([f(inp['g_qc']).T, f(inp['g_qc']).T], 0)
    shared['h_gkc2'] = np.concatenate([f(inp['g_kc']).T, f(inp['g_kc']).T], 0)
    shared['h_sinkb'] = np.broadcast_to(f(inp['a_sink'])[None], (128, L, 8))
    x_prompt = f(inp['x_prompt']); x_sample = f(inp['x_sample'])
    in_maps = []
    for c in range(8):
        b, half = c // 2, c % 2
        m = dict(shared)
        m['xp'] = x_prompt[4 * c:4 * c + 4].reshape(1024, D)
        m['xs'] = x_sample[b, half * 2048:(half + 1) * 2048]
        m['ca_k'] = f(inp['cache_a_k'])[b].reshape(L, 512, 128)
        m['ca_v'] = f(inp['cache_a_v'])[b].reshape(L, 512, 128)
        m['cb_ckv'] = f(inp['cache_b_ckv'])[b]
        m['cb_kr'] = f(inp['cache_b_krope'])[b]
        m['cc_k'] = f(inp['cache_c_k'])[b].reshape(L, 512, 128)
        m['cc_v'] = f(inp['cache_c_v'])[b].reshape(L, 512, 128)
        m['cvecT'] = np.stack([f(inp['c_ctx']), f(inp['c'])[b]], 0).reshape(2, 8, 128).transpose(2, 1, 0)
        m.update(_consts(c))
        m = {k: np.ascontiguousarray(v) for k, v in m.items()}
        in_maps.append(m)
    res = run_bass_kernel_spmd(nc, in_maps, core_ids=list(range(8)))
    r = res.results
    y_prompt = np.concatenate([r[c]['y_p'].reshape(4, 256, D) for c in range(8)], 0)
    y_sample = np.stack([np.concatenate([r[2 * b]['y_s'], r[2 * b + 1]['y_s']], 0) for b in range(4)], 0)

    def cat(name, shape):
        return np.concatenate([r[c][name].reshape((4, L, 256) + shape) for c in range(8)], 0)
    return (y_prompt.astype(np.float32), y_sample.astype(np.float32),
            cat('o_ak', (2, 64)), cat('o_av', (2, 64)), cat('o_ckv', (256,)), cat('o_kr', (32,)),
            cat('o_ck', (2, 64)), cat('o_cv', (2, 64)))
```

```python
import os
import numpy as np
from contextlib import ExitStack
import concourse.bass as bass
import concourse.mybir as mybir
from concourse.bass_utils import run_bass_kernel_spmd

F32 = mybir.dt.float32
BF16 = mybir.dt.bfloat16
AF = mybir.ActivationFunctionType
ALU = mybir.AluOpType

L = 4
D = 1024
NCOL = 5280
EPS = 1e-6
CH = 512
SC_A = 64 ** -0.5
SC_B = 96 ** -0.5
O_QA, O_KA, O_VA, O_QBD, O_KVBD, O_KBR, O_QC, O_KC, O_VC, O_G = 0, 512, 640, 768, 1152, 1408, 1440, 1952, 2080, 2208


class _Stop(Exception):
    pass


def ckpt(name):
    if os.environ.get('KSTOP') == name:
        raise _Stop()


class T:
    def __init__(self, name):
        self.name = name
        self.w = {}
        self.r = {}


def _merge(d, s):
    for k, v in s.items():
        if d.get(k, 0) < v:
            d[k] = v


class Prog:
    ENG = ('pe', 'act', 'dve', 'pool', 'sp')

    def __init__(self, nc):
        self.nc = nc
        self.streams = {e: [] for e in self.ENG}
        self.nops = {e: 0 for e in self.ENG}
        self.needed = {e: set() for e in self.ENG}
        self.dmacnt = {}
        self.keys = []

    def _deps(self, r, w, wp):
        deps = {}
        for t in r:
            _merge(deps, t.w)
        for t in w:
            _merge(deps, t.w)
            _merge(deps, t.r)
        for t in wp:
            _merge(deps, t.w)
            _merge(deps, t.r)
        return deps

    def _mark(self, deps):
        for k, v in deps.items():
            if isinstance(k, str):
                self.needed[k].add(v)

    def _update(self, ev, r, w, wp):
        for t in r:
            _merge(t.r, ev)
        for t in w:
            t.w = dict(ev)
            t.r = {}
        for t in wp:
            _merge(t.w, ev)

    def op(self, eng, fn, r=(), w=(), wp=()):
        deps = self._deps(r, w, wp)
        self._mark(deps)
        self.nops[eng] += 1
        idx = self.nops[eng]
        self.streams[eng].append((deps, fn, ('c', eng, idx)))
        self._update({eng: idx}, r, w, wp)

    def dma(self, q, out, in_, r=(), w=(), wp=(), key=None, slow=False):
        assert key is not None
        deps = self._deps(r, w, wp)
        self._mark(deps)
        kid = ('dma', id(key), q)
        if kid not in self.dmacnt:
            self.dmacnt[kid] = 0
            self.keys.append(kid)
        self.dmacnt[kid] += 16
        cnt = self.dmacnt[kid]

        def fn(e, out=out, in_=in_, slow=slow):
            if slow:
                return e.dma_start(out=out, in_=in_, allow_slow_non_contiguous=True)
            return e.dma_start(out=out, in_=in_)
        self.streams[q].append((deps, fn, ('d', kid, cnt)))
        self._update({kid: cnt}, r, w, wp)

    def finalize(self, es, block):
        nc = self.nc
        sems = {}
        for e in self.ENG:
            sems[e] = es.enter_context(nc.semaphore("s_" + e))
        for i, k in enumerate(self.keys):
            sems[k] = es.enter_context(nc.semaphore("d%d" % i))
        cmap = {}
        for e in self.ENG:
            m = {}
            c = 0
            nd = self.needed[e]
            for i in range(1, self.nops[e] + 1):
                if i in nd:
                    c += 1
                m[i] = c
            cmap[e] = m
        final = dict(self.dmacnt)

        def emit(ename, eng):
            waited = {}
            for deps, fn, me in self.streams[ename]:
                for k, v in deps.items():
                    if isinstance(k, str):
                        if k == ename and ename == 'pe':
                            pass
                        cnt = cmap[k][v]
                    else:
                        cnt = v
                    if cnt > 0 and waited.get(k, 0) < cnt:
                        eng.wait_ge(sems[k], cnt)
                        waited[k] = cnt
                inst = fn(eng)
                if me[0] == 'c':
                    if me[2] in self.needed[me[1]]:
                        inst.then_inc(sems[me[1]], 1)
                else:
                    inst.then_inc(sems[me[1]], 16)
            if ename == 'sp':
                for k, v in final.items():
                    eng.wait_ge(sems[k], v)

        block.sync(lambda e: emit('sp', e))
        block.gpsimd(lambda e: emit('pool', e))
        block.vector(lambda e: emit('dve', e))
        block.scalar(lambda e: emit('act', e))
        block.tensor(lambda e: emit('pe', e))


def build_program():
    nc = bass.Bass("TRN2", target_bir_lowering=False)
    P = Prog(nc)

    def din(name, shape, dt=F32):
        return nc.dram_tensor(name, list(shape), dt, kind="ExternalInput").ap()

    def dout(name, shape):
        return nc.dram_tensor(name, list(shape), F32, kind="ExternalOutput").ap()

    def dscr(name, shape, dt=BF16):
        return nc.dram_tensor(name, list(shape), dt)

    xp_in = din("xp", [1024, D])
    xs_in = din("xs", [2048, D])
    ca_k = din("ca_k", [L, 512, 128]); ca_v = din("ca_v", [L, 512, 128])
    cb_ckv = din("cb_ckv", [L, 512, 256]); cb_kr = din("cb_kr", [L, 512, 32])
    cc_k = din("cc_k", [L, 512, 128]); cc_v = din("cc_v", [L, 512, 128])
    cvecT = din("cvecT", [128, 8, 2])
    h_bmodT = din("h_bmodT", [128, L, 48]); h_gvecs = din("h_gvecs", [128, 4 * L * 8])
    h_gqaT = din("h_gqaT", [128, L, 3]); h_gkvaT = din("h_gkvaT", [128, L, 2])
    h_gqc2 = din("h_gqc2", [128, L]); h_gkc2 = din("h_gkc2", [128, L]); h_sinkb = din("h_sinkb", [128, L, 8])
    w_mod = din("w_mod", [L, D, 6 * D])
    w_in = din("w_in", [L, D, NCOL])
    w_qup = din("w_qup", [L, 384, 768])
    w_kvup = din("w_kvup", [L, 256, 1024])
    w_oa = din("w_oa", [L, 512, D]); w_ob = din("w_ob", [L, 512, D]); w_oc = din("w_oc", [L, 512, D])
    w_out = din("w_out", [L, D, D]); w_mlp1 = din("w_mlp1", [L, D, 4 * D]); w_mlp2 = din("w_mlp2", [L, 4 * D, D])
    c_ident = din("c_ident", [128, 128]); c_ones = din("c_ones", [128, 128]); c_bones = din("c_bones", [128, 128])
    c_perm = din("c_perm", [128, 128])
    c_ropeA = din("c_ropeA", [2, 128, 2048]); c_ropeB = din("c_ropeB", [2, 96, 2048])
    c_mask = din("c_mask", [3, 128, 384])

    y_p = dout("y_p", [1024, D]); y_s = dout("y_s", [2048, D])
    o_ak = dout("o_ak", [4, L, 256, 128]); o_av = dout("o_av", [4, L, 256, 128])
    o_ckv = dout("o_ckv", [4, L, 256, 256]); o_kr = dout("o_kr", [4, L, 256, 32])
    o_ck = dout("o_ck", [4, L, 256, 128]); o_cv = dout("o_cv", [4, L, 256, 128])

    win_b = dscr("win_b", [L, D, NCOL])
    wsw_b = dscr("wsw_b", [L, D, 1408])
    wqup_b = dscr("wqup_b", [L, 384, 768]); wqupsw_b = dscr("wqupsw_b", [L, 384, 768])
    wkvup_b = dscr("wkvup_b", [L, 256, 1024])
    wg_b = dscr("wg_b", [L, 8, 128, 8 * 384])
    wo_b = dscr("wo_b", [L, 2, 128, 12 * 512])
    wout_b = dscr("wout_b", [L, 2, 128, 8 * 512])
    w1_b = dscr("w1_b", [L, 8, 128, 8 * 512])
    w2_b = dscr("w2_b", [L, 8, 128, 32 * 128])
    xp_scr = dscr("xp_scr", [2, 128, 8 * CH], F32)
    h_scr = dscr("h_scr", [6, 128, 8 * CH])
    q_p = dscr("q_p", [1792, 1024]); k_p = dscr("k_p", [1024, 1024]); v_p = dscr("v_p", [1024, 768]); o_p = dscr("o_p", [1536, 1024])
    q_s = dscr("q_s", [1792, 2048]); o_s = dscr("o_s", [1536, 2048])
    kgi = [dscr("kg_in%d" % t, [1024, 1024]) for t in range(2)]; kgo = [dscr("kg_out%d" % t, [2048, 1024]) for t in range(2)]
    vgi = [dscr("vg_in%d" % t, [1024, 768]) for t in range(2)]; vgo = [dscr("vg_out%d" % t, [2048, 768]) for t in range(2)]
    kctx = dscr("kctx", [L, 1024, 512]); vctx = dscr("vctx", [L, 512, 768])

    es = ExitStack()
    with es:
        def sb(name, shape, dt=F32):
            return es.enter_context(nc.sbuf_tensor(name, list(shape), dt))

        xTs = sb("xTs", [128, 8, 2048]); t_xTs = [T("xTs%d" % i) for i in range(4)]
        XC = sb("XC", [128, 8, CH]); t_XC = T("XC")
        identF = sb("identF", [128, 128]); onesB = sb("onesB", [128, 128], BF16); bonesB = sb("bonesB", [128, 128], BF16)
        permF = sb("permF", [128, 128])
        t_const = T("const")
        modraw = sb("modraw", [128, L, 48, 2]); t_modraw = T("modraw")
        bmodT = sb("bmodT", [128, L, 48]); gvecs = sb("gvecs", [128, 4, L, 8])
        MV = sb("MV", [128, 6, L, 2, 8]); t_MV = T("MV")
        gqaT = sb("gqaT", [128, L, 3]); gkvaT = sb("gkvaT", [128, L, 2])
        gq4 = sb("gq4", [128, 4, L]); t_gq4 = T("gq4")
        sinkE = sb("sinkE", [128, L, 8])
        maskB = sb("maskB", [128, 3, 384], BF16)
        scT = sb("scT", [128, 8, 2])
        ARENA = sb("ARENA", [128, 24576], BF16)
        BIG = sb("BIG", [128, 16384], BF16)
        TMP = sb("TMP", [128, 6, CH]); t_TMP = [T("TMP%d" % i) for i in range(6)]
        TMPB = sb("TMPB", [128, 4, CH], BF16); t_TMPB = [T("TMPB%d" % i) for i in range(4)]
        ROPE = sb("ROPE", [128, 4, CH]); t_ROPE = T("ROPE")
        psb = [es.enter_context(nc.psum_tensor("ps%d" % i, [128, CH], F32)) for i in range(8)]
        t_ps = [T("ps%d" % i) for i in range(8)]
        bank_ctr = [0]

        def bank():
            i = bank_ctr[0] % 6
            bank_ctr[0] += 1
            return psb[i], t_ps[i]
        obank_ctr = [0]

        def obank():
            i = 6 + obank_ctr[0] % 2
            obank_ctr[0] += 1
            return psb[i], t_ps[i]

        WS = [ARENA[:, i * 4096:(i + 1) * 4096] for i in range(3)]
        t_WS = [T("WS%d" % i) for i in range(3)]
        HTflat = ARENA[:, 12288:16384]
        HT = HTflat.rearrange("p (k c) -> p k c", k=8); t_HT = T("HT")
        RES = ARENA[:, 16384:24576].bitcast(F32).rearrange("p (k c) -> p k c", k=8); t_RES = T("RES")
        ws_ctr = [0]

        def wslot():
            i = ws_ctr[0] % 3
            ws_ctr[0] += 1
            return WS[i], t_WS[i]
        KT = [ARENA[:, i * 4608:(i + 1) * 4608] for i in range(2)]
        VT = [ARENA[:, 9216 + i * 4608: 9216 + (i + 1) * 4608].rearrange("p (t c) -> p t c", c=128) for i in range(2)]
        QT = [ARENA[:, 18432 + i * 2048: 18432 + (i + 1) * 2048] for i in range(2)]
        OH = [ARENA[:, 22528 + i * 1024: 22528 + (i + 1) * 1024] for i in range(2)]
        arena_all = t_WS + [t_HT, t_RES]
        t_BIG = T("BIG")

        dma_in = T("dma_in")

        try:
            kc = T("kconst")
            P.dma('sp', identF[:], c_ident, w=[t_const], key=kc)
            P.dma('sp', permF[:], c_perm, wp=[t_const], key=kc)
            P.dma('pool', onesB[:], c_ones, wp=[t_const], key=kc)
            P.dma('pool', bonesB[:], c_bones, wp=[t_const], key=kc)
            P.dma('pool', maskB[:], c_mask.rearrange("v p c -> p v c"), wp=[t_const], key=kc)
            t_wcs = [T("wcast%d" % i) for i in range(L)]
            kw = T("kwcast")

            cast_l = [0]

            def cast(dst, src):
                P.dma('pool', dst, src, wp=[t_wcs[cast_l[0]]], key=kw)

            def cast_layer(l):
                cast_l[0] = l
                for rb in range(8):
                    rs = slice(rb * 128, (rb + 1) * 128)
                    for (o, ) in ((O_QA,), (O_QC,)):
                        for a in range(2):
                            cast(win_b.ap()[l, rs, o:o + 512].rearrange("r (j a d) -> r j a d", j=4, a=2)[:, :, a, :],
                                 w_in[l, rs, o:o + 512].rearrange("r (a j d) -> r a j d", a=2, j=4)[:, a])
                    cast(win_b.ap()[l, rs, O_KA:O_QC], w_in[l, rs, O_KA:O_QC])
                    cast(win_b.ap()[l, rs, O_KC:O_G], w_in[l, rs, O_KC:O_G])
                    for b in range(3):
                        cast(wg_b.ap()[l, :, :, rb * 384 + b * 128:rb * 384 + (b + 1) * 128].rearrange("n p c -> p n c"),
                             w_in[l, rs, O_G + b * 1024:O_G + (b + 1) * 1024].rearrange("r (n c) -> r n c", n=8))
                    cast(wout_b.ap()[l, :, :, rb * 512:(rb + 1) * 512].rearrange("g p c -> p g c"), w_out[l, rs, :].rearrange("p (g c) -> p g c", g=2))
                    cast(w1_b.ap()[l, :, :, rb * 512:(rb + 1) * 512].rearrange("g p c -> p g c"), w_mlp1[l, rs, :].rearrange("p (g c) -> p g c", g=8))
                for rb in range(3):
                    rs = slice(rb * 128, (rb + 1) * 128)
                    cast(wqup_b.ap()[l, rs, :], w_qup[l, rs, :])
                for rb in range(2):
                    rs = slice(rb * 128, (rb + 1) * 128)
                    for t in range(2):
                        cast(wkvup_b.ap()[l, rs, t * 512:(t + 1) * 512].rearrange("r (h d) -> r h d", h=8),
                             w_kvup[l, rs, :].rearrange("r (h t d) -> r h t d", h=8, t=2)[:, :, t, :])
                for bi, wsrc in enumerate((w_oa, w_ob, w_oc)):
                    for rb in range(4):
                        k12 = bi * 4 + rb
                        cast(wo_b.ap()[l, :, :, k12 * 512:(k12 + 1) * 512].rearrange("g p c -> p g c"), wsrc[l, rb * 128:(rb + 1) * 128, :].rearrange("p (g c) -> p g c", g=2))
                for k in range(32):
                    cast(w2_b.ap()[l, :, :, k * 128:(k + 1) * 128].rearrange("n p c -> p n c"),
                         w_mlp2[l, k * 128:(k + 1) * 128, :].rearrange("p (n c) -> p n c", n=8))
                cast(vctx.ap()[l, :, 0:128], ca_v[l])
                cast(vctx.ap()[l, :, 128:256], cc_v[l])

            cast_layer(0)
            ckpt('cast')
            t_wsws = [T("wsw%d" % i) for i in range(L)]
            ksw = T("ksw")
            def swap_layer(l):
                ws, tw = wslot()
                ws2, tw2 = wslot()
                for (so, wd, dst) in ((O_QA, 512, 0), (O_KA, 128, 512), (O_QC, 512, 640), (O_KC, 128, 1152), (O_KBR - 64, 96, 1280)):
                    src_v = ws[:, 0:8 * wd].rearrange("p (k c) -> p k c", k=8)
                    dst_v = ws2[:, 0:8 * wd].rearrange("p (k c) -> p k c", k=8)
                    P.dma('sp', src_v, win_b.ap()[l, :, so:so + wd].rearrange("(k p) c -> p k c", p=128), r=[t_wcs[l]], w=[tw], key=tw)
                    if wd == 96:
                        P.op('dve', lambda e, d=dst_v, s=src_v: e.tensor_copy(out=d[:, :, 0:64], in_=s[:, :, 0:64]), r=[tw], w=[tw2])
                        for b in range(2):
                            sv = src_v[:, :, 64:96].rearrange("p k (a b i) -> p k a b i", a=2, b=2)[:, :, :, 1 - b, :]
                            dv = dst_v[:, :, 64:96].rearrange("p k (a b i) -> p k a b i", a=2, b=2)[:, :, :, b, :]
                            P.op('dve', lambda e, d=dv, s=sv: e.tensor_copy(out=d, in_=s), r=[tw], wp=[tw2])
                    else:
                        for b in range(2):
                            sv = src_v.rearrange("p k (h a b i) -> p k h a b i", a=2, b=2, i=16)[:, :, :, :, 1 - b, :]
                            dv = dst_v.rearrange("p k (h a b i) -> p k h a b i", a=2, b=2, i=16)[:, :, :, :, b, :]
                            for k in range(8):
                                P.op('dve', lambda e, d=dv[:, k], s=sv[:, k]: e.tensor_copy(out=d, in_=s), r=[tw], wp=[tw2] if (b or k) else (), w=() if (b or k) else [tw2])
                    P.dma('pool', wsw_b.ap()[l, :, dst:dst + wd].rearrange("(k p) c -> p k c", p=128), dst_v, r=[tw2], wp=[t_wsws[l]], key=ksw)
                src_v = ws[:, 0:3 * 768].rearrange("p (k c) -> p k c", k=3)
                dst_v = ws2[:, 0:3 * 768].rearrange("p (k c) -> p k c", k=3)
                P.dma('sp', src_v, wqup_b.ap()[l].rearrange("(k p) c -> p k c", p=128), r=[t_wcs[l]], w=[tw], key=tw)
                P.op('dve', lambda e, d=dst_v, s=src_v: e.tensor_copy(out=d, in_=s), r=[tw], w=[tw2])
                for b in range(2):
                    for k in range(3):
                        sv = src_v[:, k].rearrange("p (h c) -> p h c", h=8)[:, :, 64:96].rearrange("p h (a b i) -> p h a b i", a=2, b=2)[:, :, :, 1 - b, :]
                        dv = dst_v[:, k].rearrange("p (h c) -> p h c", h=8)[:, :, 64:96].rearrange("p h (a b i) -> p h a b i", a=2, b=2)[:, :, :, b, :]
                        P.op('dve', lambda e, d=dv, s=sv: e.tensor_copy(out=d, in_=s), r=[tw], wp=[tw2])
                P.dma('pool', wqupsw_b.ap()[l].rearrange("(k p) c -> p k c", p=128), dst_v, r=[tw2], wp=[t_wsws[l]], key=ksw)

            swap_layer(0)
            ckpt('swap')
            kv = T("kvec")
            t_vec = T("vec")
            P.dma('sp', scT[:], cvecT, w=[t_vec], key=kv)
            P.dma('sp', bmodT[:], h_bmodT, wp=[t_vec], key=kv)
            P.dma('sp', gvecs[:].rearrange("p a l k -> p (a l k)"), h_gvecs, wp=[t_vec], key=kv)
            P.dma('sp', gqaT[:], h_gqaT, wp=[t_vec], key=kv)
            P.dma('sp', gkvaT[:], h_gkvaT, wp=[t_vec], key=kv)
            P.dma('sp', gq4[:, 0, :], h_gqc2, wp=[t_vec], key=kv)
            P.dma('sp', gq4[:, 2, :], h_gkc2, wp=[t_vec], key=kv)
            P.dma('sp', sinkE[:], h_sinkb, wp=[t_vec], key=kv)
            P.op('act', lambda e: e.activation(out=scT[:], in_=scT[:], func=AF.Silu), r=[t_vec], wp=[t_vec])
            P.op('act', lambda e: e.activation(out=sinkE[:], in_=sinkE[:], func=AF.Exp), r=[t_vec], wp=[t_vec])
            pb, tb = bank()
            P.op('pe', lambda e, pb=pb: e.matmul(pb[:, 0:L], lhsT=permF[:], rhs=gq4[:, 0, :], start=True, stop=True), r=[t_vec, t_const], w=[tb])
            P.op('dve', lambda e, pb=pb: e.tensor_copy(out=gq4[:, 1, :], in_=pb[:, 0:L]), r=[tb], wp=[t_gq4])
            pb, tb = bank()
            P.op('pe', lambda e, pb=pb: e.matmul(pb[:, 0:L], lhsT=permF[:], rhs=gq4[:, 2, :], start=True, stop=True), r=[t_vec, t_const], w=[tb])
            P.op('dve', lambda e, pb=pb: e.tensor_copy(out=gq4[:, 3, :], in_=pb[:, 0:L]), r=[tb], wp=[t_gq4])

            ckpt('vec')
            def mod_layer(l):
                pb, tb = bank()
                for g in range(12):
                    ws, tw = wslot()
                    ws2, tw2 = wslot()
                    wv = [ws.bitcast(F32).rearrange("p (k c) -> p k c", k=4), ws2.bitcast(F32).rearrange("p (k c) -> p k c", k=4)]
                    P.dma('sp', wv[0], w_mod[l, 0:512, g * 512:(g + 1) * 512].rearrange("(k p) c -> p k c", p=128), w=[tw], key=tw)
                    P.dma('sp', wv[1], w_mod[l, 512:1024, g * 512:(g + 1) * 512].rearrange("(k p) c -> p k c", p=128), w=[tw2], key=tw2)

                    def mm(e, pb=pb, wv=wv, g=g):
                        inst = None
                        for nb in range(4):
                            n = g * 4 + nb
                            for k in range(8):
                                inst = e.matmul(pb[:, n * 2:n * 2 + 2], lhsT=wv[k // 4][:, k % 4, nb * 128:(nb + 1) * 128], rhs=scT[:, k, :],
                                                start=(k == 0), stop=(k == 7))
                        return inst
                    P.op('pe', mm, r=[tw, tw2, t_vec], wp=[tb] if g else (), w=() if g else [tb])
                for j in range(2):
                    P.op('dve', lambda e, pb=pb, l=l, j=j: e.tensor_tensor(out=modraw[:, l, :, j], in0=pb[:, 0:96].rearrange("p (n j) -> p n j", j=2)[:, :, j],
                                                                          in1=bmodT[:, l, :], op=ALU.add), r=[tb, t_vec], wp=[t_modraw])
            mod_layer(0)
            def mv_layer(l):
                for j in range(2):
                    mr = lambda i, l=l, j=j: modraw[:, l, i * 8:(i + 1) * 8, j]
                    P.op('dve', lambda e, l=l, j=j, mr=mr: e.scalar_tensor_tensor(out=MV[:, 0, l, j, :], in0=mr(1), scalar=1.0, in1=gvecs[:, 0, l, :], op0=ALU.add, op1=ALU.mult), r=[t_modraw, t_vec], wp=[t_MV])
                    P.op('dve', lambda e, l=l, j=j, mr=mr: e.tensor_copy(out=MV[:, 1, l, j, :], in_=mr(0)), r=[t_modraw], wp=[t_MV])
                    P.op('dve', lambda e, l=l, j=j, mr=mr: e.tensor_tensor(out=MV[:, 2, l, j, :], in0=mr(2), in1=gvecs[:, 1, l, :], op=ALU.mult), r=[t_modraw, t_vec], wp=[t_MV])
                    P.op('dve', lambda e, l=l, j=j, mr=mr: e.scalar_tensor_tensor(out=MV[:, 3, l, j, :], in0=mr(4), scalar=1.0, in1=gvecs[:, 2, l, :], op0=ALU.add, op1=ALU.mult), r=[t_modraw, t_vec], wp=[t_MV])
                    P.op('dve', lambda e, l=l, j=j, mr=mr: e.tensor_copy(out=MV[:, 4, l, j, :], in_=mr(3)), r=[t_modraw], wp=[t_MV])
                    P.op('dve', lambda e, l=l, j=j, mr=mr: e.tensor_tensor(out=MV[:, 5, l, j, :], in0=mr(5), in1=gvecs[:, 3, l, :], op=ALU.mult), r=[t_modraw, t_vec], wp=[t_MV])
            mv_layer(0)
            t_par = [t_MV, t_vec, t_gq4, t_const]

            ckpt('mod')
            t_xps = [T("xps%d" % i) for i in range(2)]
            kxs = T("kxs")
            for grp, xin, ntile in ((1, xs_in, 16), (0, xp_in, 8)):
                for tt in range(ntile):
                    c = tt // 4
                    xt = TMP[:, 0:4].rearrange("p a c -> p (a c)") if tt % 2 == 0 else TMP[:, 4:6].rearrange("p a c -> p (a c)")
                    xt = TMP[:, (tt % 2) * 2:(tt % 2) * 2 + 2].rearrange("p a c -> p (a c)")
                    tx = t_TMP[(tt % 2) * 2]
                    tx2 = t_TMP[(tt % 2) * 2 + 1]
                    P.dma('sp', xt, xin[tt * 128:(tt + 1) * 128, :], w=[tx, tx2], key=tx)
                    for half in range(2):
                        pb, tb = bank()

                        def tr(e, pb=pb, xt=xt, half=half):
                            inst = None
                            for q in range(4):
                                k = half * 4 + q
                                inst = e.transpose(pb[:, q * 128:(q + 1) * 128], xt[:, k * 128:(k + 1) * 128], identF[:])
                            return inst
                        P.op('pe', tr, r=[tx, tx2, t_const], w=[tb])
                        if grp == 1:
                            dstv = xTs[:, half * 4:(half + 1) * 4, tt * 128:(tt + 1) * 128]
                            P.op('act' if half else 'dve',
                                 (lambda e, d=dstv, pb=pb: e.activation(out=d, in_=pb[:].rearrange("p (q c) -> p q c", q=4), func=AF.Identity)) if half else
                                 (lambda e, d=dstv, pb=pb: e.tensor_copy(out=d, in_=pb[:].rearrange("p (q c) -> p q c", q=4))),
                                 r=[tb], wp=[t_xTs[c]])
                        else:
                            dstv = XC[:, half * 4:(half + 1) * 4, (tt % 4) * 128:(tt % 4 + 1) * 128]
                            P.op('act' if half else 'dve',
                                 (lambda e, d=dstv, pb=pb: e.activation(out=d, in_=pb[:].rearrange("p (q c) -> p q c", q=4), func=AF.Identity)) if half else
                                 (lambda e, d=dstv, pb=pb: e.tensor_copy(out=d, in_=pb[:].rearrange("p (q c) -> p q c", q=4))),
                                 r=[tb], wp=[t_XC])
                    if grp == 0 and tt % 4 == 3:
                        P.dma('pool', xp_scr.ap()[c], XC[:].rearrange("p k c -> p (k c)"), r=[t_XC], w=[t_xps[c]], key=t_XC)

            ckpt('xT')
            t_kctxs = [T("kctx%d" % i) for i in range(L)]
            kck = T("kck")
            def ctx_layer(l):
                ws, tw = wslot()
                wkv = ws[:, 0:2048].rearrange("p (k c) -> p k c", k=2)
                P.dma('sp', wkv, wkvup_b.ap()[l].rearrange("(k p) c -> p k c", p=128), r=[t_wcs[l]], w=[tw], key=tw)
                ckf = TMP[:, 0:2].rearrange("p a c -> p (a c)")
                P.dma('sp', TMP[:, 0].rearrange("p (t c) -> p t c", t=4), ca_k[l].rearrange("(t p) c -> p t c", p=128), w=[t_TMP[0]], key=t_TMP[0])
                P.dma('sp', TMP[:, 1].rearrange("p (t c) -> p t c", t=4), cc_k[l].rearrange("(t p) c -> p t c", p=128), w=[t_TMP[1]], key=t_TMP[1])
                P.dma('sp', TMP[:, 2:4].rearrange("p a c -> p (a c)").rearrange("p (t c) -> p t c", t=4), cb_ckv[l].rearrange("(t p) c -> p t c", p=128), w=[t_TMP[2]], key=t_TMP[2])
                krp = TMP[:, 4, 0:384].rearrange("p (t c) -> p t c", t=4)
                P.op('pool', lambda e, krp=krp: e.memset(krp, 0.0), w=[t_TMP[4]])
                P.dma('sp', krp[:, :, 64:96], cb_kr[l].rearrange("(t p) c -> p t c", p=128), r=[], wp=[t_TMP[4]], key=t_TMP[4], slow=True)
                for si, (srcv, ts_, rows) in enumerate(((TMP[:, 0], t_TMP[0], 0), (TMP[:, 1], t_TMP[1], 128))):
                    pb, tb = bank()

                    def tr(e, pb=pb, srcv=srcv):
                        inst = None
                        for tt in range(4):
                            inst = e.transpose(pb[:, tt * 128:(tt + 1) * 128], srcv[:, tt * 128:(tt + 1) * 128], identF[:])
                        return inst
                    P.op('pe', tr, r=[ts_, t_const], w=[tb])
                    P.op('act', lambda e, pb=pb, si=si: e.activation(out=TMPB[:, si, :], in_=pb[:], func=AF.Identity), r=[tb], w=[t_TMPB[si]])
                    P.dma('pool', kctx.ap()[l, rows:rows + 128, :], TMPB[:, si, :], r=[t_TMPB[si]], wp=[t_kctxs[l]], key=t_TMPB[si])
                for j in range(2):
                    pb, tb = bank()

                    def tr(e, pb=pb, j=j):
                        inst = None
                        for tt in range(4):
                            inst = e.transpose(pb[:, tt * 128:(tt + 1) * 128], TMP[:, 2 + tt // 2, (tt % 2) * 256 + j * 128:(tt % 2) * 256 + (j + 1) * 128], identF[:])
                        return inst
                    P.op('pe', tr, r=[t_TMP[2], t_const], w=[tb])
                    P.op('act', lambda e, pb=pb, j=j: e.activation(out=TMPB[:, 2 + j, :], in_=pb[:], func=AF.Identity), r=[tb], w=[t_TMPB[2 + j]])
                pb, tb = bank()

                def tr(e, pb=pb, krp=krp):
                    inst = None
                    for tt in range(4):
                        inst = e.transpose(pb[0:96, tt * 128:(tt + 1) * 128], krp[:, tt, :], identF[:])
                    return inst
                P.op('pe', tr, r=[t_TMP[4], t_const], w=[tb])
                krb = BIG[0:96, 0:512]
                P.op('act', lambda e, pb=pb, krb=krb: e.activation(out=krb[64:96, :], in_=pb[64:96, :], func=AF.Identity), r=[tb], w=[t_BIG])
                for h in range(8):
                    pb, tb = bank()

                    def mm(e, pb=pb, h=h, wkv=wkv):
                        inst = None
                        for j in range(2):
                            inst = e.matmul(pb[0:64, :], lhsT=wkv[:, j, h * 64:(h + 1) * 64], rhs=TMPB[:, 2 + j, :], start=(j == 0), stop=(j == 1))
                        return inst
                    P.op('pe', mm, r=[tw, t_TMPB[2], t_TMPB[3]], w=[tb])
                    kh = BIG[0:96, 512 * (1 + h % 2):512 * (2 + h % 2)]
                    tkh = t_TMP[h % 2]
                    P.op('act', lambda e, pb=pb, kh=kh: e.activation(out=kh[0:64, :], in_=pb[0:64, :], func=AF.Identity), r=[tb], w=[tkh])
                    P.op('pool', lambda e, kh=kh, krb=krb: e.tensor_copy(out=kh[64:96, :], in_=krb[64:96, :]), r=[t_BIG], wp=[tkh])
                    P.dma('pool', kctx.ap()[l, 256 + h * 96:256 + (h + 1) * 96, :], kh, r=[tkh, t_BIG], wp=[t_kctxs[l]], key=tkh)
                for tt in range(4):
                    pb, tb = bank()

                    def mm(e, pb=pb, tt=tt, wkv=wkv):
                        inst = None
                        for j in range(2):
                            inst = e.matmul(pb[:, :], lhsT=TMPB[:, 2 + j, tt * 128:(tt + 1) * 128], rhs=wkv[:, j, 512:1024], start=(j == 0), stop=(j == 1))
                        return inst
                    P.op('pe', mm, r=[tw, t_TMPB[2], t_TMPB[3]], w=[tb])
                    vst = BIG[:, 2048 + (tt % 2) * 512: 2048 + (tt % 2 + 1) * 512]
                    tvs = t_TMP[4 + tt % 2]
                    P.op('dve', lambda e, pb=pb, vst=vst: e.tensor_copy(out=vst, in_=pb[:]), r=[tb], w=[tvs])
                    P.dma('pool', vctx.ap()[l, tt * 128:(tt + 1) * 128, 256:768], vst, r=[tvs, t_BIG], wp=[t_kctxs[l]], key=tvs)
            ctx_layer(0)

            ckpt('ctx')
            t_hscr = [T("hscr%d" % i) for i in range(6)]
            t_q = {0: T("q_p"), 1: T("q_s")}
            t_k = {0: T("k_p"), 1: T("kg_in")}
            t_v = {0: T("v_p"), 1: T("vg_in")}
            t_o = {0: T("o_p"), 1: T("o_s")}
            t_kgo = T("kg_out"); t_vgo = T("vg_out")
            kst = T("kstore")
            t_outs = T("outs")

            def rms_stat(src_t, nk, sq_views, scale, out_rstd, t_out, blockdiag=False):
                pb, tb = bank()

                def mm(e, pb=pb):
                    inst = None
                    for k in range(nk):
                        inst = e.matmul(pb[:], lhsT=(bonesB if blockdiag else onesB)[:], rhs=sq_views[k], start=(k == 0), stop=(k == nk - 1))
                    return inst
                P.op('pe', mm, r=list(src_t) + [t_const], w=[tb])
                P.op('act', lambda e, pb=pb: e.activation(out=out_rstd, in_=pb[:], func=AF.Ln, scale=scale, bias=EPS), r=[tb], w=[t_out])
                P.op('act', lambda e: e.activation(out=out_rstd, in_=out_rstd, func=AF.Exp, scale=-0.5), r=[t_out], w=[t_out])

            def prenorm(xv, t_x, l, j, ia, ish):
                for k in range(8):
                    P.op('pool', lambda e, k=k: e.tensor_tensor(out=HT[:, k, :], in0=xv[:, k, :], in1=xv[:, k, :], op=ALU.mult), r=[t_x], w=[t_HT] if k == 0 else (), wp=() if k == 0 else [t_HT])
                rstd = TMP[:, 5, :]
                rms_stat([t_HT], 8, [HT[:, k, :] for k in range(8)], 1.0 / D, rstd, t_TMP[5])
                for k in range(8):
                    P.op('dve', lambda e, k=k: e.scalar_tensor_tensor(out=RES[:, k, :], in0=xv[:, k, :], scalar=MV[:, ia, l, j, k:k + 1], in1=rstd, op0=ALU.mult, op1=ALU.mult),
                         r=[t_x, t_TMP[5]] + t_par, w=[t_RES] if k == 0 else (), wp=() if k == 0 else [t_RES])
                for k in range(8):
                    P.op('act', lambda e, k=k: e.activation(out=HT[:, k, :], in_=RES[:, k, :], func=AF.Identity, bias=MV[:, ish, l, j, k:k + 1], scale=1.0),
                         r=[t_RES] + t_par, w=[t_HT] if k == 0 else (), wp=() if k == 0 else [t_HT])

            def postnorm_resid(xv, t_x, l, j, ig):
                for k in range(8):
                    P.op('act', lambda e, k=k: e.activation(out=HT[:, k, :], in_=RES[:, k, :], func=AF.Square), r=[t_RES], w=[t_HT] if k == 0 else (), wp=() if k == 0 else [t_HT])
                rstd = TMP[:, 5, :]
                rms_stat([t_HT], 8, [HT[:, k, :] for k in range(8)], 1.0 / D, rstd, t_TMP[5])
                for k in range(8):
                    P.op('dve', lambda e, k=k: e.scalar_tensor_tensor(out=RES[:, k, :], in0=RES[:, k, :], scalar=MV[:, ig, l, j, k:k + 1], in1=rstd, op0=ALU.mult, op1=ALU.mult),
                         r=[t_TMP[5]] + t_par, wp=[t_RES])
                for k in range(8):
                    P.op('pool', lambda e, k=k: e.tensor_tensor(out=xv[:, k, :], in0=xv[:, k, :], in1=RES[:, k, :], op=ALU.add), r=[t_RES], wp=[t_x])

            def load_w(dram_view, shape_k, ncols, deps):
                ws, tw = wslot()
                v = ws[:, 0:shape_k * ncols].rearrange("p (k c) -> p k c", k=shape_k)
                P.dma('sp', v, dram_view, r=deps, w=[tw], key=tw)
                return v, tw

            def proj_fm(wv, tw, c0, m, nk, rhs_of_k, rhs_t, prow=None):
                pb, tb = bank()

                def mm(e, pb=pb):
                    inst = None
                    for k in range(nk):
                        inst = e.matmul(pb[0:m, :], lhsT=wv[:, k, c0:c0 + m], rhs=rhs_of_k(k), start=(k == 0), stop=(k == nk - 1))
                    return inst
                P.op('pe', mm, r=[tw] + list(rhs_t), w=[tb])
                return pb, tb

            def D1(l, grp, ci, xv, t_x, col0):
                j = grp
                hidx = ci if grp == 0 else 2 + ci
                q_d = (q_p if grp == 0 else q_s).ap()
                cs = slice(col0, col0 + CH)
                if grp == 0:
                    k_d = k_p.ap()
                    v_dst = v_p.ap()[col0:col0 + CH, :]
                    kcs = cs
                else:
                    k_d = kgi[ci // 2].ap()
                    v_dst = vgi[ci // 2].ap()[(ci % 2) * CH:(ci % 2 + 1) * CH, :]
                    kcs = slice((ci % 2) * CH, (ci % 2 + 1) * CH)
                prenorm(xv, t_x, l, j, 0, 1)
                P.dma('pool', h_scr.ap()[hidx], HTflat, r=[t_HT], w=[t_hscr[hidx]], key=t_HT)
                if grp == 1:
                    P.dma('sp', ROPE[:, 0:2, :], c_ropeA[:, :, cs].rearrange("t p c -> p t c"), w=[t_ROPE], key=t_ROPE)
                    P.dma('sp', ROPE[0:96, 2:4, :], c_ropeB[:, :, cs].rearrange("t p c -> p t c"), wp=[t_ROPE], key=t_ROPE)
                ckpt('d1a')
                hk = lambda k: HT[:, k, :]
                stage_ctr = [0]

                def stage_bf():
                    i = stage_ctr[0] % 2
                    stage_ctr[0] += 1
                    return TMPB[:, i, :], t_TMPB[i]

                def rope_combine(pb1, tb1, pb2, tb2, rows, ci_c, ci_s, outv, t_out, gcol=None, rstd=None, t_rstd=None):
                    r0, r1 = rows
                    if gcol is None:
                        P.op('dve', lambda e: e.tensor_tensor(out=TMP[r0:r1, 0, :], in0=pb1[r0:r1, :], in1=ROPE[r0:r1, ci_c, :], op=ALU.mult), r=[tb1, t_ROPE], w=[t_TMP[0]])
                        P.op('dve', lambda e: e.tensor_tensor(out=TMP[r0:r1, 1, :], in0=pb2[r0:r1, :], in1=ROPE[r0:r1, ci_s, :], op=ALU.mult), r=[tb2, t_ROPE], w=[t_TMP[1]])
                    else:
                        P.op('act', lambda e: e.activation(out=TMP[r0:r1, 0, :], in_=pb1[r0:r1, :], func=AF.Identity, scale=gq4[r0:r1, gcol, l:l + 1]), r=[tb1] + t_par, w=[t_TMP[0]])
                        P.op('act', lambda e: e.activation(out=TMP[r0:r1, 1, :], in_=pb2[r0:r1, :], func=AF.Identity, scale=gq4[r0:r1, gcol + 1, l:l + 1]), r=[tb2] + t_par, w=[t_TMP[1]])
                        P.op('dve', lambda e: e.tensor_tensor(out=TMP[r0:r1, 0, :], in0=TMP[r0:r1, 0, :], in1=ROPE[r0:r1, ci_c, :], op=ALU.mult), r=[t_ROPE], w=[t_TMP[0]])
                        P.op('dve', lambda e: e.tensor_tensor(out=TMP[r0:r1, 1, :], in0=TMP[r0:r1, 1, :], in1=ROPE[r0:r1, ci_s, :], op=ALU.mult), r=[t_ROPE], w=[t_TMP[1]])
                    if rstd is None:
                        P.op('pool', lambda e: e.tensor_tensor(out=outv[r0:r1, :], in0=TMP[r0:r1, 0, :], in1=TMP[r0:r1, 1, :], op=ALU.add), r=[t_TMP[0], t_TMP[1]], w=[t_out])
                    else:
                        P.op('pool', lambda e: e.tensor_tensor(out=TMP[r0:r1, 0, :], in0=TMP[r0:r1, 0, :], in1=TMP[r0:r1, 1, :], op=ALU.add), r=[t_TMP[1]], w=[t_TMP[0]])
                        P.op('dve', lambda e: e.tensor_tensor(out=outv[r0:r1, :], in0=TMP[r0:r1, 0, :], in1=rstd, op=ALU.mult), r=[t_TMP[0], t_rstd], w=[t_out])

                def out_tok(srcF, t_src, rows, ncols_per, dst_ap, colsel=None):
                    pb, tb = bank()

                    def tr(e, pb=pb):
                        inst = None
                        for tt in range(4):
                            inst = e.transpose(pb[:, tt * 128:tt * 128 + rows], srcF[0:rows, tt * 128:(tt + 1) * 128], identF[0:rows, 0:rows])
                        return inst
                    P.op('pe', tr, r=[t_src, t_const], w=[tb])
                    stg = TMP[:, 4, :]
                    P.op('dve', lambda e, pb=pb: e.tensor_copy(out=stg, in_=pb[:]), r=[tb], w=[t_TMP[4]])
                    sv = stg.rearrange("p (t c) -> p t c", t=4)
                    if colsel is not None:
                        sv = sv[:, :, colsel[0]:colsel[1]]
                    else:
                        sv = sv[:, :, 0:ncols_per]
                    for s_ in range(2):
                        P.dma('pool', dst_ap[s_], sv[:, 2 * s_:2 * s_ + 2, :], r=[t_TMP[4]], wp=[t_outs], key=t_TMP[4], slow=True)

                def tok_dst(o_ap, c0=None, c1=None):
                    vs_ = []
                    for s_ in range(2):
                        v = o_ap[2 * ci + s_, l].rearrange("(u p) f -> p u f", p=128)
                        if c0 is not None:
                            v = v[:, :, c0:c1]
                        vs_.append(v)
                    return vs_

                for (off, swoff, nblk, qrow0, is_c) in ((O_QA, 0, 4, 0, False), (O_KA, 512, 1, None, False), (O_QC, 640, 4, 1280, True), (O_KC, 1152, 1, None, True)):
                    wd = nblk * 128
                    is_k = nblk == 1
                    wv, tw = load_w(win_b.ap()[l, :, off:off + wd].rearrange("(k p) c -> p k c", p=128), 8, wd, [t_wcs[l]])
                    if grp == 1:
                        wv2, tw2 = load_w(wsw_b.ap()[l, :, swoff:swoff + wd].rearrange("(k p) c -> p k c", p=128), 8, wd, [t_wsws[l]])
                    for b in range(nblk):
                        pb1, tb1 = proj_fm(wv, tw, b * 128, 128, 8, hk, [t_HT])
                        if grp == 1:
                            pb2, tb2 = proj_fm(wv2, tw2, b * 128, 128, 8, hk, [t_HT])
                        outv, t_out = stage_bf()
                        rstd = None
                        if is_c:
                            P.op('act', lambda e, pb1=pb1: e.activation(out=TMPB[:, 2, :], in_=pb1[:], func=AF.Square), r=[tb1], w=[t_TMPB[2]])
                            rstd = TMP[:, 2, :]
                            rms_stat([t_TMPB[2]], 1, [TMPB[:, 2, :]], 1.0 / 64, rstd, t_TMP[2], blockdiag=True)
                        gcol = (2 if is_k else 0) if is_c else None
                        if grp == 1:
                            rope_combine(pb1, tb1, pb2, tb2, (0, 128), 0, 1, outv, t_out, gcol=gcol, rstd=rstd, t_rstd=t_TMP[2])
                        else:
                            if is_c:
                                P.op('act', lambda e, pb1=pb1, gcol=gcol: e.activation(out=TMP[:, 3, :], in_=pb1[:], func=AF.Identity, scale=gq4[:, gcol, l:l + 1]), r=[tb1] + t_par, w=[t_TMP[3]])
                                P.op('dve', lambda e, rstd=rstd: e.tensor_tensor(out=TMP[:, 3, :], in0=TMP[:, 3, :], in1=rstd, op=ALU.mult), r=[t_TMP[2]], w=[t_TMP[3]])
                            else:
                                P.op('dve', lambda e, pb1=pb1: e.tensor_copy(out=TMP[:, 3, :], in_=pb1[:]), r=[tb1], w=[t_TMP[3]])
                            P.op('act', lambda e, outv=outv: e.activation(out=outv, in_=TMP[:, 3, :], func=AF.Identity), r=[t_TMP[3]], w=[t_out])
                            if is_k:
                                out_tok(TMP[:, 3, :], t_TMP[3], 128, 128, tok_dst(o_ck if is_c else o_ak))
                        if l == 0 and grp == 1 and ci == 0:
                            ckpt('blk_%d_%d' % (off, b))
                        if is_k:
                            krow = 128 if is_c else 0
                            P.dma('pool', k_d[krow:krow + 128, kcs], outv, r=[t_out], wp=[t_k[grp]], key=t_out)
                        else:
                            P.dma('pool', q_d[qrow0 + b * 128:qrow0 + (b + 1) * 128, cs], outv, r=[t_out], wp=[t_q[grp]], key=t_out)

                ckpt('d1b')
                wv, tw = load_w(win_b.ap()[l, :, O_VA:O_VA + 128].rearrange("(k p) c -> p k c", p=128), 8, 128, [t_wcs[l]])
                wvc, twc = load_w(win_b.ap()[l, :, O_VC:O_VC + 128].rearrange("(k p) c -> p k c", p=128), 8, 128, [t_wcs[l]])
                vstage = BIG[:, 0:4 * 768].rearrange("p (t c) -> p t c", t=4)
                for vi, (wvx, twx, o_dst) in enumerate(((wv, tw, o_av), (wvc, twc, o_cv))):
                    pb, tb = bank()

                    def mm(e, pb=pb, wvx=wvx):
                        inst = None
                        for tt in range(4):
                            for k in range(8):
                                inst = e.matmul(pb[:, tt * 128:(tt + 1) * 128], lhsT=HT[:, k, tt * 128:(tt + 1) * 128], rhs=wvx[:, k, :], start=(k == 0), stop=(k == 7))
                        return inst
                    P.op('pe', mm, r=[twx, t_HT], w=[tb])
                    P.op('act', lambda e, pb=pb, vi=vi: e.activation(out=vstage[:, :, vi * 128:(vi + 1) * 128], in_=pb[:].rearrange("p (t c) -> p t c", t=4), func=AF.Identity),
                         r=[tb], w=[t_BIG] if vi == 0 else (), wp=() if vi == 0 else [t_BIG])
                    if grp == 0:
                        P.op('dve', lambda e, pb=pb: e.tensor_copy(out=TMP[:, 4, :], in_=pb[:]), r=[tb], w=[t_TMP[4]])
                        for s_ in range(2):
                            P.dma('pool', tok_dst(o_dst)[s_], TMP[:, 4, :].rearrange("p (t c) -> p t c", t=4)[:, 2 * s_:2 * s_ + 2, :], r=[t_TMP[4]], wp=[t_outs], key=t_TMP[4])

                ckpt('d1c')
                wv, tw = load_w(win_b.ap()[l, :, O_QBD:O_QBD + 384].rearrange("(k p) c -> p k c", p=128), 8, 384, [t_wcs[l]])
                QN = BIG[:, 4096:4096 + 1536].rearrange("p (k c) -> p k c", k=3)
                t_QN = T("QN")
                QF = RES[:, 0:3, :]
                for b in range(3):
                    pb, tb = proj_fm(wv, tw, b * 128, 128, 8, hk, [t_HT])
                    P.op('act', lambda e, pb=pb, b=b: e.activation(out=QF[:, b, :], in_=pb[:], func=AF.Identity), r=[tb], w=[t_RES] if b == 0 else (), wp=() if b == 0 else [t_RES])
                    P.op('pool', lambda e, b=b: e.tensor_tensor(out=QN[:, b, :], in0=QF[:, b, :], in1=QF[:, b, :], op=ALU.mult), r=[t_RES, t_BIG], w=[t_QN] if b == 0 else (), wp=() if b == 0 else [t_QN])
                rms_stat([t_QN], 3, [QN[:, b, :] for b in range(3)], 1.0 / 384, TMP[:, 2, :], t_TMP[2])
                for b in range(3):
                    P.op('dve', lambda e, b=b: e.scalar_tensor_tensor(out=QN[:, b, :], in0=QF[:, b, :], scalar=gqaT[:, l, b:b + 1], in1=TMP[:, 2, :], op0=ALU.mult, op1=ALU.mult),
                         r=[t_RES, t_TMP[2]] + t_par, w=[t_QN] if b == 0 else (), wp=() if b == 0 else [t_QN])
                wq, twq = load_w(wqup_b.ap()[l].rearrange("(k p) c -> p k c", p=128), 3, 768, [t_wcs[l]])
                if grp == 1:
                    wq2, twq2 = load_w(wqupsw_b.ap()[l].rearrange("(k p) c -> p k c", p=128), 3, 768, [t_wsws[l]])
                qn_k = lambda k: QN[:, k, :]
                for h in range(8):
                    pb1, tb1 = proj_fm(wq, twq, h * 96, 96, 3, qn_k, [t_QN])
                    outv, t_out = stage_bf()
                    if grp == 1:
                        pb2, tb2 = proj_fm(wq2, twq2, h * 96, 96, 3, qn_k, [t_QN])
                        rope_combine(pb1, tb1, pb2, tb2, (0, 96), 2, 3, outv, t_out)
                    else:
                        P.op('act', lambda e, pb1=pb1, outv=outv: e.activation(out=outv[0:96, :], in_=pb1[0:96, :], func=AF.Identity), r=[tb1], w=[t_out])
                    P.dma('pool', q_d[512 + h * 96:512 + (h + 1) * 96, cs], outv[0:96, :], r=[t_out], wp=[t_q[grp]], key=t_out)

                ckpt('d1d')
                wv, tw = load_w(win_b.ap()[l, :, O_KVBD:O_KVBD + 288].rearrange("(k p) c -> p k c", p=128), 8, 288, [t_wcs[l]])
                CKN = BIG[:, 4096:4096 + 1024].rearrange("p (k c) -> p k c", k=2)
                KF = RES[:, 0:2, :]
                for b in range(2):
                    pb, tb = proj_fm(wv, tw, b * 128, 128, 8, hk, [t_HT])
                    P.op('act', lambda e, pb=pb, b=b: e.activation(out=KF[:, b, :], in_=pb[:], func=AF.Identity), r=[tb], w=[t_RES] if b == 0 else (), wp=() if b == 0 else [t_RES])
                    P.op('pool', lambda e, b=b: e.tensor_tensor(out=CKN[:, b, :], in0=KF[:, b, :], in1=KF[:, b, :], op=ALU.mult), r=[t_RES, t_BIG], w=[t_QN] if b == 0 else (), wp=() if b == 0 else [t_QN])
                rms_stat([t_QN], 2, [CKN[:, b, :] for b in range(2)], 1.0 / 256, TMP[:, 2, :], t_TMP[2])
                for b in range(2):
                    P.op('dve', lambda e, b=b: e.scalar_tensor_tensor(out=KF[:, b, :], in0=KF[:, b, :], scalar=gkvaT[:, l, b:b + 1], in1=TMP[:, 2, :], op0=ALU.mult, op1=ALU.mult),
                         r=[t_TMP[2]] + t_par, wp=[t_RES])
                    P.op('act', lambda e, b=b: e.activation(out=CKN[:, b, :], in_=KF[:, b, :], func=AF.Identity), r=[t_RES], w=[t_QN] if b == 0 else (), wp=() if b == 0 else [t_QN])
                    if grp == 0:
                        out_tok(KF[:, b, :], t_RES, 128, 128, tok_dst(o_ckv, b * 128, (b + 1) * 128))
                pbk, tbk = proj_fm(wv, tw, 192, 96, 8, hk, [t_HT])
                KR = TMPB[:, 3, :]
                if grp == 1:
                    wv2, tw2 = load_w(wsw_b.ap()[l, :, 1280:1376].rearrange("(k p) c -> p k c", p=128), 8, 96, [t_wsws[l]])
                    pbk2, tbk2 = proj_fm(wv2, tw2, 0, 96, 8, hk, [t_HT])
                    rope_combine(pbk, tbk, pbk2, tbk2, (64, 96), 2, 3, KR, t_TMPB[3])
                else:
                    P.op('dve', lambda e: e.tensor_copy(out=TMP[0:96, 3, :], in_=pbk[0:96, :]), r=[tbk], w=[t_TMP[3]])
                    P.op('act', lambda e: e.activation(out=KR[64:96, :], in_=TMP[64:96, 3, :], func=AF.Identity), r=[t_TMP[3]], w=[t_TMPB[3]])
                    out_tok(TMP[:, 3, :], t_TMP[3], 96, 96, tok_dst(o_kr), colsel=(64, 96))
                wk, twk = load_w(wkvup_b.ap()[l].rearrange("(k p) c -> p k c", p=128), 2, 1024, [t_wcs[l]])
                ck_k = lambda k: CKN[:, k, :]
                for h in range(8):
                    pb, tb = proj_fm(wk, twk, h * 64, 64, 2, ck_k, [t_QN])
                    outv, t_out = stage_bf()
                    P.op('act', lambda e, pb=pb, outv=outv: e.activation(out=outv[0:64, :], in_=pb[0:64, :], func=AF.Identity), r=[tb], w=[t_out])
                    P.op('pool', lambda e, outv=outv: e.tensor_copy(out=outv[64:96, :], in_=KR[64:96, :]), r=[t_TMPB[3]], wp=[t_out])
                    P.dma('pool', k_d[256 + h * 96:256 + (h + 1) * 96, kcs], outv[0:96, :], r=[t_out], wp=[t_k[grp]], key=t_out)
                for tt in range(4):
                    pb, tb = bank()

                    def mm(e, pb=pb, tt=tt):
                        inst = None
                        for k in range(2):
                            inst = e.matmul(pb[:, :], lhsT=CKN[:, k, tt * 128:(tt + 1) * 128], rhs=wk[:, k, 512:1024], start=(k == 0), stop=(k == 1))
                        return inst
                    P.op('pe', mm, r=[twk, t_QN], w=[tb])
                    P.op('act' if tt % 2 else 'dve',
                         (lambda e, pb=pb, tt=tt: e.activation(out=vstage[:, tt, 256:768], in_=pb[:], func=AF.Identity)) if tt % 2 else
                         (lambda e, pb=pb, tt=tt: e.tensor_copy(out=vstage[:, tt, 256:768], in_=pb[:])), r=[tb], wp=[t_BIG])
                P.dma('pool', v_dst.rearrange("(t p) c -> p t c", p=128), vstage, r=[t_BIG], wp=[t_v[grp]], key=t_BIG)

            att_ctr = [0]

            def att_head(Kv, t_K, krows, Vv, t_V, Qv, t_Q, qcols, ktiles, scale, out_rows_ap, t_o_dst, sink_ap=None, odd=False, okey=None, masks=None):
                r0, r1 = krows
                q0, nq = qcols
                po, tpo = obank()
                nkt = len(ktiles)
                LOOK = 3
                pend = {}
                for i in range(nkt + LOOK):
                    if i < nkt:
                        kt = ktiles[i]
                        pbs, tbs = bank()
                        P.op('pe', lambda e, pbs=pbs, kt=kt: e.matmul(pbs[:, 0:nq], lhsT=Kv[r0:r1, kt * 128:(kt + 1) * 128], rhs=Qv[r0:r1, q0:q0 + nq], start=True, stop=True),
                             r=[t_K, t_Q], w=[tbs])
                        pend[i] = (pbs, tbs)
                    jx = i - LOOK
                    if jx >= 0:
                        kt = ktiles[jx]
                        pbs, tbs = pend.pop(jx)
                        pi = att_ctr[0] % 4
                        att_ctr[0] += 1
                        pt = TMPB[:, pi, 0:nq]
                        P.op('act', lambda e, pbs=pbs, pt=pt: e.activation(out=pt, in_=pbs[:, 0:nq], func=AF.Exp, scale=scale), r=[tbs], w=[t_TMPB[pi]])
                        if masks is not None and masks[jx] is not None:
                            P.op('dve', lambda e, pt=pt, m=masks[jx]: e.tensor_tensor(out=pt, in0=pt, in1=m, op=ALU.mult), r=[t_const], w=[t_TMPB[pi]])
                        P.op('pe', lambda e, pt=pt, kt=kt, jx=jx: e.matmul(po[:, 0:nq], lhsT=Vv[:, kt, :], rhs=pt, start=(jx == 0), stop=(jx == nkt - 1)),
                             r=[t_V, t_TMPB[pi]], w=[tpo] if jx == 0 else (), wp=() if jx == 0 else [tpo])
                ti = att_ctr[0] % 2
                dt_ = TMP[0:64, ti, 0:nq]
                tdt = t_TMP[ti]
                if sink_ap is not None:
                    P.op('dve', lambda e: e.tensor_copy(out=dt_, in_=po[64:128, 0:nq]), r=[tpo], w=[tdt])
                    P.op('dve', lambda e: e.tensor_scalar(out=dt_, in0=dt_, scalar1=sink_ap, scalar2=None, op0=ALU.add), r=t_par, w=[tdt])
                    P.op('dve', lambda e: e.reciprocal(out=dt_, in_=dt_), r=[tdt], w=[tdt])
                else:
                    P.op('dve', lambda e: e.reciprocal(out=dt_, in_=po[64:128, 0:nq]), r=[tpo], w=[tdt])
                oi = 2 + att_ctr[0] % 2
                ob = TMP[0:64, oi, :].bitcast(BF16)[:, 0:nq]
                P.op('dve', lambda e: e.tensor_tensor(out=ob, in0=po[0:64, 0:nq], in1=dt_, op=ALU.mult), r=[tpo, tdt], w=[t_TMP[oi]])
                P.dma('pool', out_rows_ap, ob, r=[t_TMP[oi]], wp=[t_o_dst], key=t_TMP[oi])

            def att_multi(Kv, t_K, krows, Vv, t_V, Qv, t_Q, scale, segs, t_o_dst):
                r0, r1 = krows
                flat = []
                for si, sg in enumerate(segs):
                    for i, kt in enumerate(sg['ktiles']):
                        flat.append((si, i, kt))
                LOOK = 3
                pend = {}
                po_of = {}
                n = len(flat)
                for x in range(n + LOOK):
                    if x < n:
                        si, i, kt = flat[x]
                        q0, nq = segs[si]['q']
                        pbs, tbs = bank()
                        P.op('pe', lambda e, pbs=pbs, kt=kt, q0=q0, nq=nq: e.matmul(pbs[:, 0:nq], lhsT=Kv[r0:r1, kt * 128:(kt + 1) * 128], rhs=Qv[r0:r1, q0:q0 + nq], start=True, stop=True),
                             r=[t_K, t_Q], w=[tbs])
                        pend[x] = (pbs, tbs)
                    y = x - LOOK
                    if y < 0:
                        continue
                    si, i, kt = flat[y]
                    sg = segs[si]
                    q0, nq = sg['q']
                    nkt = len(sg['ktiles'])
                    if i == 0:
                        po_of[si] = obank()
                    po, tpo = po_of[si]
                    pbs, tbs = pend.pop(y)
                    pi = att_ctr[0] % 4
                    att_ctr[0] += 1
                    pt = TMPB[:, pi, 0:nq]
                    P.op('act', lambda e, pbs=pbs, pt=pt, nq=nq: e.activation(out=pt, in_=pbs[:, 0:nq], func=AF.Exp, scale=scale), r=[tbs], w=[t_TMPB[pi]])
                    mk = sg.get('masks')
                    if mk is not None and mk[i] is not None:
                        P.op('dve', lambda e, pt=pt, m=mk[i]: e.tensor_tensor(out=pt, in0=pt, in1=m, op=ALU.mult), r=[t_const], w=[t_TMPB[pi]])
                    P.op('pe', lambda e, pt=pt, kt=kt, i=i, po=po, nq=nq, nkt=nkt: e.matmul(po[:, 0:nq], lhsT=Vv[:, kt, :], rhs=pt, start=(i == 0), stop=(i == nkt - 1)),
                         r=[t_V, t_TMPB[pi]], w=[tpo] if i == 0 else (), wp=() if i == 0 else [tpo])
                    if i != nkt - 1:
                        continue
                    ti = att_ctr[0] % 2
                    dt_ = TMP[0:64, ti, 0:nq]
                    tdt = t_TMP[ti]
                    sink_ap = sg.get('sink')
                    if False:
                        P.op('dve', lambda e, dt_=dt_, po=po, nq=nq: e.tensor_copy(out=dt_, in_=po[64:128, 0:nq]), r=[tpo], w=[tdt])
                        if sink_ap is not None:
                            P.op('dve', lambda e, dt_=dt_, sink_ap=sink_ap: e.tensor_scalar(out=dt_, in0=dt_, scalar1=sink_ap, scalar2=None, op0=ALU.add), r=t_par, w=[tdt])
                        P.op('act', lambda e, dt_=dt_: e.activation(out=dt_, in_=dt_, func=AF.Ln), r=[tdt], w=[tdt])
                        P.op('act', lambda e, dt_=dt_: e.activation(out=dt_, in_=dt_, func=AF.Exp, scale=-1.0), r=[tdt], w=[tdt])
                    elif sink_ap is not None:
                        P.op('dve', lambda e, dt_=dt_, po=po, nq=nq: e.tensor_copy(out=dt_, in_=po[64:128, 0:nq]), r=[tpo], w=[tdt])
                        P.op('dve', lambda e, dt_=dt_, sink_ap=sink_ap: e.tensor_scalar(out=dt_, in0=dt_, scalar1=sink_ap, scalar2=None, op0=ALU.add), r=t_par, w=[tdt])
                        P.op('dve', lambda e, dt_=dt_: e.reciprocal(out=dt_, in_=dt_), r=[tdt], w=[tdt])
                    else:
                        P.op('dve', lambda e, dt_=dt_, po=po, nq=nq: e.reciprocal(out=dt_, in_=po[64:128, 0:nq]), r=[tpo], w=[tdt])
                    oi = 2 + att_ctr[0] % 2
                    ob = TMP[0:64, oi, :].bitcast(BF16)[:, 0:nq]
                    P.op('dve', lambda e, ob=ob, po=po, dt_=dt_, nq=nq: e.tensor_tensor(out=ob, in0=po[0:64, 0:nq], in1=dt_, op=ALU.mult), r=[tpo, tdt], w=[t_TMP[oi]])
                    P.dma('pool', sg['out'], ob, r=[t_TMP[oi]], wp=[t_o_dst], key=t_TMP[oi])

            t_KT = [T("KT0"), T("KT1")]; t_VT = [T("VT0"), T("VT1")]; t_QT = [T("QT0"), T("QT1")]
            ones_set = [False]

            def arena_to_att():
                for s_ in t_KT + t_VT + t_QT:
                    for t in arena_all:
                        _merge(s_.r, t.r)
                        _merge(s_.r, t.w)
                for i in range(2):
                    P.op('pool', lambda e, i=i: e.memset(VT[i][:, :, 64:128], 1.0), r=[], w=[t_VT[i]])

            def att_to_arena():
                for t in arena_all:
                    _merge(t.w, {})
                for t in arena_all:
                    for s in t_KT + t_VT + t_QT:
                        _merge(t.r, s.r)
                        _merge(t.r, s.w)

            def ATT_prompt(l):
                arena_to_att()
                qd, kd, vd, od = q_p.ap(), k_p.ap(), v_p.ap(), o_p.ap()
                for br, (krow, vcol, qrow, orow, sc) in enumerate(((0, 0, 0, 0, SC_A), (128, 128, 1280, 1024, SC_A))):
                    ks = br % 2
                    Kv = KT[ks][:, 0:1024]
                    P.dma('sp', Kv, kd[krow:krow + 128, :], r=[t_k[0]], w=[t_KT[ks]], key=t_KT[ks])
                    for g in range(2):
                        Vv = VT[g]
                        P.dma('sp', Vv[:, 0:8, 0:64], vd[:, vcol + g * 64:vcol + (g + 1) * 64].rearrange("(t p) c -> p t c", p=128), r=[t_v[0]], wp=[t_VT[g]], key=t_VT[g], slow=True)
                    for jb in range(4):
                        qs_ = jb % 2
                        Qv = QT[qs_][:, 0:1024]
                        P.dma('sp', Qv, qd[qrow + jb * 128:qrow + (jb + 1) * 128, :], r=[t_q[0]], w=[t_QT[qs_]], key=t_QT[qs_])
                        for g in range(2):
                            h = jb + 4 * g
                            att_multi(Kv, t_KT[ks], (g * 64, g * 64 + 64), VT[g], t_VT[g], Qv, t_QT[qs_], sc,
                                      [dict(q=(s_ * 256, 256), ktiles=[2 * s_, 2 * s_ + 1], out=od[orow + h * 64:orow + (h + 1) * 64, s_ * 256:(s_ + 1) * 256],
                                            sink=(sinkE[0:64, l, h:h + 1] if br == 0 else None)) for s_ in range(4)], t_o[0])
                for h in range(8):
                    ks = h % 2
                    Kv = KT[ks][:, 0:1024]
                    P.dma('sp', Kv[0:96, :], kd[256 + h * 96:256 + (h + 1) * 96, :], r=[t_k[0]], w=[t_KT[ks]], key=t_KT[ks])
                    Vv = VT[ks]
                    P.dma('sp', Vv[:, 0:8, 0:64], vd[:, 256 + h * 64:256 + (h + 1) * 64].rearrange("(t p) c -> p t c", p=128), r=[t_v[0]], wp=[t_VT[ks]], key=t_VT[ks], slow=True)
                    Qv = QT[ks][:, 0:1024]
                    P.dma('sp', Qv[0:96, :], qd[512 + h * 96:512 + (h + 1) * 96, :], r=[t_q[0]], w=[t_QT[ks]], key=t_QT[ks])
                    att_multi(Kv, t_KT[ks], (0, 96), Vv, t_VT[ks], Qv, t_QT[ks], SC_B,
                              [dict(q=(s_ * 256, 256), ktiles=[2 * s_, 2 * s_ + 1], out=od[512 + h * 64:512 + (h + 1) * 64, s_ * 256:(s_ + 1) * 256]) for s_ in range(4)], t_o[0])
                att_to_arena()

            def ATT_sample(l):
                arena_to_att()
                qd, od = q_s.ap(), o_s.ap()

                def load_K(ks, row0, nrows):
                    Kv = KT[ks]
                    for r in range(2):
                        for t in range(2):
                            first = (r == 0 and t == 0)
                            P.dma('sp', Kv[0:nrows, (2 * r + t) * 1024:(2 * r + t + 1) * 1024], kgo[t].ap()[r * 1024 + row0:r * 1024 + row0 + nrows, :], r=[t_kgo],
                                  w=[t_KT[ks]] if first else (), wp=() if first else [t_KT[ks]], key=t_KT[ks])
                    P.dma('sp', Kv[0:nrows, 4096:4608], kctx.ap()[l, row0:row0 + nrows, :], r=[t_kctxs[l], t_wcs[l]], wp=[t_KT[ks]], key=t_KT[ks])
                    return Kv

                def load_V(vs, col0):
                    Vv = VT[vs]
                    for r in range(2):
                        for t in range(2):
                            P.dma('sp', Vv[:, (2 * r + t) * 8:(2 * r + t + 1) * 8, 0:64], vgo[t].ap()[r * 1024:(r + 1) * 1024, col0:col0 + 64].rearrange("(t p) c -> p t c", p=128), r=[t_vgo], wp=[t_VT[vs]], key=t_VT[vs], slow=True)
                    P.dma('sp', Vv[:, 32:36, 0:64], vctx.ap()[l, :, col0:col0 + 64].rearrange("(t p) c -> p t c", p=128), r=[t_kctxs[l], t_wcs[l]], wp=[t_VT[vs]], key=t_VT[vs], slow=True)
                    return Vv
                Kv = load_K(0, 128, 128)
                for g in range(2):
                    load_V(g, 128 + g * 64)
                for jb in range(4):
                    qs_ = jb % 2
                    Qv = QT[qs_]
                    P.dma('sp', Qv, qd[1280 + jb * 128:1280 + (jb + 1) * 128, :], r=[t_q[1]], w=[t_QT[qs_]], key=t_QT[qs_])
                    for g in range(2):
                        h = jb + 4 * g
                        att_multi(Kv, t_KT[0], (g * 64, g * 64 + 64), VT[g], t_VT[g], Qv, t_QT[qs_], SC_A,
                                  [dict(q=(qc * 512, 512), ktiles=list(range(36)), out=od[1024 + h * 64:1024 + (h + 1) * 64, qc * 512:(qc + 1) * 512]) for qc in range(4)], t_o[1])
                for h in range(8):
                    ks = (h + 1) % 2
                    Kv = load_K(ks, 256 + h * 96, 96)
                    Vv = load_V(ks, 256 + h * 64)
                    Qv = QT[ks]
                    P.dma('sp', Qv[0:96, :], qd[512 + h * 96:512 + (h + 1) * 96, :], r=[t_q[1]], w=[t_QT[ks]], key=t_QT[ks])
                    att_multi(Kv, t_KT[ks], (0, 96), Vv, t_VT[ks], Qv, t_QT[ks], SC_B,
                              [dict(q=(qc * 512, 512), ktiles=list(range(36)), out=od[512 + h * 64:512 + (h + 1) * 64, qc * 512:(qc + 1) * 512]) for qc in range(4)], t_o[1])
                Kv = KT[0]
                P.dma('sp', Kv[:, 0:128], kgo[1].ap()[0:128, 896:1024], r=[t_kgo], w=[t_KT[0]], key=t_KT[0])
                for t in range(2):
                    P.dma('sp', Kv[:, 128 + t * 1024:128 + (t + 1) * 1024], kgi[t].ap()[0:128, :], r=[t_k[1]], wp=[t_KT[0]], key=t_KT[0])
                P.dma('sp', Kv[:, 2176:2304], kgo[0].ap()[1024:1152, 0:128], r=[t_kgo], wp=[t_KT[0]], key=t_KT[0])
                P.dma('sp', Kv[:, 2304:2816], kctx.ap()[l, 0:128, :], r=[t_kctxs[l], t_wcs[l]], wp=[t_KT[0]], key=t_KT[0])
                for g in range(2):
                    Vv = VT[g]
                    c0 = g * 64
                    P.dma('sp', Vv[:, 0:1, 0:64], vgo[1].ap()[896:1024, c0:c0 + 64].rearrange("(t p) c -> p t c", p=128), r=[t_vgo], wp=[t_VT[g]], key=t_VT[g], slow=True)
                    for r in range(2):
                        P.dma('sp', Vv[:, 1 + r * 8:1 + (r + 1) * 8, 0:64], vgi[r].ap()[:, c0:c0 + 64].rearrange("(t p) c -> p t c", p=128), r=[t_v[1]], wp=[t_VT[g]], key=t_VT[g], slow=True)
                    P.dma('sp', Vv[:, 17:18, 0:64], vgo[0].ap()[1024:1152, c0:c0 + 64].rearrange("(t p) c -> p t c", p=128), r=[t_vgo], wp=[t_VT[g]], key=t_VT[g], slow=True)
                    P.dma('sp', Vv[:, 18:22, 0:64], vctx.ap()[l, :, c0:c0 + 64].rearrange("(t p) c -> p t c", p=128), r=[t_kctxs[l], t_wcs[l]], wp=[t_VT[g]], key=t_VT[g], slow=True)
                for jb in range(4):
                    qs_ = jb % 2
                    Qv = QT[qs_]
                    P.dma('sp', Qv, qd[jb * 128:(jb + 1) * 128, :], r=[t_q[1]], w=[t_QT[qs_]], key=t_QT[qs_])
                    for g in range(2):
                        h = jb + 4 * g
                        segs = []
                        for qb in range(16):
                            var = 0 if qb == 0 else (2 if qb == 15 else 1)
                            masks = [maskB[:, var, 0:128], None if var == 1 else maskB[:, var, 128:256], maskB[:, var, 256:384], None, None, None, None]
                            segs.append(dict(q=(qb * 128, 128), ktiles=[qb, qb + 1, qb + 2, 18, 19, 20, 21], out=od[h * 64:(h + 1) * 64, qb * 128:(qb + 1) * 128],
                                             sink=sinkE[0:64, l, h:h + 1], masks=masks))
                        att_multi(Kv, t_KT[0], (g * 64, g * 64 + 64), VT[g], t_VT[g], Qv, t_QT[qs_], SC_A, segs, t_o[1])
                att_to_arena()

            def D2(l, grp, ci, xv, t_x, col0):
                j = grp
                hidx = ci if grp == 0 else 2 + ci
                o_d = (o_p if grp == 0 else o_s).ap()
                cs = slice(col0, col0 + CH)
                P.dma('sp', HTflat, h_scr.ap()[hidx], r=[t_hscr[hidx]], w=[t_HT], key=t_HT)
                OT = BIG[:, 0:6144].rearrange("p (k c) -> p k c", k=12)
                t_OT = T("OT")
                P.dma('sp', OT, o_d[:, cs].rearrange("(k p) c -> p k c", p=128), r=[t_o[grp]], w=[t_OT, t_BIG], key=t_OT)
                MT = BIG[:, 6144:10240].rearrange("p (k c) -> p k c", k=8)
                t_MT = T("MT")
                WO = BIG[:, 10240:16384].rearrange("p (k c) -> p k c", k=12)
                t_WO = T("WO")
                for n in range(8):
                    if n % 4 == 0:
                        ng = n // 4
                        P.dma('sp', WO, wo_b.ap()[l, ng].rearrange("p (k c) -> p k c", k=12), r=[t_wcs[l]], w=[t_WO], key=t_WO)
                    wg, twg = load_w(wg_b.ap()[l, n].rearrange("p (k c) -> p k c", k=8), 8, 384, [t_wcs[l]])
                    for b in range(3):
                        pg, tg = bank()

                        def mmg(e, pg=pg, b=b, wg=wg):
                            inst = None
                            for k in range(8):
                                inst = e.matmul(pg[:], lhsT=wg[:, k, b * 128:(b + 1) * 128], rhs=HT[:, k, :], start=(k == 0), stop=(k == 7))
                            return inst
                        P.op('pe', mmg, r=[twg, t_HT], w=[tg])
                        sg = TMP[:, b % 2, :]
                        P.op('act', lambda e, pg=pg, sg=sg: e.activation(out=sg, in_=pg[:], func=AF.Sigmoid), r=[tg], w=[t_TMP[b % 2]])
                        pp, tp = bank()

                        def mmp(e, pp=pp, b=b, n=n):
                            inst = None
                            for k in range(4):
                                inst = e.matmul(pp[:], lhsT=WO[:, b * 4 + k, (n % 4) * 128:(n % 4 + 1) * 128], rhs=OT[:, b * 4 + k, :], start=(k == 0), stop=(k == 3))
                            return inst
                        P.op('pe', mmp, r=[t_WO, t_OT], w=[tp])
                        if b == 0:
                            P.op('dve', lambda e, pp=pp, sg=sg: e.tensor_tensor(out=TMP[:, 2, :], in0=pp[:], in1=sg, op=ALU.mult), r=[tp, t_TMP[0]], w=[t_TMP[2]])
                        else:
                            P.op('dve', lambda e, pp=pp, sg=sg: e.tensor_tensor(out=TMP[:, 3, :], in0=pp[:], in1=sg, op=ALU.mult), r=[tp, t_TMP[b % 2]], w=[t_TMP[3]])
                            if b == 1:
                                P.op('pool', lambda e: e.tensor_tensor(out=TMP[:, 2, :], in0=TMP[:, 2, :], in1=TMP[:, 3, :], op=ALU.add), r=[t_TMP[3]], w=[t_TMP[2]])
                            else:
                                P.op('pool', lambda e, n=n: e.tensor_tensor(out=MT[:, n, :], in0=TMP[:, 2, :], in1=TMP[:, 3, :], op=ALU.add), r=[t_TMP[2], t_TMP[3]], wp=[t_MT])
                for ng in range(2):
                    wv, tw = load_w(wout_b.ap()[l, ng].rearrange("p (k c) -> p k c", k=8), 8, 512, [t_wcs[l]])
                    for nb in range(4):
                        n = ng * 4 + nb
                        pb, tb = proj_fm(wv, tw, nb * 128, 128, 8, lambda k: MT[:, k, :], [t_MT])
                        P.op('act' if n % 2 else 'dve',
                             (lambda e, pb=pb, n=n: e.activation(out=RES[:, n, :], in_=pb[:], func=AF.Identity)) if n % 2 else
                             (lambda e, pb=pb, n=n: e.tensor_copy(out=RES[:, n, :], in_=pb[:])), r=[tb], w=[t_RES] if n == 0 else (), wp=() if n == 0 else [t_RES])
                postnorm_resid(xv, t_x, l, j, 2)
                prenorm(xv, t_x, l, j, 3, 4)
                UT = BIG[:].rearrange("p (k c) -> p k c", k=32)
                t_UT = T("UT")
                for g8 in range(8):
                    wv, tw = load_w(w1_b.ap()[l, g8].rearrange("p (k c) -> p k c", k=8), 8, 512, [t_wcs[l]])
                    for nb in range(4):
                        n = g8 * 4 + nb
                        pb, tb = proj_fm(wv, tw, nb * 128, 128, 8, lambda k: HT[:, k, :], [t_HT])
                        ti = n % 2
                        P.op('act', lambda e, pb=pb, ti=ti: e.activation(out=TMP[:, ti, :], in_=pb[:], func=AF.Relu), r=[tb], w=[t_TMP[ti]])
                        first = (n == 0)
                        P.op('pool', lambda e, n=n, ti=ti: e.tensor_tensor(out=UT[:, n, :], in0=TMP[:, ti, :], in1=TMP[:, ti, :], op=ALU.mult), r=[t_TMP[ti]],
                             w=[t_UT, t_BIG, t_OT, t_MT, t_WO] if first else (), wp=() if first else [t_UT])
                for n in range(8):
                    wv, tw = load_w(w2_b.ap()[l, n].rearrange("p (k c) -> p k c", k=32), 32, 128, [t_wcs[l]])
                    pb, tb = proj_fm(wv, tw, 0, 128, 32, lambda k: UT[:, k, :], [t_UT])
                    P.op('act' if n % 2 else 'dve',
                         (lambda e, pb=pb, n=n: e.activation(out=RES[:, n, :], in_=pb[:], func=AF.Identity)) if n % 2 else
                         (lambda e, pb=pb, n=n: e.tensor_copy(out=RES[:, n, :], in_=pb[:])), r=[tb], w=[t_RES] if n == 0 else (), wp=() if n == 0 else [t_RES])
                postnorm_resid(xv, t_x, l, j, 5)
                _merge(t_BIG.r, t_UT.r); _merge(t_BIG.r, t_UT.w)

            ccsems = []
            for l in range(L):
                for ci in range(4):
                    D1(l, 1, ci, xTs[:, :, ci * CH:(ci + 1) * CH], t_xTs[ci], ci * CH)
                ckpt('D1s%d' % l)
                def flat512(d):
                    a = d.ap()
                    if a.shape[1] == 1024:
                        return a.rearrange("r (a c) -> (r a) c", c=512)
                    return a.rearrange("r c -> (r c)").rearrange("(n c) -> n c", c=512)
                for (src, dst, tsrc, tdst) in ((kgi[0], kgo[0], t_k[1], t_kgo), (kgi[1], kgo[1], t_k[1], t_kgo), (vgi[0], vgo[0], t_v[1], t_vgo), (vgi[1], vgo[1], t_v[1], t_vgo)):
                    csem = es.enter_context(nc.semaphore("cc%d" % len(ccsems)))
                    ccsems.append(csem)
                    deps = P._deps([tsrc], [], [tdst])
                    P._mark(deps)
                    kid = ('cc', len(ccsems))
                    P.streams['pool'].append((deps, (lambda e, src=src, dst=dst: e.collective_compute(
                        "AllGather", ALU.bypass, replica_groups=[[0, 1], [2, 3], [4, 5], [6, 7]],
                        ins=[flat512(src)], outs=[flat512(dst)])), ('x', csem)))
                    P._update({kid: 1}, [tsrc], [], [tdst])
                    P.ccmap = getattr(P, 'ccmap', {})
                    P.ccmap[kid] = csem
                ckpt('cc%d' % l)
                if l + 1 < L:
                    mod_layer(l + 1)
                    mv_layer(l + 1)
                    cast_layer(l + 1)
                    swap_layer(l + 1)
                    ctx_layer(l + 1)
                for ci in range(2):
                    P.dma('sp', XC[:].rearrange("p k c -> p (k c)"), xp_scr.ap()[ci], r=[t_xps[ci]], w=[t_XC], key=t_XC)
                    D1(l, 0, ci, XC, t_XC, ci * CH)
                ckpt('D1p%d' % l)
                ATT_prompt(l)
                ckpt('ATTp%d' % l)
                for ci in range(2):
                    P.dma('sp', XC[:].rearrange("p k c -> p (k c)"), xp_scr.ap()[ci], r=[t_xps[ci]], w=[t_XC], key=t_XC)
                    D2(l, 0, ci, XC, t_XC, ci * CH)
                    P.dma('pool', xp_scr.ap()[ci], XC[:].rearrange("p k c -> p (k c)"), r=[t_XC], w=[t_xps[ci]], key=t_XC)
                ckpt('D2p%d' % l)
                ATT_sample(l)
                ckpt('ATTs%d' % l)
                for ci in range(4):
                    D2(l, 1, ci, xTs[:, :, ci * CH:(ci + 1) * CH], t_xTs[ci], ci * CH)

                ckpt('D2s%d' % l)
            for grp, yout, ntile in ((0, y_p, 8), (1, y_s, 16)):
                for tt in range(ntile):
                    c = tt // 4
                    if grp == 0 and tt % 4 == 0:
                        P.dma('sp', XC[:].rearrange("p k c -> p (k c)"), xp_scr.ap()[c], r=[t_xps[c]], w=[t_XC], key=t_XC)
                    stg = TMP[:, (tt % 2) * 2:(tt % 2) * 2 + 2].rearrange("p a c -> p (a c)")
                    tstg = t_TMP[(tt % 2) * 2]
                    tstg2 = t_TMP[(tt % 2) * 2 + 1]
                    for half in range(2):
                        pb, tb = bank()

                        def tr(e, pb=pb, half=half, tt=tt, grp=grp):
                            inst = None
                            for q in range(4):
                                k = half * 4 + q
                                src = XC[:, k, (tt % 4) * 128:(tt % 4 + 1) * 128] if grp == 0 else xTs[:, k, tt * 128:(tt + 1) * 128]
                                inst = e.transpose(pb[:, q * 128:(q + 1) * 128], src, identF[:])
                            return inst
                        P.op('pe', tr, r=[t_XC if grp == 0 else t_xTs[c], t_const], w=[tb])
                        P.op('act' if half else 'dve',
                             (lambda e, pb=pb, stg=stg, half=half: e.activation(out=stg[:, half * 512:(half + 1) * 512], in_=pb[:], func=AF.Identity)) if half else
                             (lambda e, pb=pb, stg=stg, half=half: e.tensor_copy(out=stg[:, half * 512:(half + 1) * 512], in_=pb[:])),
                             r=[tb], w=[tstg, tstg2] if half == 0 else (), wp=() if half == 0 else [tstg])
                    P.dma('pool', yout[tt * 128:(tt + 1) * 128, :], stg, r=[tstg, tstg2], wp=[t_outs], key=tstg)


        except _Stop:
            pass
        block = es.enter_context(nc.Block())
        _finalize_with_cc(P, es, block)
    return nc


def _finalize_with_cc(P, es, block):
    nc = P.nc
    sems = {}
    for e in P.ENG:
        sems[e] = es.enter_context(nc.semaphore("s_" + e))
    for i, k in enumerate(P.keys):
        sems[k] = es.enter_context(nc.semaphore("d%d" % i))
    for kid, s in getattr(P, 'ccmap', {}).items():
        sems[kid] = s
    cmap = {}
    for e in P.ENG:
        m = {}
        c = 0
        nd = P.needed[e]
        for i in range(1, P.nops[e] + 1):
            if i in nd:
                c += 1
            m[i] = c
        cmap[e] = m
    final = dict(P.dmacnt)

    def emit(ename, eng):
        waited = {}
        for deps, fn, me in P.streams[ename]:
            for k, v in deps.items():
                if k == 'pe' and ename == 'pe':
                    continue
                cnt = cmap[k][v] if isinstance(k, str) else v
                if cnt > 0 and waited.get(k, 0) < cnt:
                    eng.wait_ge(sems[k], cnt)
                    waited[k] = cnt
            inst = fn(eng)
            if me[0] == 'c':
                if me[2] in P.needed[me[1]]:
                    inst.then_inc(sems[me[1]], 1)
            elif me[0] == 'd':
                inst.then_inc(sems[me[1]], 16)
            else:
                inst.then_inc(me[1])
        if ename == 'sp':
            for k, v in final.items():
                eng.wait_ge(sems[k], v)

    block.sync(lambda e: emit('sp', e))
    block.gpsimd(lambda e: emit('pool', e))
    block.vector(lambda e: emit('dve', e))
    block.scalar(lambda e: emit('act', e))
    block.tensor(lambda e: emit('pe', e))


def _consts(core):
    half = core % 2
    ident = np.eye(128, dtype=np.float32)
    ones = np.ones((128, 128), np.float32)
    bones = np.zeros((128, 128), np.float32)
    bones[0:64, 0:64] = 1.0
    bones[64:128, 64:128] = 1.0
    perm = np.zeros((128, 128), np.float32)
    for m in range(128):
        hh, d = divmod(m, 64)
        a, r = divmod(d, 32)
        b, i = divmod(r, 16)
        k = hh * 64 + a * 32 + (1 - b) * 16 + i
        perm[k, m] = 1.0
    t = np.arange(2048) + half * 2048
    row = (t // 64).astype(np.float64)
    col = (t % 64).astype(np.float64)

    def tables(hd):
        q = hd // 4
        inv = (10000.0 ** (-np.arange(q, dtype=np.float32) / np.float32(q))).astype(np.float32)
        C = np.zeros((hd, 2048), np.float32)
        S = np.zeros((hd, 2048), np.float32)
        for a, pos in enumerate((row, col)):
            ang = (pos.astype(np.float32)[None, :] * inv[:, None]).astype(np.float32)
            c = np.cos(ang).astype(np.float32)
            s = np.sin(ang).astype(np.float32)
            base = a * 2 * q
            C[base:base + q] = c
            C[base + q:base + 2 * q] = c
            S[base:base + q] = -s
            S[base + q:base + 2 * q] = s
        return C, S
    C64, S64 = tables(64)
    ropeA = np.stack([np.concatenate([C64, C64], 0), np.concatenate([S64, S64], 0)], 0)
    C32, S32 = tables(32)
    CB = np.concatenate([np.ones((64, 2048), np.float32), C32], 0)
    SB = np.concatenate([np.zeros((64, 2048), np.float32), S32], 0)
    ropeB = np.stack([CB, SB], 0)
    kk = np.arange(128)[:, None]
    qq = np.arange(128)[None, :]
    m0 = (kk >= qq).astype(np.float32)
    m1 = np.ones((128, 128), np.float32)
    m2 = (kk <= qq).astype(np.float32)
    mid = np.concatenate([m0, m1, m2], 1)
    first = np.concatenate([m0 * (1.0 if half == 1 else 0.0), m1, m2], 1)
    last = np.concatenate([m0, m1, m2 * (1.0 if half == 0 else 0.0)], 1)
    mask = np.stack([first, mid, last], 0)
    return dict(c_ident=ident, c_ones=ones, c_bones=bones, c_perm=perm, c_ropeA=np.ascontiguousarray(ropeA),
                c_ropeB=np.ascontiguousarray(ropeB), c_mask=np.ascontiguousarray(mask))


_NC_CACHE = {}


def kernel(**inp):
    f = lambda a: np.ascontiguousarray(np.asarray(a, dtype=np.float32))
    if 'nc' not in _NC_CACHE:
        _NC_CACHE['nc'] = build_program()
    nc = _NC_CACHE['nc']
    wnames = ['w_mod', 'w_in', 'w_qup', 'w_kvup', 'w_oa', 'w_ob', 'w_oc', 'w_out', 'w_mlp1', 'w_mlp2']
    shared = {k: f(inp[k]) for k in wnames}
    shared['h_bmodT'] = f(inp['b_mod']).reshape(L, 48, 128).transpose(2, 0, 1)
    shared['h_gvecs'] = np.stack([f(inp[k]).reshape(L, 8, 128).transpose(2, 0, 1) for k in ('g_pre_mix', 'g_post_mix', 'g_pre_mlp', 'g_post_mlp')], 1).reshape(128, 4 * L * 8)
    shared['h_gqaT'] = f(inp['g_qa']).reshape(L, 3, 128).transpose(2, 0, 1)
    shared['h_gkvaT'] = f(inp['g_kva']).reshape(L, 2, 128).transpose(2, 0, 1)
    shared['h_gqc2'] = np.concatenate([f(inp['g_qc']).T, f(inp['g_qc']).T], 0)
    shared['h_gkc2'] = np.concatenate([f(inp['g_kc']).T, f(inp['g_kc']).T], 0)
    shared['h_sinkb'] = np.broadcast_to(f(inp['a_sink'])[None], (128, L, 8))
    x_prompt = f(inp['x_prompt']); x_sample = f(inp['x_sample'])
    in_maps = []
    for c in range(8):
        b, half = c // 2, c % 2
        m = dict(shared)
        m['xp'] = x_prompt[4 * c:4 * c + 4].reshape(1024, D)
        m['xs'] = x_sample[b, half * 2048:(half + 1) * 2048]
        m['ca_k'] = f(inp['cache_a_k'])[b].reshape(L, 512, 128)
        m['ca_v'] = f(inp['cache_a_v'])[b].reshape(L, 512, 128)
        m['cb_ckv'] = f(inp['cache_b_ckv'])[b]
        m['cb_kr'] = f(inp['cache_b_krope'])[b]
        m['cc_k'] = f(inp['cache_c_k'])[b].reshape(L, 512, 128)
        m['cc_v'] = f(inp['cache_c_v'])[b].reshape(L, 512, 128)
        m['cvecT'] = np.stack([f(inp['c_ctx']), f(inp['c'])[b]], 0).reshape(2, 8, 128).transpose(2, 1, 0)
        m.update(_consts(c))
        m = {k: np.ascontiguousarray(v) for k, v in m.items()}
        in_maps.append(m)
    res = run_bass_kernel_spmd(nc, in_maps, core_ids=list(range(8)))
    r = res.results
    y_prompt = np.concatenate([r[c]['y_p'].reshape(4, 256, D) for c in range(8)], 0)
    y_sample = np.stack([np.concatenate([r[2 * b]['y_s'], r[2 * b + 1]['y_s']], 0) for b in range(4)], 0)

    def cat(name, shape):
        return np.concatenate([r[c][name].reshape((4, L, 256) + shape) for c in range(8)], 0)
    return (y_prompt.astype(np.float32), y_sample.astype(np.float32),
            cat('o_ak', (2, 64)), cat('o_av', (2, 64)), cat('o_ckv', (256,)), cat('o_kr', (32,)),
            cat('o_ck', (2, 64)), cat('o_cv', (2, 64)))
```

```python
import os
import numpy as np
from contextlib import ExitStack
import concourse.bass as bass
import concourse.mybir as mybir
from concourse.bass_utils import run_bass_kernel_spmd

F32 = mybir.dt.float32
BF16 = mybir.dt.bfloat16
AF = mybir.ActivationFunctionType
ALU = mybir.AluOpType

L = 4
D = 1024
NCOL = 5280
EPS = 1e-6
CH = 512
SC_A = 64 ** -0.5
SC_B = 96 ** -0.5
O_QA, O_KA, O_VA, O_QBD, O_KVBD, O_KBR, O_QC, O_KC, O_VC, O_G = 0, 512, 640, 768, 1152, 1408, 1440, 1952, 2080, 2208


class _Stop(Exception):
    pass


def ckpt(name):
    if os.environ.get('KSTOP') == name:
        raise _Stop()


class T:
    def __init__(self, name):
        self.name = name
        self.w = {}
        self.r = {}


def _merge(d, s):
    for k, v in s.items():
        if d.get(k, 0) < v:
            d[k] = v


class Prog:
    ENG = ('pe', 'act', 'dve', 'pool', 'sp')

    def __init__(self, nc):
        self.nc = nc
        self.streams = {e: [] for e in self.ENG}
        self.nops = {e: 0 for e in self.ENG}
        self.needed = {e: set() for e in self.ENG}
        self.dmacnt = {}
        self.keys = []

    def _deps(self, r, w, wp):
        deps = {}
        for t in r:
            _merge(deps, t.w)
        for t in w:
            _merge(deps, t.w)
            _merge(deps, t.r)
        for t in wp:
            _merge(deps, t.w)
            _merge(deps, t.r)
        return deps

    def _mark(self, deps):
        for k, v in deps.items():
            if isinstance(k, str):
                self.needed[k].add(v)

    def _update(self, ev, r, w, wp):
        for t in r:
            _merge(t.r, ev)
        for t in w:
            t.w = dict(ev)
            t.r = {}
        for t in wp:
            _merge(t.w, ev)

    def op(self, eng, fn, r=(), w=(), wp=()):
        deps = self._deps(r, w, wp)
        self._mark(deps)
        self.nops[eng] += 1
        idx = self.nops[eng]
        self.streams[eng].append((deps, fn, ('c', eng, idx)))
        self._update({eng: idx}, r, w, wp)

    def dma(self, q, out, in_, r=(), w=(), wp=(), key=None, slow=False):
        assert key is not None
        deps = self._deps(r, w, wp)
        self._mark(deps)
        kid = ('dma', id(key), q)
        if kid not in self.dmacnt:
            self.dmacnt[kid] = 0
            self.keys.append(kid)
        self.dmacnt[kid] += 16
        cnt = self.dmacnt[kid]

        def fn(e, out=out, in_=in_, slow=slow):
            if slow:
                return e.dma_start(out=out, in_=in_, allow_slow_non_contiguous=True)
            return e.dma_start(out=out, in_=in_)
        self.streams[q].append((deps, fn, ('d', kid, cnt)))
        self._update({kid: cnt}, r, w, wp)

    def finalize(self, es, block):
        nc = self.nc
        sems = {}
        for e in self.ENG:
            sems[e] = es.enter_context(nc.semaphore("s_" + e))
        for i, k in enumerate(self.keys):
            sems[k] = es.enter_context(nc.semaphore("d%d" % i))
        cmap = {}
        for e in self.ENG:
            m = {}
            c = 0
            nd = self.needed[e]
            for i in range(1, self.nops[e] + 1):
                if i in nd:
                    c += 1
                m[i] = c
            cmap[e] = m
        final = dict(self.dmacnt)

        def emit(ename, eng):
            waited = {}
            for deps, fn, me in self.streams[ename]:
                for k, v in deps.items():
                    if isinstance(k, str):
                        if k == ename and ename == 'pe':
                            pass
                        cnt = cmap[k][v]
                    else:
                        cnt = v
                    if cnt > 0 and waited.get(k, 0) < cnt:
                        eng.wait_ge(sems[k], cnt)
                        waited[k] = cnt
                inst = fn(eng)
                if me[0] == 'c':
                    if me[2] in self.needed[me[1]]:
                        inst.then_inc(sems[me[1]], 1)
                else:
                    inst.then_inc(sems[me[1]], 16)
            if ename == 'sp':
                for k, v in final.items():
                    eng.wait_ge(sems[k], v)

        block.sync(lambda e: emit('sp', e))
        block.gpsimd(lambda e: emit('pool', e))
        block.vector(lambda e: emit('dve', e))
        block.scalar(lambda e: emit('act', e))
        block.tensor(lambda e: emit('pe', e))


def build_program():
    nc = bass.Bass("TRN2", target_bir_lowering=False)
    P = Prog(nc)

    def din(name, shape, dt=F32):
        return nc.dram_tensor(name, list(shape), dt, kind="ExternalInput").ap()

    def dout(name, shape):
        return nc.dram_tensor(name, list(shape), F32, kind="ExternalOutput").ap()

    def dscr(name, shape, dt=BF16):
        return nc.dram_tensor(name, list(shape), dt)

    xp_in = din("xp", [1024, D])
    xs_in = din("xs", [2048, D])
    ca_k = din("ca_k", [L, 512, 128]); ca_v = din("ca_v", [L, 512, 128])
    cb_ckv = din("cb_ckv", [L, 512, 256]); cb_kr = din("cb_kr", [L, 512, 32])
    cc_k = din("cc_k", [L, 512, 128]); cc_v = din("cc_v", [L, 512, 128])
    cvecT = din("cvecT", [128, 8, 2])
    h_bmodT = din("h_bmodT", [128, L, 48]); h_gvecs = din("h_gvecs", [128, 4 * L * 8])
    h_gqaT = din("h_gqaT", [128, L, 3]); h_gkvaT = din("h_gkvaT", [128, L, 2])
    h_gqc2 = din("h_gqc2", [128, L]); h_gkc2 = din("h_gkc2", [128, L]); h_sinkb = din("h_sinkb", [128, L, 8])
    w_mod = din("w_mod", [L, D, 6 * D])
    w_in = din("w_in", [L, D, NCOL])
    w_qup = din("w_qup", [L, 384, 768])
    w_kvup = din("w_kvup", [L, 256, 1024])
    w_oa = din("w_oa", [L, 512, D]); w_ob = din("w_ob", [L, 512, D]); w_oc = din("w_oc", [L, 512, D])
    w_out = din("w_out", [L, D, D]); w_mlp1 = din("w_mlp1", [L, D, 4 * D]); w_mlp2 = din("w_mlp2", [L, 4 * D, D])
    c_ident = din("c_ident", [128, 128]); c_ones = din("c_ones", [128, 128]); c_bones = din("c_bones", [128, 128])
    c_perm = din("c_perm", [128, 128])
    c_ropeA = din("c_ropeA", [2, 128, 2048]); c_ropeB = din("c_ropeB", [2, 96, 2048])
    c_mask = din("c_mask", [3, 128, 384])

    y_p = dout("y_p", [1024, D]); y_s = dout("y_s", [2048, D])
    o_ak = dout("o_ak", [4, L, 256, 128]); o_av = dout("o_av", [4, L, 256, 128])
    o_ckv = dout("o_ckv", [4, L, 256, 256]); o_kr = dout("o_kr", [4, L, 256, 32])
    o_ck = dout("o_ck", [4, L, 256, 128]); o_cv = dout("o_cv", [4, L, 256, 128])

    win_b = dscr("win_b", [L, D, NCOL])
    wsw_b = dscr("wsw_b", [L, D, 1408])
    wqup_b = dscr("wqup_b", [L, 384, 768]); wqupsw_b = dscr("wqupsw_b", [L, 384, 768])
    wkvup_b = dscr("wkvup_b", [L, 256, 1024])
    wg_b = dscr("wg_b", [L, 8, 128, 8 * 384])
    wo_b = dscr("wo_b", [L, 2, 128, 12 * 512])
    wout_b = dscr("wout_b", [L, 2, 128, 8 * 512])
    w1_b = dscr("w1_b", [L, 8, 128, 8 * 512])
    w2_b = dscr("w2_b", [L, 8, 128, 32 * 128])
    xp_scr = dscr("xp_scr", [2, 128, 8 * CH], F32)
    h_scr = dscr("h_scr", [6, 128, 8 * CH])
    q_p = dscr("q_p", [1792, 1024]); k_p = dscr("k_p", [1024, 1024]); v_p = dscr("v_p", [1024, 768]); o_p = dscr("o_p", [1536, 1024])
    q_s = dscr("q_s", [1792, 2048]); o_s = dscr("o_s", [1536, 2048])
    kgi = [dscr("kg_in%d" % t, [1024, 1024]) for t in range(2)]; kgo = [dscr("kg_out%d" % t, [2048, 1024]) for t in range(2)]
    vgi = [dscr("vg_in%d" % t, [1024, 768]) for t in range(2)]; vgo = [dscr("vg_out%d" % t, [2048, 768]) for t in range(2)]
    kctx = dscr("kctx", [L, 1024, 512]); vctx = dscr("vctx", [L, 512, 768])

    es = ExitStack()
    with es:
        def sb(name, shape, dt=F32):
            return es.enter_context(nc.sbuf_tensor(name, list(shape), dt))

        xTs = sb("xTs", [128, 8, 2048]); t_xTs = [T("xTs%d" % i) for i in range(4)]
        XC = sb("XC", [128, 8, CH]); t_XC = T("XC")
        identF = sb("identF", [128, 128]); onesB = sb("onesB", [128, 128], BF16); bonesB = sb("bonesB", [128, 128], BF16)
        permF = sb("permF", [128, 128])
        t_const = T("const")
        modraw = sb("modraw", [128, L, 48, 2]); t_modraw = T("modraw")
        bmodT = sb("bmodT", [128, L, 48]); gvecs = sb("gvecs", [128, 4, L, 8])
        MV = sb("MV", [128, 6, L, 2, 8]); t_MV = T("MV")
        gqaT = sb("gqaT", [128, L, 3]); gkvaT = sb("gkvaT", [128, L, 2])
        gq4 = sb("gq4", [128, 4, L]); t_gq4 = T("gq4")
        sinkE = sb("sinkE", [128, L, 8])
        maskB = sb("maskB", [128, 3, 384], BF16)
        scT = sb("scT", [128, 8, 2])
        ARENA = sb("ARENA", [128, 24576], BF16)
        BIG = sb("BIG", [128, 16384], BF16)
        TMP = sb("TMP", [128, 6, CH]); t_TMP = [T("TMP%d" % i) for i in range(6)]
        TMPB = sb("TMPB", [128, 4, CH], BF16); t_TMPB = [T("TMPB%d" % i) for i in range(4)]
        ROPE = sb("ROPE", [128, 4, CH]); t_ROPE = T("ROPE")
        psb = [es.enter_context(nc.psum_tensor("ps%d" % i, [128, CH], F32)) for i in range(8)]
        t_ps = [T("ps%d" % i) for i in range(8)]
        bank_ctr = [0]

        def bank():
            i = bank_ctr[0] % 6
            bank_ctr[0] += 1
            return psb[i], t_ps[i]
        obank_ctr = [0]

        def obank():
            i = 6 + obank_ctr[0] % 2
            obank_ctr[0] += 1
            return psb[i], t_ps[i]

        WS = [ARENA[:, i * 4096:(i + 1) * 4096] for i in range(3)]
        t_WS = [T("WS%d" % i) for i in range(3)]
        HTflat = ARENA[:, 12288:16384]
        HT = HTflat.rearrange("p (k c) -> p k c", k=8); t_HT = T("HT")
        RES = ARENA[:, 16384:24576].bitcast(F32).rearrange("p (k c) -> p k c", k=8); t_RES = T("RES")
        ws_ctr = [0]

        def wslot():
            i = ws_ctr[0] % 3
            ws_ctr[0] += 1
            return WS[i], t_WS[i]
        KT = [ARENA[:, i * 4608:(i + 1) * 4608] for i in range(2)]
        VT = [ARENA[:, 9216 + i * 4608: 9216 + (i + 1) * 4608].rearrange("p (t c) -> p t c", c=128) for i in range(2)]
        QT = [ARENA[:, 18432 + i * 2048: 18432 + (i + 1) * 2048] for i in range(2)]
        OH = [ARENA[:, 22528 + i * 1024: 22528 + (i + 1) * 1024] for i in range(2)]
        arena_all = t_WS + [t_HT, t_RES]
        t_BIG = T("BIG")

        dma_in = T("dma_in")

        try:
            kc = T("kconst")
            P.dma('sp', identF[:], c_ident, w=[t_const], key=kc)
            P.dma('sp', permF[:], c_perm, wp=[t_const], key=kc)
            P.dma('pool', onesB[:], c_ones, wp=[t_const], key=kc)
            P.dma('pool', bonesB[:], c_bones, wp=[t_const], key=kc)
            P.dma('pool', maskB[:], c_mask.rearrange("v p c -> p v c"), wp=[t_const], key=kc)
            t_wcs = [T("wcast%d" % i) for i in range(L)]
            kw = T("kwcast")

            cast_l = [0]

            def cast(dst, src):
                P.dma('pool', dst, src, wp=[t_wcs[cast_l[0]]], key=kw)

            def cast_layer(l):
                cast_l[0] = l
                for rb in range(8):
                    rs = slice(rb * 128, (rb + 1) * 128)
                    for (o, ) in ((O_QA,), (O_QC,)):
                        for a in range(2):
                            cast(win_b.ap()[l, rs, o:o + 512].rearrange("r (j a d) -> r j a d", j=4, a=2)[:, :, a, :],
                                 w_in[l, rs, o:o + 512].rearrange("r (a j d) -> r a j d", a=2, j=4)[:, a])
                    cast(win_b.ap()[l, rs, O_KA:O_QC], w_in[l, rs, O_KA:O_QC])
                    cast(win_b.ap()[l, rs, O_KC:O_G], w_in[l, rs, O_KC:O_G])
                    for b in range(3):
                        cast(wg_b.ap()[l, :, :, rb * 384 + b * 128:rb * 384 + (b + 1) * 128].rearrange("n p c -> p n c"),
                             w_in[l, rs, O_G + b * 1024:O_G + (b + 1) * 1024].rearrange("r (n c) -> r n c", n=8))
                    cast(wout_b.ap()[l, :, :, rb * 512:(rb + 1) * 512].rearrange("g p c -> p g c"), w_out[l, rs, :].rearrange("p (g c) -> p g c", g=2))
                    cast(w1_b.ap()[l, :, :, rb * 512:(rb + 1) * 512].rearrange("g p c -> p g c"), w_mlp1[l, rs, :].rearrange("p (g c) -> p g c", g=8))
                for rb in range(3):
                    rs = slice(rb * 128, (rb + 1) * 128)
                    cast(wqup_b.ap()[l, rs, :], w_qup[l, rs, :])
                for rb in range(2):
                    rs = slice(rb * 128, (rb + 1) * 128)
                    for t in range(2):
                        cast(wkvup_b.ap()[l, rs, t * 512:(t + 1) * 512].rearrange("r (h d) -> r h d", h=8),
                             w_kvup[l, rs, :].rearrange("r (h t d) -> r h t d", h=8, t=2)[:, :, t, :])
                for bi, wsrc in enumerate((w_oa, w_ob, w_oc)):
                    for rb in range(4):
                        k12 = bi * 4 + rb
                        cast(wo_b.ap()[l, :, :, k12 * 512:(k12 + 1) * 512].rearrange("g p c -> p g c"), wsrc[l, rb * 128:(rb + 1) * 128, :].rearrange("p (g c) -> p g c", g=2))
                for k in range(32):
                    cast(w2_b.ap()[l, :, :, k * 128:(k + 1) * 128].rearrange("n p c -> p n c"),
                         w_mlp2[l, k * 128:(k + 1) * 128, :].rearrange("p (n c) -> p n c", n=8))
                cast(vctx.ap()[l, :, 0:128], ca_v[l])
                cast(vctx.ap()[l, :, 128:256], cc_v[l])

            cast_layer(0)
            ckpt('cast')
            t_wsws = [T("wsw%d" % i) for i in range(L)]
            ksw = T("ksw")
            def swap_layer(l):
                ws, tw = wslot()
                ws2, tw2 = wslot()
                for (so, wd, dst) in ((O_QA, 512, 0), (O_KA, 128, 512), (O_QC, 512, 640), (O_KC, 128, 1152), (O_KBR - 64, 96, 1280)):
                    src_v = ws[:, 0:8 * wd].rearrange("p (k c) -> p k c", k=8)
                    dst_v = ws2[:, 0:8 * wd].rearrange("p (k c) -> p k c", k=8)
                    P.dma('sp', src_v, win_b.ap()[l, :, so:so + wd].rearrange("(k p) c -> p k c", p=128), r=[t_wcs[l]], w=[tw], key=tw)
                    if wd == 96:
                        P.op('dve', lambda e, d=dst_v, s=src_v: e.tensor_copy(out=d[:, :, 0:64], in_=s[:, :, 0:64]), r=[tw], w=[tw2])
                        for b in range(2):
                            sv = src_v[:, :, 64:96].rearrange("p k (a b i) -> p k a b i", a=2, b=2)[:, :, :, 1 - b, :]
                            dv = dst_v[:, :, 64:96].rearrange("p k (a b i) -> p k a b i", a=2, b=2)[:, :, :, b, :]
                            P.op('dve', lambda e, d=dv, s=sv: e.tensor_copy(out=d, in_=s), r=[tw], wp=[tw2])
                    else:
                        for b in range(2):
                            sv = src_v.rearrange("p k (h a b i) -> p k h a b i", a=2, b=2, i=16)[:, :, :, :, 1 - b, :]
                            dv = dst_v.rearrange("p k (h a b i) -> p k h a b i", a=2, b=2, i=16)[:, :, :, :, b, :]
                            for k in range(8):
                                P.op('dve', lambda e, d=dv[:, k], s=sv[:, k]: e.tensor_copy(out=d, in_=s), r=[tw], wp=[tw2] if (b or k) else (), w=() if (b or k) else [tw2])
                    P.dma('pool', wsw_b.ap()[l, :, dst:dst + wd].rearrange("(k p) c -> p k c", p=128), dst_v, r=[tw2], wp=[t_wsws[l]], key=ksw)
                src_v = ws[:, 0:3 * 768].rearrange("p (k c) -> p k c", k=3)
                dst_v = ws2[:, 0:3 * 768].rearrange("p (k c) -> p k c", k=3)
                P.dma('sp', src_v, wqup_b.ap()[l].rearrange("(k p) c -> p k c", p=128), r=[t_wcs[l]], w=[tw], key=tw)
                P.op('dve', lambda e, d=dst_v, s=src_v: e.tensor_copy(out=d, in_=s), r=[tw], w=[tw2])
                for b in range(2):
                    for k in range(3):
                        sv = src_v[:, k].rearrange("p (h c) -> p h c", h=8)[:, :, 64:96].rearrange("p h (a b i) -> p h a b i", a=2, b=2)[:, :, :, 1 - b, :]
                        dv = dst_v[:, k].rearrange("p (h c) -> p h c", h=8)[:, :, 64:96].rearrange("p h (a b i) -> p h a b i", a=2, b=2)[:, :, :, b, :]
                        P.op('dve', lambda e, d=dv, s=sv: e.tensor_copy(out=d, in_=s), r=[tw], wp=[tw2])
                P.dma('pool', wqupsw_b.ap()[l].rearrange("(k p) c -> p k c", p=128), dst_v, r=[tw2], wp=[t_wsws[l]], key=ksw)

            swap_layer(0)
            ckpt('swap')
            kv = T("kvec")
            t_vec = T("vec")
            P.dma('sp', scT[:], cvecT, w=[t_vec], key=kv)
            P.dma('sp', bmodT[:], h_bmodT, wp=[t_vec], key=kv)
            P.dma('sp', gvecs[:].rearrange("p a l k -> p (a l k)"), h_gvecs, wp=[t_vec], key=kv)
            P.dma('sp', gqaT[:], h_gqaT, wp=[t_vec], key=kv)
            P.dma('sp', gkvaT[:], h_gkvaT, wp=[t_vec], key=kv)
            P.dma('sp', gq4[:, 0, :], h_gqc2, wp=[t_vec], key=kv)
            P.dma('sp', gq4[:, 2, :], h_gkc2, wp=[t_vec], key=kv)
            P.dma('sp', sinkE[:], h_sinkb, wp=[t_vec], key=kv)
            P.op('act', lambda e: e.activation(out=scT[:], in_=scT[:], func=AF.Silu), r=[t_vec], wp=[t_vec])
            P.op('act', lambda e: e.activation(out=sinkE[:], in_=sinkE[:], func=AF.Exp), r=[t_vec], wp=[t_vec])
            pb, tb = bank()
            P.op('pe', lambda e, pb=pb: e.matmul(pb[:, 0:L], lhsT=permF[:], rhs=gq4[:, 0, :], start=True, stop=True), r=[t_vec, t_const], w=[tb])
            P.op('dve', lambda e, pb=pb: e.tensor_copy(out=gq4[:, 1, :], in_=pb[:, 0:L]), r=[tb], wp=[t_gq4])
            pb, tb = bank()
            P.op('pe', lambda e, pb=pb: e.matmul(pb[:, 0:L], lhsT=permF[:], rhs=gq4[:, 2, :], start=True, stop=True), r=[t_vec, t_const], w=[tb])
            P.op('dve', lambda e, pb=pb: e.tensor_copy(out=gq4[:, 3, :], in_=pb[:, 0:L]), r=[tb], wp=[t_gq4])

            ckpt('vec')
            def mod_layer(l):
                pb, tb = bank()
                for g in range(12):
                    ws, tw = wslot()
                    ws2, tw2 = wslot()
                    wv = [ws.bitcast(F32).rearrange("p (k c) -> p k c", k=4), ws2.bitcast(F32).rearrange("p (k c) -> p k c", k=4)]
                    P.dma('sp', wv[0], w_mod[l, 0:512, g * 512:(g + 1) * 512].rearrange("(k p) c -> p k c", p=128), w=[tw], key=tw)
                    P.dma('sp', wv[1], w_mod[l, 512:1024, g * 512:(g + 1) * 512].rearrange("(k p) c -> p k c", p=128), w=[tw2], key=tw2)

                    def mm(e, pb=pb, wv=wv, g=g):
                        inst = None
                        for nb in range(4):
                            n = g * 4 + nb
                            for k in range(8):
                                inst = e.matmul(pb[:, n * 2:n * 2 + 2], lhsT=wv[k // 4][:, k % 4, nb * 128:(nb + 1) * 128], rhs=scT[:, k, :],
                                                start=(k == 0), stop=(k == 7))
                        return inst
                    P.op('pe', mm, r=[tw, tw2, t_vec], wp=[tb] if g else (), w=() if g else [tb])
                for j in range(2):
                    P.op('dve', lambda e, pb=pb, l=l, j=j: e.tensor_tensor(out=modraw[:, l, :, j], in0=pb[:, 0:96].rearrange("p (n j) -> p n j", j=2)[:, :, j],
                                                                          in1=bmodT[:, l, :], op=ALU.add), r=[tb, t_vec], wp=[t_modraw])
            mod_layer(0)
            def mv_layer(l):
                for j in range(2):
                    mr = lambda i, l=l, j=j: modraw[:, l, i * 8:(i + 1) * 8, j]
                    P.op('dve', lambda e, l=l, j=j, mr=mr: e.scalar_tensor_tensor(out=MV[:, 0, l, j, :], in0=mr(1), scalar=1.0, in1=gvecs[:, 0, l, :], op0=ALU.add, op1=ALU.mult), r=[t_modraw, t_vec], wp=[t_MV])
                    P.op('dve', lambda e, l=l, j=j, mr=mr: e.tensor_copy(out=MV[:, 1, l, j, :], in_=mr(0)), r=[t_modraw], wp=[t_MV])
                    P.op('dve', lambda e, l=l, j=j, mr=mr: e.tensor_tensor(out=MV[:, 2, l, j, :], in0=mr(2), in1=gvecs[:, 1, l, :], op=ALU.mult), r=[t_modraw, t_vec], wp=[t_MV])
                    P.op('dve', lambda e, l=l, j=j, mr=mr: e.scalar_tensor_tensor(out=MV[:, 3, l, j, :], in0=mr(4), scalar=1.0, in1=gvecs[:, 2, l, :], op0=ALU.add, op1=ALU.mult), r=[t_modraw, t_vec], wp=[t_MV])
                    P.op('dve', lambda e, l=l, j=j, mr=mr: e.tensor_copy(out=MV[:, 4, l, j, :], in_=mr(3)), r=[t_modraw], wp=[t_MV])
                    P.op('dve', lambda e, l=l, j=j, mr=mr: e.tensor_tensor(out=MV[:, 5, l, j, :], in0=mr(5), in1=gvecs[:, 3, l, :], op=ALU.mult), r=[t_modraw, t_vec], wp=[t_MV])
            mv_layer(0)
            t_par = [t_MV, t_vec, t_gq4, t_const]

            ckpt('mod')
            t_xps = [T("xps%d" % i) for i in range(2)]
            kxs = T("kxs")
            for grp, xin, ntile in ((1, xs_in, 16), (0, xp_in, 8)):
                for tt in range(ntile):
                    c = tt // 4
                    xt = TMP[:, 0:4].rearrange("p a c -> p (a c)") if tt % 2 == 0 else TMP[:, 4:6].rearrange("p a c -> p (a c)")
                    xt = TMP[:, (tt % 2) * 2:(tt % 2) * 2 + 2].rearrange("p a c -> p (a c)")
                    tx = t_TMP[(tt % 2) * 2]
                    tx2 = t_TMP[(tt % 2) * 2 + 1]
                    P.dma('sp', xt, xin[tt * 128:(tt + 1) * 128, :], w=[tx, tx2], key=tx)
                    for half in range(2):
                        pb, tb = bank()

                        def tr(e, pb=pb, xt=xt, half=half):
                            inst = None
                            for q in range(4):
                                k = half * 4 + q
                                inst = e.transpose(pb[:, q * 128:(q + 1) * 128], xt[:, k * 128:(k + 1) * 128], identF[:])
                            return inst
                        P.op('pe', tr, r=[tx, tx2, t_const], w=[tb])
                        if grp == 1:
                            dstv = xTs[:, half * 4:(half + 1) * 4, tt * 128:(tt + 1) * 128]
                            P.op('act' if half else 'dve',
                                 (lambda e, d=dstv, pb=pb: e.activation(out=d, in_=pb[:].rearrange("p (q c) -> p q c", q=4), func=AF.Identity)) if half else
                                 (lambda e, d=dstv, pb=pb: e.tensor_copy(out=d, in_=pb[:].rearrange("p (q c) -> p q c", q=4))),
                                 r=[tb], wp=[t_xTs[c]])
                        else:
                            dstv = XC[:, half * 4:(half + 1) * 4, (tt % 4) * 128:(tt % 4 + 1) * 128]
                            P.op('act' if half else 'dve',
                                 (lambda e, d=dstv, pb=pb: e.activation(out=d, in_=pb[:].rearrange("p (q c) -> p q c", q=4), func=AF.Identity)) if half else
                                 (lambda e, d=dstv, pb=pb: e.tensor_copy(out=d, in_=pb[:].rearrange("p (q c) -> p q c", q=4))),
                                 r=[tb], wp=[t_XC])
                    if grp == 0 and tt % 4 == 3:
                        P.dma('pool', xp_scr.ap()[c], XC[:].rearrange("p k c -> p (k c)"), r=[t_XC], w=[t_xps[c]], key=t_XC)

            ckpt('xT')
            t_kctxs = [T("kctx%d" % i) for i in range(L)]
            kck = T("kck")
            def ctx_layer(l):
                ws, tw = wslot()
                wkv = ws[:, 0:2048].rearrange("p (k c) -> p k c", k=2)
                P.dma('sp', wkv, wkvup_b.ap()[l].rearrange("(k p) c -> p k c", p=128), r=[t_wcs[l]], w=[tw], key=tw)
                ckf = TMP[:, 0:2].rearrange("p a c -> p (a c)")
                P.dma('sp', TMP[:, 0].rearrange("p (t c) -> p t c", t=4), ca_k[l].rearrange("(t p) c -> p t c", p=128), w=[t_TMP[0]], key=t_TMP[0])
                P.dma('sp', TMP[:, 1].rearrange("p (t c) -> p t c", t=4), cc_k[l].rearrange("(t p) c -> p t c", p=128), w=[t_TMP[1]], key=t_TMP[1])
                P.dma('sp', TMP[:, 2:4].rearrange("p a c -> p (a c)").rearrange("p (t c) -> p t c", t=4), cb_ckv[l].rearrange("(t p) c -> p t c", p=128), w=[t_TMP[2]], key=t_TMP[2])
                krp = TMP[:, 4, 0:384].rearrange("p (t c) -> p t c", t=4)
                P.op('pool', lambda e, krp=krp: e.memset(krp, 0.0), w=[t_TMP[4]])
                P.dma('sp', krp[:, :, 64:96], cb_kr[l].rearrange("(t p) c -> p t c", p=128), r=[], wp=[t_TMP[4]], key=t_TMP[4], slow=True)
                for si, (srcv, ts_, rows) in enumerate(((TMP[:, 0], t_TMP[0], 0), (TMP[:, 1], t_TMP[1], 128))):
                    pb, tb = bank()

                    def tr(e, pb=pb, srcv=srcv):
                        inst = None
                        for tt in range(4):
                            inst = e.transpose(pb[:, tt * 128:(tt + 1) * 128], srcv[:, tt * 128:(tt + 1) * 128], identF[:])
                        return inst
                    P.op('pe', tr, r=[ts_, t_const], w=[tb])
                    P.op('act', lambda e, pb=pb, si=si: e.activation(out=TMPB[:, si, :], in_=pb[:], func=AF.Identity), r=[tb], w=[t_TMPB[si]])
                    P.dma('pool', kctx.ap()[l, rows:rows + 128, :], TMPB[:, si, :], r=[t_TMPB[si]], wp=[t_kctxs[l]], key=t_TMPB[si])
                for j in range(2):
                    pb, tb = bank()

                    def tr(e, pb=pb, j=j):
                        inst = None
                        for tt in range(4):
                            inst = e.transpose(pb[:, tt * 128:(tt + 1) * 128], TMP[:, 2 + tt // 2, (tt % 2) * 256 + j * 128:(tt % 2) * 256 + (j + 1) * 128], identF[:])
                        return inst
                    P.op('pe', tr, r=[t_TMP[2], t_const], w=[tb])
                    P.op('act', lambda e, pb=pb, j=j: e.activation(out=TMPB[:, 2 + j, :], in_=pb[:], func=AF.Identity), r=[tb], w=[t_TMPB[2 + j]])
                pb, tb = bank()

                def tr(e, pb=pb, krp=krp):
                    inst = None
                    for tt in range(4):
                        inst = e.transpose(pb[0:96, tt * 128:(tt + 1) * 128], krp[:, tt, :], identF[:])
                    return inst
                P.op('pe', tr, r=[t_TMP[4], t_const], w=[tb])
                krb = BIG[0:96, 0:512]
                P.op('act', lambda e, pb=pb, krb=krb: e.activation(out=krb[64:96, :], in_=pb[64:96, :], func=AF.Identity), r=[tb], w=[t_BIG])
                for h in range(8):
                    pb, tb = bank()

                    def mm(e, pb=pb, h=h, wkv=wkv):
                        inst = None
                        for j in range(2):
                            inst = e.matmul(pb[0:64, :], lhsT=wkv[:, j, h * 64:(h + 1) * 64], rhs=TMPB[:, 2 + j, :], start=(j == 0), stop=(j == 1))
                        return inst
                    P.op('pe', mm, r=[tw, t_TMPB[2], t_TMPB[3]], w=[tb])
                    kh = BIG[0:96, 512 * (1 + h % 2):512 * (2 + h % 2)]
                    tkh = t_TMP[h % 2]
                    P.op('act', lambda e, pb=pb, kh=kh: e.activation(out=kh[0:64, :], in_=pb[0:64, :], func=AF.Identity), r=[tb], w=[tkh])
                    P.op('pool', lambda e, kh=kh, krb=krb: e.tensor_copy(out=kh[64:96, :], in_=krb[64:96, :]), r=[t_BIG], wp=[tkh])
                    P.dma('pool', kctx.ap()[l, 256 + h * 96:256 + (h + 1) * 96, :], kh, r=[tkh, t_BIG], wp=[t_kctxs[l]], key=tkh)
                for tt in range(4):
                    pb, tb = bank()

                    def mm(e, pb=pb, tt=tt, wkv=wkv):
                        inst = None
                        for j in range(2):
                            inst = e.matmul(pb[:, :], lhsT=TMPB[:, 2 + j, tt * 128:(tt + 1) * 128], rhs=wkv[:, j, 512:1024], start=(j == 0), stop=(j == 1))
                        return inst
                    P.op('pe', mm, r=[tw, t_TMPB[2], t_TMPB[3]], w=[tb])
                    vst = BIG[:, 2048 + (tt % 2) * 512: 2048 + (tt % 2 + 1) * 512]
                    tvs = t_TMP[4 + tt % 2]
                    P.op('dve', lambda e, pb=pb, vst=vst: e.tensor_copy(out=vst, in_=pb[:]), r=[tb], w=[tvs])
                    P.dma('pool', vctx.ap()[l, tt * 128:(tt + 1) * 128, 256:768], vst, r=[tvs, t_BIG], wp=[t_kctxs[l]], key=tvs)
            ctx_layer(0)

            ckpt('ctx')
            t_hscr = [T("hscr%d" % i) for i in range(6)]
            t_q = {0: T("q_p"), 1: T("q_s")}
            t_k = {0: T("k_p"), 1: T("kg_in")}
            t_v = {0: T("v_p"), 1: T("vg_in")}
            t_o = {0: T("o_p"), 1: T("o_s")}
            t_kgo = T("kg_out"); t_vgo = T("vg_out")
            kst = T("kstore")
            t_outs = T("outs")

            def rms_stat(src_t, nk, sq_views, scale, out_rstd, t_out, blockdiag=False):
                pb, tb = bank()

                def mm(e, pb=pb):
                    inst = None
                    for k in range(nk):
                        inst = e.matmul(pb[:], lhsT=(bonesB if blockdiag else onesB)[:], rhs=sq_views[k], start=(k == 0), stop=(k == nk - 1))
                    return inst
                P.op('pe', mm, r=list(src_t) + [t_const], w=[tb])
                P.op('act', lambda e, pb=pb: e.activation(out=out_rstd, in_=pb[:], func=AF.Ln, scale=scale, bias=EPS), r=[tb], w=[t_out])
                P.op('act', lambda e: e.activation(out=out_rstd, in_=out_rstd, func=AF.Exp, scale=-0.5), r=[t_out], w=[t_out])

            def prenorm(xv, t_x, l, j, ia, ish):
                for k in range(8):
                    P.op('dve' if k % 2 else 'pool', lambda e, k=k: e.tensor_tensor(out=HT[:, k, :], in0=xv[:, k, :], in1=xv[:, k, :], op=ALU.mult), r=[t_x], w=[t_HT] if k == 0 else (), wp=() if k == 0 else [t_HT])
                rstd = TMP[:, 5, :]
                rms_stat([t_HT], 8, [HT[:, k, :] for k in range(8)], 1.0 / D, rstd, t_TMP[5])
                for k in range(8):
                    P.op('dve', lambda e, k=k: e.scalar_tensor_tensor(out=RES[:, k, :], in0=xv[:, k, :], scalar=MV[:, ia, l, j, k:k + 1], in1=rstd, op0=ALU.mult, op1=ALU.mult),
                         r=[t_x, t_TMP[5]] + t_par, w=[t_RES] if k == 0 else (), wp=() if k == 0 else [t_RES])
                for k in range(8):
                    P.op('act', lambda e, k=k: e.activation(out=HT[:, k, :], in_=RES[:, k, :], func=AF.Identity, bias=MV[:, ish, l, j, k:k + 1], scale=1.0),
                         r=[t_RES] + t_par, w=[t_HT] if k == 0 else (), wp=() if k == 0 else [t_HT])

            def postnorm_resid(xv, t_x, l, j, ig):
                for k in range(8):
                    P.op('act', lambda e, k=k: e.activation(out=HT[:, k, :], in_=RES[:, k, :], func=AF.Square), r=[t_RES], w=[t_HT] if k == 0 else (), wp=() if k == 0 else [t_HT])
                rstd = TMP[:, 5, :]
                rms_stat([t_HT], 8, [HT[:, k, :] for k in range(8)], 1.0 / D, rstd, t_TMP[5])
                for k in range(8):
                    P.op('dve', lambda e, k=k: e.scalar_tensor_tensor(out=RES[:, k, :], in0=RES[:, k, :], scalar=MV[:, ig, l, j, k:k + 1], in1=rstd, op0=ALU.mult, op1=ALU.mult),
                         r=[t_TMP[5]] + t_par, wp=[t_RES])
                for k in range(8):
                    P.op('dve' if k % 2 else 'pool', lambda e, k=k: e.tensor_tensor(out=xv[:, k, :], in0=xv[:, k, :], in1=RES[:, k, :], op=ALU.add), r=[t_RES], wp=[t_x])

            def load_w(dram_view, shape_k, ncols, deps):
                ws, tw = wslot()
                v = ws[:, 0:shape_k * ncols].rearrange("p (k c) -> p k c", k=shape_k)
                P.dma('sp', v, dram_view, r=deps, w=[tw], key=tw)
                return v, tw

            def proj_fm(wv, tw, c0, m, nk, rhs_of_k, rhs_t, prow=None):
                pb, tb = bank()

                def mm(e, pb=pb):
                    inst = None
                    for k in range(nk):
                        inst = e.matmul(pb[0:m, :], lhsT=wv[:, k, c0:c0 + m], rhs=rhs_of_k(k), start=(k == 0), stop=(k == nk - 1))
                    return inst
                P.op('pe', mm, r=[tw] + list(rhs_t), w=[tb])
                return pb, tb

            def D1(l, grp, ci, xv, t_x, col0):
                j = grp
                hidx = ci if grp == 0 else 2 + ci
                q_d = (q_p if grp == 0 else q_s).ap()
                cs = slice(col0, col0 + CH)
                if grp == 0:
                    k_d = k_p.ap()
                    v_dst = v_p.ap()[col0:col0 + CH, :]
                    kcs = cs
                else:
                    k_d = kgi[ci // 2].ap()
                    v_dst = vgi[ci // 2].ap()[(ci % 2) * CH:(ci % 2 + 1) * CH, :]
                    kcs = slice((ci % 2) * CH, (ci % 2 + 1) * CH)
                prenorm(xv, t_x, l, j, 0, 1)
                P.dma('pool', h_scr.ap()[hidx], HTflat, r=[t_HT], w=[t_hscr[hidx]], key=t_HT)
                if grp == 1:
                    P.dma('sp', ROPE[:, 0:2, :], c_ropeA[:, :, cs].rearrange("t p c -> p t c"), w=[t_ROPE], key=t_ROPE)
                    P.dma('sp', ROPE[0:96, 2:4, :], c_ropeB[:, :, cs].rearrange("t p c -> p t c"), wp=[t_ROPE], key=t_ROPE)
                ckpt('d1a')
                hk = lambda k: HT[:, k, :]
                stage_ctr = [0]

                def stage_bf():
                    i = stage_ctr[0] % 2
                    stage_ctr[0] += 1
                    return TMPB[:, i, :], t_TMPB[i]

                def rope_combine(pb1, tb1, pb2, tb2, rows, ci_c, ci_s, outv, t_out, gcol=None, rstd=None, t_rstd=None):
                    r0, r1 = rows
                    if gcol is None:
                        P.op('dve', lambda e: e.tensor_tensor(out=TMP[r0:r1, 0, :], in0=pb1[r0:r1, :], in1=ROPE[r0:r1, ci_c, :], op=ALU.mult), r=[tb1, t_ROPE], w=[t_TMP[0]])
                        P.op('dve', lambda e: e.tensor_tensor(out=TMP[r0:r1, 1, :], in0=pb2[r0:r1, :], in1=ROPE[r0:r1, ci_s, :], op=ALU.mult), r=[tb2, t_ROPE], w=[t_TMP[1]])
                    else:
                        P.op('act', lambda e: e.activation(out=TMP[r0:r1, 0, :], in_=pb1[r0:r1, :], func=AF.Identity, scale=gq4[r0:r1, gcol, l:l + 1]), r=[tb1] + t_par, w=[t_TMP[0]])
                        P.op('act', lambda e: e.activation(out=TMP[r0:r1, 1, :], in_=pb2[r0:r1, :], func=AF.Identity, scale=gq4[r0:r1, gcol + 1, l:l + 1]), r=[tb2] + t_par, w=[t_TMP[1]])
                        P.op('dve', lambda e: e.tensor_tensor(out=TMP[r0:r1, 0, :], in0=TMP[r0:r1, 0, :], in1=ROPE[r0:r1, ci_c, :], op=ALU.mult), r=[t_ROPE], w=[t_TMP[0]])
                        P.op('dve', lambda e: e.tensor_tensor(out=TMP[r0:r1, 1, :], in0=TMP[r0:r1, 1, :], in1=ROPE[r0:r1, ci_s, :], op=ALU.mult), r=[t_ROPE], w=[t_TMP[1]])
                    if rstd is None:
                        P.op('pool', lambda e: e.tensor_tensor(out=outv[r0:r1, :], in0=TMP[r0:r1, 0, :], in1=TMP[r0:r1, 1, :], op=ALU.add), r=[t_TMP[0], t_TMP[1]], w=[t_out])
                    else:
                        P.op('pool', lambda e: e.tensor_tensor(out=TMP[r0:r1, 0, :], in0=TMP[r0:r1, 0, :], in1=TMP[r0:r1, 1, :], op=ALU.add), r=[t_TMP[1]], w=[t_TMP[0]])
                        P.op('dve', lambda e: e.tensor_tensor(out=outv[r0:r1, :], in0=TMP[r0:r1, 0, :], in1=rstd, op=ALU.mult), r=[t_TMP[0], t_rstd], w=[t_out])

                def out_tok(srcF, t_src, rows, ncols_per, dst_ap, colsel=None):
                    pb, tb = bank()

                    def tr(e, pb=pb):
                        inst = None
                        for tt in range(4):
                            inst = e.transpose(pb[:, tt * 128:tt * 128 + rows], srcF[0:rows, tt * 128:(tt + 1) * 128], identF[0:rows, 0:rows])
                        return inst
                    P.op('pe', tr, r=[t_src, t_const], w=[tb])
                    stg = TMP[:, 4, :]
                    P.op('dve', lambda e, pb=pb: e.tensor_copy(out=stg, in_=pb[:]), r=[tb], w=[t_TMP[4]])
                    sv = stg.rearrange("p (t c) -> p t c", t=4)
                    if colsel is not None:
                        sv = sv[:, :, colsel[0]:colsel[1]]
                    else:
                        sv = sv[:, :, 0:ncols_per]
                    for s_ in range(2):
                        P.dma('pool', dst_ap[s_], sv[:, 2 * s_:2 * s_ + 2, :], r=[t_TMP[4]], wp=[t_outs], key=t_TMP[4], slow=True)

                def tok_dst(o_ap, c0=None, c1=None):
                    vs_ = []
                    for s_ in range(2):
                        v = o_ap[2 * ci + s_, l].rearrange("(u p) f -> p u f", p=128)
                        if c0 is not None:
                            v = v[:, :, c0:c1]
                        vs_.append(v)
                    return vs_

                for (off, swoff, nblk, qrow0, is_c) in ((O_QA, 0, 4, 0, False), (O_KA, 512, 1, None, False), (O_QC, 640, 4, 1280, True), (O_KC, 1152, 1, None, True)):
                    wd = nblk * 128
                    is_k = nblk == 1
                    wv, tw = load_w(win_b.ap()[l, :, off:off + wd].rearrange("(k p) c -> p k c", p=128), 8, wd, [t_wcs[l]])
                    if grp == 1:
                        wv2, tw2 = load_w(wsw_b.ap()[l, :, swoff:swoff + wd].rearrange("(k p) c -> p k c", p=128), 8, wd, [t_wsws[l]])
                    for b in range(nblk):
                        pb1, tb1 = proj_fm(wv, tw, b * 128, 128, 8, hk, [t_HT])
                        if grp == 1:
                            pb2, tb2 = proj_fm(wv2, tw2, b * 128, 128, 8, hk, [t_HT])
                        outv, t_out = stage_bf()
                        rstd = None
                        if is_c:
                            P.op('act', lambda e, pb1=pb1: e.activation(out=TMPB[:, 2, :], in_=pb1[:], func=AF.Square), r=[tb1], w=[t_TMPB[2]])
                            rstd = TMP[:, 2, :]
                            rms_stat([t_TMPB[2]], 1, [TMPB[:, 2, :]], 1.0 / 64, rstd, t_TMP[2], blockdiag=True)
                        gcol = (2 if is_k else 0) if is_c else None
                        if grp == 1:
                            rope_combine(pb1, tb1, pb2, tb2, (0, 128), 0, 1, outv, t_out, gcol=gcol, rstd=rstd, t_rstd=t_TMP[2])
                        else:
                            if is_c:
                                P.op('act', lambda e, pb1=pb1, gcol=gcol: e.activation(out=TMP[:, 3, :], in_=pb1[:], func=AF.Identity, scale=gq4[:, gcol, l:l + 1]), r=[tb1] + t_par, w=[t_TMP[3]])
                                P.op('dve', lambda e, rstd=rstd: e.tensor_tensor(out=TMP[:, 3, :], in0=TMP[:, 3, :], in1=rstd, op=ALU.mult), r=[t_TMP[2]], w=[t_TMP[3]])
                            else:
                                P.op('dve', lambda e, pb1=pb1: e.tensor_copy(out=TMP[:, 3, :], in_=pb1[:]), r=[tb1], w=[t_TMP[3]])
                            P.op('act', lambda e, outv=outv: e.activation(out=outv, in_=TMP[:, 3, :], func=AF.Identity), r=[t_TMP[3]], w=[t_out])
                            if is_k:
                                out_tok(TMP[:, 3, :], t_TMP[3], 128, 128, tok_dst(o_ck if is_c else o_ak))
                        if l == 0 and grp == 1 and ci == 0:
                            ckpt('blk_%d_%d' % (off, b))
                        if is_k:
                            krow = 128 if is_c else 0
                            P.dma('pool', k_d[krow:krow + 128, kcs], outv, r=[t_out], wp=[t_k[grp]], key=t_out)
                        else:
                            P.dma('pool', q_d[qrow0 + b * 128:qrow0 + (b + 1) * 128, cs], outv, r=[t_out], wp=[t_q[grp]], key=t_out)

                ckpt('d1b')
                wv, tw = load_w(win_b.ap()[l, :, O_VA:O_VA + 128].rearrange("(k p) c -> p k c", p=128), 8, 128, [t_wcs[l]])
                wvc, twc = load_w(win_b.ap()[l, :, O_VC:O_VC + 128].rearrange("(k p) c -> p k c", p=128), 8, 128, [t_wcs[l]])
                vstage = BIG[:, 0:4 * 768].rearrange("p (t c) -> p t c", t=4)
                for vi, (wvx, twx, o_dst) in enumerate(((wv, tw, o_av), (wvc, twc, o_cv))):
                    pb, tb = bank()

                    def mm(e, pb=pb, wvx=wvx):
                        inst = None
                        for tt in range(4):
                            for k in range(8):
                                inst = e.matmul(pb[:, tt * 128:(tt + 1) * 128], lhsT=HT[:, k, tt * 128:(tt + 1) * 128], rhs=wvx[:, k, :], start=(k == 0), stop=(k == 7))
                        return inst
                    P.op('pe', mm, r=[twx, t_HT], w=[tb])
                    P.op('act', lambda e, pb=pb, vi=vi: e.activation(out=vstage[:, :, vi * 128:(vi + 1) * 128], in_=pb[:].rearrange("p (t c) -> p t c", t=4), func=AF.Identity),
                         r=[tb], w=[t_BIG] if vi == 0 else (), wp=() if vi == 0 else [t_BIG])
                    if grp == 0:
                        P.op('dve', lambda e, pb=pb: e.tensor_copy(out=TMP[:, 4, :], in_=pb[:]), r=[tb], w=[t_TMP[4]])
                        for s_ in range(2):
                            P.dma('pool', tok_dst(o_dst)[s_], TMP[:, 4, :].rearrange("p (t c) -> p t c", t=4)[:, 2 * s_:2 * s_ + 2, :], r=[t_TMP[4]], wp=[t_outs], key=t_TMP[4])

                ckpt('d1c')
                wv, tw = load_w(win_b.ap()[l, :, O_QBD:O_QBD + 384].rearrange("(k p) c -> p k c", p=128), 8, 384, [t_wcs[l]])
                QN = BIG[:, 4096:4096 + 1536].rearrange("p (k c) -> p k c", k=3)
                t_QN = T("QN")
                QF = RES[:, 0:3, :]
                for b in range(3):
                    pb, tb = proj_fm(wv, tw, b * 128, 128, 8, hk, [t_HT])
                    P.op('act', lambda e, pb=pb, b=b: e.activation(out=QF[:, b, :], in_=pb[:], func=AF.Identity), r=[tb], w=[t_RES] if b == 0 else (), wp=() if b == 0 else [t_RES])
                    P.op('pool', lambda e, b=b: e.tensor_tensor(out=QN[:, b, :], in0=QF[:, b, :], in1=QF[:, b, :], op=ALU.mult), r=[t_RES, t_BIG], w=[t_QN] if b == 0 else (), wp=() if b == 0 else [t_QN])
                rms_stat([t_QN], 3, [QN[:, b, :] for b in range(3)], 1.0 / 384, TMP[:, 2, :], t_TMP[2])
                for b in range(3):
                    P.op('dve', lambda e, b=b: e.scalar_tensor_tensor(out=QN[:, b, :], in0=QF[:, b, :], scalar=gqaT[:, l, b:b + 1], in1=TMP[:, 2, :], op0=ALU.mult, op1=ALU.mult),
                         r=[t_RES, t_TMP[2]] + t_par, w=[t_QN] if b == 0 else (), wp=() if b == 0 else [t_QN])
                wq, twq = load_w(wqup_b.ap()[l].rearrange("(k p) c -> p k c", p=128), 3, 768, [t_wcs[l]])
                if grp == 1:
                    wq2, twq2 = load_w(wqupsw_b.ap()[l].rearrange("(k p) c -> p k c", p=128), 3, 768, [t_wsws[l]])
                qn_k = lambda k: QN[:, k, :]
                for h in range(8):
                    pb1, tb1 = proj_fm(wq, twq, h * 96, 96, 3, qn_k, [t_QN])
                    outv, t_out = stage_bf()
                    if grp == 1:
                        pb2, tb2 = proj_fm(wq2, twq2, h * 96, 96, 3, qn_k, [t_QN])
                        rope_combine(pb1, tb1, pb2, tb2, (0, 96), 2, 3, outv, t_out)
                    else:
                        P.op('act', lambda e, pb1=pb1, outv=outv: e.activation(out=outv[0:96, :], in_=pb1[0:96, :], func=AF.Identity), r=[tb1], w=[t_out])
                    P.dma('pool', q_d[512 + h * 96:512 + (h + 1) * 96, cs], outv[0:96, :], r=[t_out], wp=[t_q[grp]], key=t_out)

                ckpt('d1d')
                wv, tw = load_w(win_b.ap()[l, :, O_KVBD:O_KVBD + 288].rearrange("(k p) c -> p k c", p=128), 8, 288, [t_wcs[l]])
                CKN = BIG[:, 4096:4096 + 1024].rearrange("p (k c) -> p k c", k=2)
                KF = RES[:, 0:2, :]
                for b in range(2):
                    pb, tb = proj_fm(wv, tw, b * 128, 128, 8, hk, [t_HT])
                    P.op('act', lambda e, pb=pb, b=b: e.activation(out=KF[:, b, :], in_=pb[:], func=AF.Identity), r=[tb], w=[t_RES] if b == 0 else (), wp=() if b == 0 else [t_RES])
                    P.op('pool', lambda e, b=b: e.tensor_tensor(out=CKN[:, b, :], in0=KF[:, b, :], in1=KF[:, b, :], op=ALU.mult), r=[t_RES, t_BIG], w=[t_QN] if b == 0 else (), wp=() if b == 0 else [t_QN])
                rms_stat([t_QN], 2, [CKN[:, b, :] for b in range(2)], 1.0 / 256, TMP[:, 2, :], t_TMP[2])
                for b in range(2):
                    P.op('dve', lambda e, b=b: e.scalar_tensor_tensor(out=KF[:, b, :], in0=KF[:, b, :], scalar=gkvaT[:, l, b:b + 1], in1=TMP[:, 2, :], op0=ALU.mult, op1=ALU.mult),
                         r=[t_TMP[2]] + t_par, wp=[t_RES])
                    P.op('act', lambda e, b=b: e.activation(out=CKN[:, b, :], in_=KF[:, b, :], func=AF.Identity), r=[t_RES], w=[t_QN] if b == 0 else (), wp=() if b == 0 else [t_QN])
                    if grp == 0:
                        out_tok(KF[:, b, :], t_RES, 128, 128, tok_dst(o_ckv, b * 128, (b + 1) * 128))
                pbk, tbk = proj_fm(wv, tw, 192, 96, 8, hk, [t_HT])
                KR = TMPB[:, 3, :]
                if grp == 1:
                    wv2, tw2 = load_w(wsw_b.ap()[l, :, 1280:1376].rearrange("(k p) c -> p k c", p=128), 8, 96, [t_wsws[l]])
                    pbk2, tbk2 = proj_fm(wv2, tw2, 0, 96, 8, hk, [t_HT])
                    rope_combine(pbk, tbk, pbk2, tbk2, (64, 96), 2, 3, KR, t_TMPB[3])
                else:
                    P.op('dve', lambda e: e.tensor_copy(out=TMP[0:96, 3, :], in_=pbk[0:96, :]), r=[tbk], w=[t_TMP[3]])
                    P.op('act', lambda e: e.activation(out=KR[64:96, :], in_=TMP[64:96, 3, :], func=AF.Identity), r=[t_TMP[3]], w=[t_TMPB[3]])
                    out_tok(TMP[:, 3, :], t_TMP[3], 96, 96, tok_dst(o_kr), colsel=(64, 96))
                wk, twk = load_w(wkvup_b.ap()[l].rearrange("(k p) c -> p k c", p=128), 2, 1024, [t_wcs[l]])
                ck_k = lambda k: CKN[:, k, :]
                for h in range(8):
                    pb, tb = proj_fm(wk, twk, h * 64, 64, 2, ck_k, [t_QN])
                    outv, t_out = stage_bf()
                    P.op('act', lambda e, pb=pb, outv=outv: e.activation(out=outv[0:64, :], in_=pb[0:64, :], func=AF.Identity), r=[tb], w=[t_out])
                    P.op('pool', lambda e, outv=outv: e.tensor_copy(out=outv[64:96, :], in_=KR[64:96, :]), r=[t_TMPB[3]], wp=[t_out])
                    P.dma('pool', k_d[256 + h * 96:256 + (h + 1) * 96, kcs], outv[0:96, :], r=[t_out], wp=[t_k[grp]], key=t_out)
                for tt in range(4):
                    pb, tb = bank()

                    def mm(e, pb=pb, tt=tt):
                        inst = None
                        for k in range(2):
                            inst = e.matmul(pb[:, :], lhsT=CKN[:, k, tt * 128:(tt + 1) * 128], rhs=wk[:, k, 512:1024], start=(k == 0), stop=(k == 1))
                        return inst
                    P.op('pe', mm, r=[twk, t_QN], w=[tb])
                    P.op('act' if tt % 2 else 'dve',
                         (lambda e, pb=pb, tt=tt: e.activation(out=vstage[:, tt, 256:768], in_=pb[:], func=AF.Identity)) if tt % 2 else
                         (lambda e, pb=pb, tt=tt: e.tensor_copy(out=vstage[:, tt, 256:768], in_=pb[:])), r=[tb], wp=[t_BIG])
                P.dma('pool', v_dst.rearrange("(t p) c -> p t c", p=128), vstage, r=[t_BIG], wp=[t_v[grp]], key=t_BIG)

            att_ctr = [0]

            def att_head(Kv, t_K, krows, Vv, t_V, Qv, t_Q, qcols, ktiles, scale, out_rows_ap, t_o_dst, sink_ap=None, odd=False, okey=None, masks=None):
                r0, r1 = krows
                q0, nq = qcols
                po, tpo = obank()
                nkt = len(ktiles)
                LOOK = 3
                pend = {}
                for i in range(nkt + LOOK):
                    if i < nkt:
                        kt = ktiles[i]
                        pbs, tbs = bank()
                        P.op('pe', lambda e, pbs=pbs, kt=kt: e.matmul(pbs[:, 0:nq], lhsT=Kv[r0:r1, kt * 128:(kt + 1) * 128], rhs=Qv[r0:r1, q0:q0 + nq], start=True, stop=True),
                             r=[t_K, t_Q], w=[tbs])
                        pend[i] = (pbs, tbs)
                    jx = i - LOOK
                    if jx >= 0:
                        kt = ktiles[jx]
                        pbs, tbs = pend.pop(jx)
                        pi = att_ctr[0] % 4
                        att_ctr[0] += 1
                        pt = TMPB[:, pi, 0:nq]
                        P.op('act', lambda e, pbs=pbs, pt=pt: e.activation(out=pt, in_=pbs[:, 0:nq], func=AF.Exp, scale=scale), r=[tbs], w=[t_TMPB[pi]])
                        if masks is not None and masks[jx] is not None:
                            P.op('dve', lambda e, pt=pt, m=masks[jx]: e.tensor_tensor(out=pt, in0=pt, in1=m, op=ALU.mult), r=[t_const], w=[t_TMPB[pi]])
                        P.op('pe', lambda e, pt=pt, kt=kt, jx=jx: e.matmul(po[:, 0:nq], lhsT=Vv[:, kt, :], rhs=pt, start=(jx == 0), stop=(jx == nkt - 1)),
                             r=[t_V, t_TMPB[pi]], w=[tpo] if jx == 0 else (), wp=() if jx == 0 else [tpo])
                ti = att_ctr[0] % 2
                dt_ = TMP[0:64, ti, 0:nq]
                tdt = t_TMP[ti]
                if sink_ap is not None:
                    P.op('dve', lambda e: e.tensor_copy(out=dt_, in_=po[64:128, 0:nq]), r=[tpo], w=[tdt])
                    P.op('dve', lambda e: e.tensor_scalar(out=dt_, in0=dt_, scalar1=sink_ap, scalar2=None, op0=ALU.add), r=t_par, w=[tdt])
                    P.op('dve', lambda e: e.reciprocal(out=dt_, in_=dt_), r=[tdt], w=[tdt])
                else:
                    P.op('dve', lambda e: e.reciprocal(out=dt_, in_=po[64:128, 0:nq]), r=[tpo], w=[tdt])
                oi = 2 + att_ctr[0] % 2
                ob = TMP[0:64, oi, :].bitcast(BF16)[:, 0:nq]
                P.op('dve', lambda e: e.tensor_tensor(out=ob, in0=po[0:64, 0:nq], in1=dt_, op=ALU.mult), r=[tpo, tdt], w=[t_TMP[oi]])
                P.dma('pool', out_rows_ap, ob, r=[t_TMP[oi]], wp=[t_o_dst], key=t_TMP[oi])

            t_KT = [T("KT0"), T("KT1")]; t_VT = [T("VT0"), T("VT1")]; t_QT = [T("QT0"), T("QT1")]
            ones_set = [False]

            def arena_to_att():
                for s_ in t_KT + t_VT + t_QT:
                    for t in arena_all:
                        _merge(s_.r, t.r)
                        _merge(s_.r, t.w)
                for i in range(2):
                    P.op('pool', lambda e, i=i: e.memset(VT[i][:, :, 64:128], 1.0), r=[], w=[t_VT[i]])

            def att_to_arena():
                for t in arena_all:
                    _merge(t.w, {})
                for t in arena_all:
                    for s in t_KT + t_VT + t_QT:
                        _merge(t.r, s.r)
                        _merge(t.r, s.w)

            def ATT_prompt(l):
                arena_to_att()
                qd, kd, vd, od = q_p.ap(), k_p.ap(), v_p.ap(), o_p.ap()
                for br, (krow, vcol, qrow, orow, sc) in enumerate(((0, 0, 0, 0, SC_A), (128, 128, 1280, 1024, SC_A))):
                    ks = br % 2
                    Kv = KT[ks][:, 0:1024]
                    P.dma('sp', Kv, kd[krow:krow + 128, :], r=[t_k[0]], w=[t_KT[ks]], key=t_KT[ks])
                    for g in range(2):
                        Vv = VT[g]
                        P.dma('sp', Vv[:, 0:8, 0:64], vd[:, vcol + g * 64:vcol + (g + 1) * 64].rearrange("(t p) c -> p t c", p=128), r=[t_v[0]], wp=[t_VT[g]], key=t_VT[g], slow=True)
                    for jb in range(4):
                        qs_ = jb % 2
                        Qv = QT[qs_][:, 0:1024]
                        P.dma('sp', Qv, qd[qrow + jb * 128:qrow + (jb + 1) * 128, :], r=[t_q[0]], w=[t_QT[qs_]], key=t_QT[qs_])
                        for g in range(2):
                            h = jb + 4 * g
                            for s in range(4):
                                att_head(Kv, t_KT[ks], (g * 64, g * 64 + 64), VT[g], t_VT[g], Qv, t_QT[qs_], (s * 256, 256), [2 * s, 2 * s + 1], sc,
                                         od[orow + h * 64:orow + (h + 1) * 64, s * 256:(s + 1) * 256], t_o[0],
                                         sink_ap=(sinkE[0:64, l, h:h + 1] if br == 0 else None))
                for h in range(8):
                    ks = h % 2
                    Kv = KT[ks][:, 0:1024]
                    P.dma('sp', Kv[0:96, :], kd[256 + h * 96:256 + (h + 1) * 96, :], r=[t_k[0]], w=[t_KT[ks]], key=t_KT[ks])
                    Vv = VT[ks]
                    P.dma('sp', Vv[:, 0:8, 0:64], vd[:, 256 + h * 64:256 + (h + 1) * 64].rearrange("(t p) c -> p t c", p=128), r=[t_v[0]], wp=[t_VT[ks]], key=t_VT[ks], slow=True)
                    Qv = QT[ks][:, 0:1024]
                    P.dma('sp', Qv[0:96, :], qd[512 + h * 96:512 + (h + 1) * 96, :], r=[t_q[0]], w=[t_QT[ks]], key=t_QT[ks])
                    for s in range(4):
                        att_head(Kv, t_KT[ks], (0, 96), Vv, t_VT[ks], Qv, t_QT[ks], (s * 256, 256), [2 * s, 2 * s + 1], SC_B,
                                 od[512 + h * 64:512 + (h + 1) * 64, s * 256:(s + 1) * 256], t_o[0])
                att_to_arena()

            def ATT_sample(l):
                arena_to_att()
                qd, od = q_s.ap(), o_s.ap()

                def load_K(ks, row0, nrows):
                    Kv = KT[ks]
                    for r in range(2):
                        for t in range(2):
                            first = (r == 0 and t == 0)
                            P.dma('sp', Kv[0:nrows, (2 * r + t) * 1024:(2 * r + t + 1) * 1024], kgo[t].ap()[r * 1024 + row0:r * 1024 + row0 + nrows, :], r=[t_kgo],
                                  w=[t_KT[ks]] if first else (), wp=() if first else [t_KT[ks]], key=t_KT[ks])
                    P.dma('sp', Kv[0:nrows, 4096:4608], kctx.ap()[l, row0:row0 + nrows, :], r=[t_kctxs[l], t_wcs[l]], wp=[t_KT[ks]], key=t_KT[ks])
                    return Kv

                def load_V(vs, col0):
                    Vv = VT[vs]
                    for r in range(2):
                        for t in range(2):
                            P.dma('sp', Vv[:, (2 * r + t) * 8:(2 * r + t + 1) * 8, 0:64], vgo[t].ap()[r * 1024:(r + 1) * 1024, col0:col0 + 64].rearrange("(t p) c -> p t c", p=128), r=[t_vgo], wp=[t_VT[vs]], key=t_VT[vs], slow=True)
                    P.dma('sp', Vv[:, 32:36, 0:64], vctx.ap()[l, :, col0:col0 + 64].rearrange("(t p) c -> p t c", p=128), r=[t_kctxs[l], t_wcs[l]], wp=[t_VT[vs]], key=t_VT[vs], slow=True)
                    return Vv
                Kv = load_K(0, 128, 128)
                for g in range(2):
                    load_V(g, 128 + g * 64)
                for jb in range(4):
                    qs_ = jb % 2
                    Qv = QT[qs_]
                    P.dma('sp', Qv, qd[1280 + jb * 128:1280 + (jb + 1) * 128, :], r=[t_q[1]], w=[t_QT[qs_]], key=t_QT[qs_])
                    for g in range(2):
                        h = jb + 4 * g
                        for qc in range(4):
                            att_head(Kv, t_KT[0], (g * 64, g * 64 + 64), VT[g], t_VT[g], Qv, t_QT[qs_], (qc * 512, 512), list(range(36)), SC_A,
                                     od[1024 + h * 64:1024 + (h + 1) * 64, qc * 512:(qc + 1) * 512], t_o[1])
                for h in range(8):
                    ks = (h + 1) % 2
                    Kv = load_K(ks, 256 + h * 96, 96)
                    Vv = load_V(ks, 256 + h * 64)
                    Qv = QT[ks]
                    P.dma('sp', Qv[0:96, :], qd[512 + h * 96:512 + (h + 1) * 96, :], r=[t_q[1]], w=[t_QT[ks]], key=t_QT[ks])
                    for qc in range(4):
                        att_head(Kv, t_KT[ks], (0, 96), Vv, t_VT[ks], Qv, t_QT[ks], (qc * 512, 512), list(range(36)), SC_B,
                                 od[512 + h * 64:512 + (h + 1) * 64, qc * 512:(qc + 1) * 512], t_o[1])
                Kv = KT[0]
                P.dma('sp', Kv[:, 0:128], kgo[1].ap()[0:128, 896:1024], r=[t_kgo], w=[t_KT[0]], key=t_KT[0])
                for t in range(2):
                    P.dma('sp', Kv[:, 128 + t * 1024:128 + (t + 1) * 1024], kgi[t].ap()[0:128, :], r=[t_k[1]], wp=[t_KT[0]], key=t_KT[0])
                P.dma('sp', Kv[:, 2176:2304], kgo[0].ap()[1024:1152, 0:128], r=[t_kgo], wp=[t_KT[0]], key=t_KT[0])
                P.dma('sp', Kv[:, 2304:2816], kctx.ap()[l, 0:128, :], r=[t_kctxs[l], t_wcs[l]], wp=[t_KT[0]], key=t_KT[0])
                for g in range(2):
                    Vv = VT[g]
                    c0 = g * 64
                    P.dma('sp', Vv[:, 0:1, 0:64], vgo[1].ap()[896:1024, c0:c0 + 64].rearrange("(t p) c -> p t c", p=128), r=[t_vgo], wp=[t_VT[g]], key=t_VT[g], slow=True)
                    for r in range(2):
                        P.dma('sp', Vv[:, 1 + r * 8:1 + (r + 1) * 8, 0:64], vgi[r].ap()[:, c0:c0 + 64].rearrange("(t p) c -> p t c", p=128), r=[t_v[1]], wp=[t_VT[g]], key=t_VT[g], slow=True)
                    P.dma('sp', Vv[:, 17:18, 0:64], vgo[0].ap()[1024:1152, c0:c0 + 64].rearrange("(t p) c -> p t c", p=128), r=[t_vgo], wp=[t_VT[g]], key=t_VT[g], slow=True)
                    P.dma('sp', Vv[:, 18:22, 0:64], vctx.ap()[l, :, c0:c0 + 64].rearrange("(t p) c -> p t c", p=128), r=[t_kctxs[l], t_wcs[l]], wp=[t_VT[g]], key=t_VT[g], slow=True)
                for jb in range(4):
                    qs_ = jb % 2
                    Qv = QT[qs_]
                    P.dma('sp', Qv, qd[jb * 128:(jb + 1) * 128, :], r=[t_q[1]], w=[t_QT[qs_]], key=t_QT[qs_])
                    for g in range(2):
                        h = jb + 4 * g
                        for qb in range(16):
                            var = 0 if qb == 0 else (2 if qb == 15 else 1)
                            masks = [maskB[:, var, 0:128], None if var == 1 else maskB[:, var, 128:256], maskB[:, var, 256:384], None, None, None, None]
                            if var == 1:
                                masks[1] = None
                            att_head(Kv, t_KT[0], (g * 64, g * 64 + 64), VT[g], t_VT[g], Qv, t_QT[qs_], (qb * 128, 128),
                                     [qb, qb + 1, qb + 2, 18, 19, 20, 21], SC_A,
                                     od[h * 64:(h + 1) * 64, qb * 128:(qb + 1) * 128], t_o[1],
                                     sink_ap=sinkE[0:64, l, h:h + 1], masks=masks)
                att_to_arena()

            def D2(l, grp, ci, xv, t_x, col0):
                j = grp
                hidx = ci if grp == 0 else 2 + ci
                o_d = (o_p if grp == 0 else o_s).ap()
                cs = slice(col0, col0 + CH)
                P.dma('sp', HTflat, h_scr.ap()[hidx], r=[t_hscr[hidx]], w=[t_HT], key=t_HT)
                OT = BIG[:, 0:6144].rearrange("p (k c) -> p k c", k=12)
                t_OT = T("OT")
                P.dma('sp', OT, o_d[:, cs].rearrange("(k p) c -> p k c", p=128), r=[t_o[grp]], w=[t_OT, t_BIG], key=t_OT)
                MT = BIG[:, 6144:10240].rearrange("p (k c) -> p k c", k=8)
                t_MT = T("MT")
                WO = BIG[:, 10240:16384].rearrange("p (k c) -> p k c", k=12)
                t_WO = T("WO")
                for n in range(8):
                    if n % 4 == 0:
                        ng = n // 4
                        P.dma('sp', WO, wo_b.ap()[l, ng].rearrange("p (k c) -> p k c", k=12), r=[t_wcs[l]], w=[t_WO], key=t_WO)
                    wg, twg = load_w(wg_b.ap()[l, n].rearrange("p (k c) -> p k c", k=8), 8, 384, [t_wcs[l]])
                    for b in range(3):
                        pg, tg = bank()

                        def mmg(e, pg=pg, b=b, wg=wg):
                            inst = None
                            for k in range(8):
                                inst = e.matmul(pg[:], lhsT=wg[:, k, b * 128:(b + 1) * 128], rhs=HT[:, k, :], start=(k == 0), stop=(k == 7))
                            return inst
                        P.op('pe', mmg, r=[twg, t_HT], w=[tg])
                        sg = TMP[:, b % 2, :]
                        P.op('act', lambda e, pg=pg, sg=sg: e.activation(out=sg, in_=pg[:], func=AF.Sigmoid), r=[tg], w=[t_TMP[b % 2]])
                        pp, tp = bank()

                        def mmp(e, pp=pp, b=b, n=n):
                            inst = None
                            for k in range(4):
                                inst = e.matmul(pp[:], lhsT=WO[:, b * 4 + k, (n % 4) * 128:(n % 4 + 1) * 128], rhs=OT[:, b * 4 + k, :], start=(k == 0), stop=(k == 3))
                            return inst
                        P.op('pe', mmp, r=[t_WO, t_OT], w=[tp])
                        if b == 0:
                            P.op('dve', lambda e, pp=pp, sg=sg: e.tensor_tensor(out=TMP[:, 2, :], in0=pp[:], in1=sg, op=ALU.mult), r=[tp, t_TMP[0]], w=[t_TMP[2]])
                        else:
                            P.op('dve', lambda e, pp=pp, sg=sg: e.tensor_tensor(out=TMP[:, 3, :], in0=pp[:], in1=sg, op=ALU.mult), r=[tp, t_TMP[b % 2]], w=[t_TMP[3]])
                            if b == 1:
                                P.op('pool', lambda e: e.tensor_tensor(out=TMP[:, 2, :], in0=TMP[:, 2, :], in1=TMP[:, 3, :], op=ALU.add), r=[t_TMP[3]], w=[t_TMP[2]])
                            else:
                                P.op('pool', lambda e, n=n: e.tensor_tensor(out=MT[:, n, :], in0=TMP[:, 2, :], in1=TMP[:, 3, :], op=ALU.add), r=[t_TMP[2], t_TMP[3]], wp=[t_MT])
                for ng in range(2):
                    wv, tw = load_w(wout_b.ap()[l, ng].rearrange("p (k c) -> p k c", k=8), 8, 512, [t_wcs[l]])
                    for nb in range(4):
                        n = ng * 4 + nb
                        pb, tb = proj_fm(wv, tw, nb * 128, 128, 8, lambda k: MT[:, k, :], [t_MT])
                        P.op('act' if n % 2 else 'dve',
                             (lambda e, pb=pb, n=n: e.activation(out=RES[:, n, :], in_=pb[:], func=AF.Identity)) if n % 2 else
                             (lambda e, pb=pb, n=n: e.tensor_copy(out=RES[:, n, :], in_=pb[:])), r=[tb], w=[t_RES] if n == 0 else (), wp=() if n == 0 else [t_RES])
                postnorm_resid(xv, t_x, l, j, 2)
                prenorm(xv, t_x, l, j, 3, 4)
                UT = BIG[:].rearrange("p (k c) -> p k c", k=32)
                t_UT = T("UT")
                for g8 in range(8):
                    wv, tw = load_w(w1_b.ap()[l, g8].rearrange("p (k c) -> p k c", k=8), 8, 512, [t_wcs[l]])
                    for nb in range(4):
                        n = g8 * 4 + nb
                        pb, tb = proj_fm(wv, tw, nb * 128, 128, 8, lambda k: HT[:, k, :], [t_HT])
                        ti = n % 2
                        P.op('act', lambda e, pb=pb, ti=ti: e.activation(out=TMP[:, ti, :], in_=pb[:], func=AF.Relu), r=[tb], w=[t_TMP[ti]])
                        first = (n == 0)
                        P.op('dve', lambda e, n=n, ti=ti: e.tensor_tensor(out=UT[:, n, :], in0=TMP[:, ti, :], in1=TMP[:, ti, :], op=ALU.mult), r=[t_TMP[ti]],
                             w=[t_UT, t_BIG, t_OT, t_MT, t_WO] if first else (), wp=() if first else [t_UT])
                for n in range(8):
                    wv, tw = load_w(w2_b.ap()[l, n].rearrange("p (k c) -> p k c", k=32), 32, 128, [t_wcs[l]])
                    pb, tb = proj_fm(wv, tw, 0, 128, 32, lambda k: UT[:, k, :], [t_UT])
                    P.op('act' if n % 2 else 'dve',
                         (lambda e, pb=pb, n=n: e.activation(out=RES[:, n, :], in_=pb[:], func=AF.Identity)) if n % 2 else
                         (lambda e, pb=pb, n=n: e.tensor_copy(out=RES[:, n, :], in_=pb[:])), r=[tb], w=[t_RES] if n == 0 else (), wp=() if n == 0 else [t_RES])
                postnorm_resid(xv, t_x, l, j, 5)
                _merge(t_BIG.r, t_UT.r); _merge(t_BIG.r, t_UT.w)

            ccsems = []
            for l in range(L):
                for ci in range(4):
                    D1(l, 1, ci, xTs[:, :, ci * CH:(ci + 1) * CH], t_xTs[ci], ci * CH)
                ckpt('D1s%d' % l)
                def flat512(d):
                    a = d.ap()
                    if a.shape[1] == 1024:
                        return a.rearrange("r (a c) -> (r a) c", c=512)
                    return a.rearrange("r c -> (r c)").rearrange("(n c) -> n c", c=512)
                for (src, dst, tsrc, tdst) in ((kgi[0], kgo[0], t_k[1], t_kgo), (kgi[1], kgo[1], t_k[1], t_kgo), (vgi[0], vgo[0], t_v[1], t_vgo), (vgi[1], vgo[1], t_v[1], t_vgo)):
                    csem = es.enter_context(nc.semaphore("cc%d" % len(ccsems)))
                    ccsems.append(csem)
                    deps = P._deps([tsrc], [], [tdst])
                    P._mark(deps)
                    kid = ('cc', len(ccsems))
                    P.streams['pool'].append((deps, (lambda e, src=src, dst=dst: e.collective_compute(
                        "AllGather", ALU.bypass, replica_groups=[[0, 1], [2, 3], [4, 5], [6, 7]],
                        ins=[flat512(src)], outs=[flat512(dst)])), ('x', csem)))
                    P._update({kid: 1}, [tsrc], [], [tdst])
                    P.ccmap = getattr(P, 'ccmap', {})
                    P.ccmap[kid] = csem
                ckpt('cc%d' % l)
                if l + 1 < L:
                    mod_layer(l + 1)
                    mv_layer(l + 1)
                    cast_layer(l + 1)
                    swap_layer(l + 1)
                    ctx_layer(l + 1)
                for ci in range(2):
                    P.dma('sp', XC[:].rearrange("p k c -> p (k c)"), xp_scr.ap()[ci], r=[t_xps[ci]], w=[t_XC], key=t_XC)
                    D1(l, 0, ci, XC, t_XC, ci * CH)
                ckpt('D1p%d' % l)
                ATT_prompt(l)
                ckpt('ATTp%d' % l)
                for ci in range(2):
                    P.dma('sp', XC[:].rearrange("p k c -> p (k c)"), xp_scr.ap()[ci], r=[t_xps[ci]], w=[t_XC], key=t_XC)
                    D2(l, 0, ci, XC, t_XC, ci * CH)
                    P.dma('pool', xp_scr.ap()[ci], XC[:].rearrange("p k c -> p (k c)"), r=[t_XC], w=[t_xps[ci]], key=t_XC)
                ckpt('D2p%d' % l)
                ATT_sample(l)
                ckpt('ATTs%d' % l)
                for ci in range(4):
                    D2(l, 1, ci, xTs[:, :, ci * CH:(ci + 1) * CH], t_xTs[ci], ci * CH)

                ckpt('D2s%d' % l)
            for grp, yout, ntile in ((0, y_p, 8), (1, y_s, 16)):
                for tt in range(ntile):
                    c = tt // 4
                    if grp == 0 and tt % 4 == 0:
                        P.dma('sp', XC[:].rearrange("p k c -> p (k c)"), xp_scr.ap()[c], r=[t_xps[c]], w=[t_XC], key=t_XC)
                    stg = TMP[:, (tt % 2) * 2:(tt % 2) * 2 + 2].rearrange("p a c -> p (a c)")
                    tstg = t_TMP[(tt % 2) * 2]
                    tstg2 = t_TMP[(tt % 2) * 2 + 1]
                    for half in range(2):
                        pb, tb = bank()

                        def tr(e, pb=pb, half=half, tt=tt, grp=grp):
                            inst = None
                            for q in range(4):
                                k = half * 4 + q
                                src = XC[:, k, (tt % 4) * 128:(tt % 4 + 1) * 128] if grp == 0 else xTs[:, k, tt * 128:(tt + 1) * 128]
                                inst = e.transpose(pb[:, q * 128:(q + 1) * 128], src, identF[:])
                            return inst
                        P.op('pe', tr, r=[t_XC if grp == 0 else t_xTs[c], t_const], w=[tb])
                        P.op('act' if half else 'dve',
                             (lambda e, pb=pb, stg=stg, half=half: e.activation(out=stg[:, half * 512:(half + 1) * 512], in_=pb[:], func=AF.Identity)) if half else
                             (lambda e, pb=pb, stg=stg, half=half: e.tensor_copy(out=stg[:, half * 512:(half + 1) * 512], in_=pb[:])),
                             r=[tb], w=[tstg, tstg2] if half == 0 else (), wp=() if half == 0 else [tstg])
                    P.dma('pool', yout[tt * 128:(tt + 1) * 128, :], stg, r=[tstg, tstg2], wp=[t_outs], key=tstg)


        except _Stop:
            pass
        block = es.enter_context(nc.Block())
        _finalize_with_cc(P, es, block)
    return nc


def _finalize_with_cc(P, es, block):
    nc = P.nc
    sems = {}
    for e in P.ENG:
        sems[e] = es.enter_context(nc.semaphore("s_" + e))
    for i, k in enumerate(P.keys):
        sems[k] = es.enter_context(nc.semaphore("d%d" % i))
    for kid, s in getattr(P, 'ccmap', {}).items():
        sems[kid] = s
    cmap = {}
    for e in P.ENG:
        m = {}
        c = 0
        nd = P.needed[e]
        for i in range(1, P.nops[e] + 1):
            if i in nd:
                c += 1
            m[i] = c
        cmap[e] = m
    final = dict(P.dmacnt)

    def emit(ename, eng):
        waited = {}
        for deps, fn, me in P.streams[ename]:
            for k, v in deps.items():
                if k == 'pe' and ename == 'pe':
                    continue
                cnt = cmap[k][v] if isinstance(k, str) else v
                if cnt > 0 and waited.get(k, 0) < cnt:
                    eng.wait_ge(sems[k], cnt)
                    waited[k] = cnt
            inst = fn(eng)
            if me[0] == 'c':
                if me[2] in P.needed[me[1]]:
                    inst.then_inc(sems[me[1]], 1)
            elif me[0] == 'd':
                inst.then_inc(sems[me[1]], 16)
            else:
                inst.then_inc(me[1])
        if ename == 'sp':
            for k, v in final.items():
                eng.wait_ge(sems[k], v)

    block.sync(lambda e: emit('sp', e))
    block.gpsimd(lambda e: emit('pool', e))
    block.vector(lambda e: emit('dve', e))
    block.scalar(lambda e: emit('act', e))
    block.tensor(lambda e: emit('pe', e))


def _consts(core):
    half = core % 2
    ident = np.eye(128, dtype=np.float32)
    ones = np.ones((128, 128), np.float32)
    bones = np.zeros((128, 128), np.float32)
    bones[0:64, 0:64] = 1.0
    bones[64:128, 64:128] = 1.0
    perm = np.zeros((128, 128), np.float32)
    for m in range(128):
        hh, d = divmod(m, 64)
        a, r = divmod(d, 32)
        b, i = divmod(r, 16)
        k = hh * 64 + a * 32 + (1 - b) * 16 + i
        perm[k, m] = 1.0
    t = np.arange(2048) + half * 2048
    row = (t // 64).astype(np.float64)
    col = (t % 64).astype(np.float64)

    def tables(hd):
        q = hd // 4
        inv = (10000.0 ** (-np.arange(q, dtype=np.float32) / np.float32(q))).astype(np.float32)
        C = np.zeros((hd, 2048), np.float32)
        S = np.zeros((hd, 2048), np.float32)
        for a, pos in enumerate((row, col)):
            ang = (pos.astype(np.float32)[None, :] * inv[:, None]).astype(np.float32)
            c = np.cos(ang).astype(np.float32)
            s = np.sin(ang).astype(np.float32)
            base = a * 2 * q
            C[base:base + q] = c
            C[base + q:base + 2 * q] = c
            S[base:base + q] = -s
            S[base + q:base + 2 * q] = s
        return C, S
    C64, S64 = tables(64)
    ropeA = np.stack([np.concatenate([C64, C64], 0), np.concatenate([S64, S64], 0)], 0)
    C32, S32 = tables(32)
    CB = np.concatenate([np.ones((64, 2048), np.float32), C32], 0)
    SB = np.concatenate([np.zeros((64, 2048), np.float32), S32], 0)
    ropeB = np.stack([CB, SB], 0)
    kk = np.arange(128)[:, None]
    qq = np.arange(128)[None, :]
    m0 = (kk >= qq).astype(np.float32)
    m1 = np.ones((128, 128), np.float32)
    m2 = (kk <= qq).astype(np.float32)
    mid = np.concatenate([m0, m1, m2], 1)
    first = np.concatenate([m0 * (1.0 if half == 1 else 0.0), m1, m2], 1)
    last = np.concatenate([m0, m1, m2 * (1.0 if half == 0 else 0.0)], 1)
    mask = np.stack([first, mid, last], 0)
    return dict(c_ident=ident, c_ones=ones, c_bones=bones, c_perm=perm, c_ropeA=np.ascontiguousarray(ropeA),
                c_ropeB=np.ascontiguousarray(ropeB), c_mask=np.ascontiguousarray(mask))


_NC_CACHE = {}


def kernel(**inp):
    f = lambda a: np.ascontiguousarray(np.asarray(a, dtype=np.float32))
    if 'nc' not in _NC_CACHE:
        _NC_CACHE['nc'] = build_program()
    nc = _NC_CACHE['nc']
    wnames = ['w_mod', 'w_in', 'w_qup', 'w_kvup', 'w_oa', 'w_ob', 'w_oc', 'w_out', 'w_mlp1', 'w_mlp2']
    shared = {k: f(inp[k]) for k in wnames}
    shared['h_bmodT'] = f(inp['b_mod']).reshape(L, 48, 128).transpose(2, 0, 1)
    shared['h_gvecs'] = np.stack([f(inp[k]).reshape(L, 8, 128).transpose(2, 0, 1) for k in ('g_pre_mix', 'g_post_mix', 'g_pre_mlp', 'g_post_mlp')], 1).reshape(128, 4 * L * 8)
    shared['h_gqaT'] = f(inp['g_qa']).reshape(L, 3, 128).transpose(2, 0, 1)
    shared['h_gkvaT'] = f(inp['g_kva']).reshape(L, 2, 128).transpose(2, 0, 1)
    shared['h_gqc2'] = np.concatenate([f(inp['g_qc']).T, f(inp['g_qc']).T], 0)
    shared['h_gkc2'] = np.concatenate([f(inp['g_kc']).T, f(inp['g_kc']).T], 0)
    shared['h_sinkb'] = np.broadcast_to(f(inp['a_sink'])[None], (128, L, 8))
    x_prompt = f(inp['x_prompt']); x_sample = f(inp['x_sample'])
    in_maps = []
    for c in range(8):
        b, half = c // 2, c % 2
        m = dict(shared)
        m['xp'] = x_prompt[4 * c:4 * c + 4].reshape(1024, D)
        m['xs'] = x_sample[b, half * 2048:(half + 1) * 2048]
        m['ca_k'] = f(inp['cache_a_k'])[b].reshape(L, 512, 128)
        m['ca_v'] = f(inp['cache_a_v'])[b].reshape(L, 512, 128)
        m['cb_ckv'] = f(inp['cache_b_ckv'])[b]
        m['cb_kr'] = f(inp['cache_b_krope'])[b]
        m['cc_k'] = f(inp['cache_c_k'])[b].reshape(L, 512, 128)
        m['cc_v'] = f(inp['cache_c_v'])[b].reshape(L, 512, 128)
        m['cvecT'] = np.stack([f(inp['c_ctx']), f(inp['c'])[b]], 0).reshape(2, 8, 128).transpose(2, 1, 0)
        m.update(_consts(c))
        m = {k: np.ascontiguousarray(v) for k, v in m.items()}
        in_maps.append(m)
    res = run_bass_kernel_spmd(nc, in_maps, core_ids=list(range(8)))
    r = res.results
    y_prompt = np.concatenate([r[c]['y_p'].reshape(4, 256, D) for c in range(8)], 0)
    y_sample = np.stack([np.concatenate([r[2 * b]['y_s'], r[2 * b + 1]['y_s']], 0) for b in range(4)], 0)

    def cat(name, shape):
        return np.concatenate([r[c][name].reshape((4, L, 256) + shape) for c in range(8)], 0)
    return (y_prompt.astype(np.float32), y_sample.astype(np.float32),
            cat('o_ak', (2, 64)), cat('o_av', (2, 64)), cat('o_ckv', (256,)), cat('o_kr', (32,)),
            cat('o_ck', (2, 64)), cat('o_cv', (2, 64)))
```
